# Optimizing a Trainium2 kernel written in Bass

```python
import jax
import jax.numpy as jnp
from jax import lax
import numpy as np


D_MODEL = 2048
BATCH = 4
SEQ = 8192
DEPTH = 4

GRID_W = 64
CTX_LEN = 256
D_FF = 4 * D_MODEL
ROPE_BASE = 10000.0
NORM_EPS = 1e-6
Q_BLOCK = 128

MLA_HEADS = 4
MLA_Q_RANK = 512
MLA_KV_RANK = 256
MLA_NOPE = 128
MLA_ROPE = 64
MLA_V = 128

SWA_HEADS = 16
SWA_KV_HEADS = 2
SWA_HEAD_DIM = 64
WINDOW = 128
SWA_BLOCK = 128

MLSTM_HEADS = 4
MLSTM_QK = 64
MLSTM_V = 128
MLSTM_CHUNK = 64

D_MIX = MLA_HEADS * MLA_V + SWA_HEADS * SWA_HEAD_DIM + MLSTM_HEADS * MLSTM_V
IN_WIDTHS = (MLA_Q_RANK, MLA_KV_RANK, MLA_ROPE,
             SWA_HEADS * SWA_HEAD_DIM, SWA_KV_HEADS * SWA_HEAD_DIM, SWA_KV_HEADS * SWA_HEAD_DIM,
             MLSTM_HEADS * MLSTM_QK, MLSTM_HEADS * MLSTM_QK, MLSTM_HEADS * MLSTM_V,
             4 * MLSTM_HEADS, MLSTM_HEADS * MLSTM_V)
D_IN = sum(IN_WIDTHS)

kernel_name = 'hybrid_mla_swa_mlstm_dit_trunk'

F32 = jnp.float32


def rmsnorm(x, g):
    xf = x.astype(F32)
    y = xf * lax.rsqrt(jnp.mean(xf * xf, axis=-1, keepdims=True) + NORM_EPS)
    return y.astype(x.dtype) * g


def modulate_norm(x, g, shift, scale):
    return rmsnorm(x, g) * (1 + scale) + shift


def split_in(z):
    offs = [int(o) for o in np.cumsum(IN_WIDTHS)[:-1]]
    return jnp.split(z, offs, axis=-1)


def rope_tables(row, col, dim):
    half = dim // 2
    inv = ROPE_BASE ** (-jnp.arange(0, half, 2, dtype=F32) / half)
    ar = row[:, None].astype(F32) * inv
    ac = col[:, None].astype(F32) * inv
    ang = jnp.concatenate([ar, ar, ac, ac], axis=-1)
    return jnp.cos(ang), jnp.sin(ang)


def apply_rope(x, cos, sin):
    x1, x2, x3, x4 = jnp.split(x, 4, axis=-1)
    rot = jnp.concatenate([-x2, x1, -x4, x3], axis=-1)
    return x * cos[:, None].astype(x.dtype) + rot * sin[:, None].astype(x.dtype)


def softmax_with_sink(logits, sink):
    sink = jnp.broadcast_to(sink, logits.shape[:-1] + (1,))
    return jax.nn.softmax(jnp.concatenate([sink, logits], axis=-1), axis=-1)[..., 1:]


def mla_qkv(zq, zkv, zr, g_q, w_uq, g_kv, w_ukv, rope):
    B, T, _ = zq.shape
    q = (rmsnorm(zq, g_q) @ w_uq).reshape(B, T, MLA_HEADS, MLA_NOPE + MLA_ROPE)
    q_nope, q_rope = q[..., :MLA_NOPE], q[..., MLA_NOPE:]
    kv = (rmsnorm(zkv, g_kv) @ w_ukv).reshape(B, T, MLA_HEADS, MLA_NOPE + MLA_V)
    k_nope, v = kv[..., :MLA_NOPE], kv[..., MLA_NOPE:]
    k_rope = zr[:, :, None, :]
    if rope is not None:
        cos, sin = rope
        q_rope = apply_rope(q_rope, cos, sin)
        k_rope = apply_rope(k_rope, cos, sin)
    return q_nope, q_rope, k_nope, k_rope, v


def mla_attend(q_nope, q_rope, k_nope, k_rope, v):
    B, T, H, _ = q_nope.shape
    nb = T // Q_BLOCK
    scale = (MLA_NOPE + MLA_ROPE) ** -0.5
    kr = k_rope[:, :, 0]

    def blk(args):
        qn, qr = args
        s = jnp.einsum('bqhd,bkhd->bhqk', qn, k_nope) + jnp.einsum('bqhd,bkd->bhqk', qr, kr)
        p = jax.nn.softmax(s.astype(F32) * scale, axis=-1).astype(v.dtype)
        return jnp.einsum('bhqk,bkhd->bqhd', p, v)

    qn_b = q_nope.reshape(B, nb, Q_BLOCK, H, MLA_NOPE).swapaxes(0, 1)
    qr_b = q_rope.reshape(B, nb, Q_BLOCK, H, MLA_ROPE).swapaxes(0, 1)
    out = lax.map(blk, (qn_b, qr_b))
    return out.swapaxes(0, 1).reshape(B, T, H * MLA_V)


def gqa_sink_dense(q, k, v, sink):
    B, T, Hq, d = q.shape
    G = k.shape[2]
    R = Hq // G
    qg = q.reshape(B, T, G, R, d)
    s = jnp.einsum('bqgrd,bkgd->bgrqk', qg, k).astype(F32) * (d ** -0.5)
    p = softmax_with_sink(s, sink.reshape(G, R)[None, :, :, None, None].astype(F32)).astype(v.dtype)
    return jnp.einsum('bgrqk,bkgd->bqgrd', p, v).reshape(B, T, Hq * d)


def swa_latent(q, k, v, kc, vc, sink):
    B, S, Hq, d = q.shape
    G = SWA_KV_HEADS
    R = Hq // G
    Lc = kc.shape[1]
    nb = S // SWA_BLOCK
    span = SWA_BLOCK + 2 * WINDOW
    pad = ((0, 0), (WINDOW, WINDOW), (0, 0), (0, 0))
    kp = jnp.pad(k, pad)
    vp = jnp.pad(v, pad)
    qb = q.reshape(B, nb, SWA_BLOCK, G, R, d).swapaxes(0, 1)
    qi = jnp.arange(SWA_BLOCK)[:, None]
    ki = jnp.arange(span)[None, :]
    rel = ki - WINDOW - qi
    scale = d ** -0.5
    sink_b = sink.reshape(G, R)[None, :, :, None, None].astype(F32)

    def blk(args):
        j, qj = args
        kj = lax.dynamic_slice_in_dim(kp, j * SWA_BLOCK, span, axis=1)
        vj = lax.dynamic_slice_in_dim(vp, j * SWA_BLOCK, span, axis=1)
        u = j * SWA_BLOCK - WINDOW + ki
        valid = (jnp.abs(rel) <= WINDOW) & (u >= 0) & (u < S)
        s_loc = jnp.einsum('bqgrd,bkgd->bgrqk', qj, kj).astype(F32) * scale
        s_loc = jnp.where(valid, s_loc, -jnp.inf)
        s_ctx = jnp.einsum('bqgrd,bkgd->bgrqk', qj, kc).astype(F32) * scale
        p = softmax_with_sink(jnp.concatenate([s_ctx, s_loc], axis=-1), sink_b).astype(v.dtype)
        return (jnp.einsum('bgrqk,bkgd->bqgrd', p[..., :Lc], vc)
                + jnp.einsum('bgrqk,bkgd->bqgrd', p[..., Lc:], vj))

    out = lax.map(blk, (jnp.arange(nb), qb))
    return out.swapaxes(0, 1).reshape(B, S, Hq * d)


def mlstm_heads(zq, zk, zv, zg, gate_bias):
    B, T, _ = zq.shape
    q = zq.reshape(B, T, MLSTM_HEADS, MLSTM_QK).transpose(0, 2, 1, 3).astype(F32) * (MLSTM_QK ** -0.5)
    k = zk.reshape(B, T, MLSTM_HEADS, MLSTM_QK).transpose(0, 2, 1, 3).astype(F32)
    v = zv.reshape(B, T, MLSTM_HEADS, MLSTM_V).transpose(0, 2, 1, 3).astype(F32)
    g = (zg.reshape(B, T, 4, MLSTM_HEADS).astype(F32) + gate_bias.astype(F32)).transpose(2, 0, 3, 1)
    return q, k, v, g


def mlstm_zero_state(B):
    return (jnp.zeros((B, MLSTM_HEADS, MLSTM_QK, MLSTM_V), F32),
            jnp.zeros((B, MLSTM_HEADS, MLSTM_QK), F32),
            jnp.zeros((B, MLSTM_HEADS), F32))


def mlstm_chunkwise(q, k, v, i_pre, f_pre, state):
    B, H, T, dk = q.shape
    dv = v.shape[-1]
    L = MLSTM_CHUNK
    N = T // L
    qc = q.reshape(B, H, N, L, dk)
    kc = k.reshape(B, H, N, L, dk)
    vc = v.reshape(B, H, N, L, dv)
    ig = i_pre.reshape(B, H, N, L)
    b = jnp.cumsum(jax.nn.log_sigmoid(f_pre).reshape(B, H, N, L), axis=-1)
    g = b[..., -1]
    a = g[..., None] - b + ig
    a_max = jnp.max(a, axis=-1)
    w = jnp.exp(a - a_max[..., None])
    dC = jnp.einsum('bhnl,bhnld,bhnle->bhnde', w, kc, vc)
    dn = jnp.einsum('bhnl,bhnld->bhnd', w, kc)

    def step(carry, inp):
        C, n, m = carry
        g_c, am_c, dC_c, dn_c = inp
        m_new = jnp.maximum(g_c + m, am_c)
        decay = jnp.exp(g_c + m - m_new)
        grow = jnp.exp(am_c - m_new)
        C_new = decay[..., None, None] * C + grow[..., None, None] * dC_c
        n_new = decay[..., None] * n + grow[..., None] * dn_c
        return (C_new, n_new, m_new), (C, n, m)

    to_t = lambda t: jnp.moveaxis(t, 2, 0)
    final, (C_in, n_in, m_in) = lax.scan(step, state, (to_t(g), to_t(a_max), to_t(dC), to_t(dn)))
    C_in = jnp.moveaxis(C_in, 0, 2)
    n_in = jnp.moveaxis(n_in, 0, 2)
    m_in = jnp.moveaxis(m_in, 0, 2)
    causal = jnp.tril(jnp.ones((L, L), dtype=bool))
    d_log = jnp.where(causal, b[..., :, None] - b[..., None, :] + ig[..., None, :], -jnp.inf)
    inter_log = b + m_in[..., None]
    m_t = jnp.maximum(inter_log, jnp.max(d_log, axis=-1))
    s = jnp.einsum('bhnld,bhnsd->bhnls', qc, kc) * jnp.exp(d_log - m_t[..., None])
    inter_w = jnp.exp(inter_log - m_t)
    num = (jnp.einsum('bhnls,bhnse->bhnle', s, vc)
           + inter_w[..., None] * jnp.einsum('bhnld,bhnde->bhnle', qc, C_in))
    den = jnp.sum(s, axis=-1) + inter_w * jnp.einsum('bhnld,bhnd->bhnl', qc, n_in)
    h = num / jnp.maximum(jnp.abs(den), jnp.exp(-m_t))[..., None]
    return h.reshape(B, H, T, dv), final


def mlstm_bidir(q, k, v, g, init_f, init_b):
    h_f, st_f = mlstm_chunkwise(q, k, v, g[0], g[1], init_f)
    fl = lambda t: jnp.flip(t, axis=2)
    h_b, st_b = mlstm_chunkwise(fl(q), fl(k), fl(v), jnp.flip(g[2], axis=-1), jnp.flip(g[3], axis=-1), init_b)
    return h_f + fl(h_b), st_f, st_b


def mlstm_out(hm, zo, g_h, dtype):
    B, H, T, dv = hm.shape
    hn = rmsnorm(hm.transpose(0, 2, 1, 3), g_h.astype(F32))
    return (hn.reshape(B, T, H * dv) * jax.nn.sigmoid(zo.astype(F32))).astype(dtype)


def token_mixers(h, hc, w_in, mla_g_q, mla_w_uq, mla_g_kv, mla_w_ukv, swa_sink,
                 mlstm_gate_bias, mlstm_g_h, rope_mla, rope_swa, need_ctx):
    B = h.shape[0]
    z = split_in(h @ w_in)
    zc = split_in(hc @ w_in)

    qn, qr, kn, kr, vv = mla_qkv(z[0], z[1], z[2], mla_g_q, mla_w_uq, mla_g_kv, mla_w_ukv, rope_mla)
    qnc, qrc, knc, krc, vvc = mla_qkv(zc[0], zc[1], zc[2], mla_g_q, mla_w_uq, mla_g_kv, mla_w_ukv, None)
    y_mla = mla_attend(qn, qr, jnp.concatenate([knc, kn], axis=1),
                       jnp.concatenate([krc, kr], axis=1), jnp.concatenate([vvc, vv], axis=1))

    heads = lambda t, n: t.reshape(t.shape[0], t.shape[1], n, SWA_HEAD_DIM)
    cos_s, sin_s = rope_swa
    qs = apply_rope(heads(z[3], SWA_HEADS), cos_s, sin_s)
    ks = apply_rope(heads(z[4], SWA_KV_HEADS), cos_s, sin_s)
    vs = heads(z[5], SWA_KV_HEADS)
    qsc, ksc, vsc = heads(zc[3], SWA_HEADS), heads(zc[4], SWA_KV_HEADS), heads(zc[5], SWA_KV_HEADS)
    y_swa = swa_latent(qs, ks, vs, ksc, vsc, swa_sink)

    qm, km, vm, gm = mlstm_heads(z[6], z[7], z[8], z[9], mlstm_gate_bias)
    qmc, kmc, vmc, gmc = mlstm_heads(zc[6], zc[7], zc[8], zc[9], mlstm_gate_bias)
    zero = mlstm_zero_state(B)
    hmc, st_f, st_b = mlstm_bidir(qmc, kmc, vmc, gmc, zero, zero)
    hm, _, _ = mlstm_bidir(qm, km, vm, gm, st_f, st_b)
    y_mlstm = mlstm_out(hm, z[10], mlstm_g_h, h.dtype)

    y = jnp.concatenate([y_mla, y_swa, y_mlstm], axis=-1)
    if need_ctx:
        yc = jnp.concatenate([mla_attend(qnc, qrc, knc, krc, vvc),
                              gqa_sink_dense(qsc, ksc, vsc, swa_sink),
                              mlstm_out(hmc, zc[10], mlstm_g_h, hc.dtype)], axis=-1)
    else:
        yc = None
    return y, yc


def squared_relu_mlp(h, w1, w2):
    return jnp.square(jax.nn.relu(h @ w1)) @ w2


def setup_inputs(seed: int = 0) -> dict:
    key = jax.random.key(seed)
    ks = jax.random.split(key, 24)
    nrm = lambda k, shape, s: jax.random.normal(k, shape, F32) * s
    L, D = DEPTH, D_MODEL
    f_sel = jnp.array([0.0, 1.0, 0.0, 1.0], F32)[None, :, None]
    gate_bias = nrm(ks[14], (L, 4, MLSTM_HEADS), 0.1) + f_sel * jax.random.uniform(
        ks[15], (L, 4, MLSTM_HEADS), F32, minval=3.0, maxval=6.0)
    return {
        'x': nrm(ks[0], (BATCH, SEQ, D), 1.0),
        'c': nrm(ks[1], (BATCH, D), 1.0),
        'ctx': nrm(ks[2], (BATCH, CTX_LEN, D), 1.0),
        'c_ctx': nrm(ks[3], (D,), 1.0),
        'w_mod': nrm(ks[4], (L, D, 6 * D), 0.5 * D ** -0.5),
        'b_mod': nrm(ks[5], (L, 6 * D), 0.01),
        'g_norm1': 1.0 + nrm(ks[6], (L, D), 0.02),
        'g_norm2': 1.0 + nrm(ks[7], (L, D), 0.02),
        'w_in': nrm(ks[8], (L, D, D_IN), D ** -0.5),
        'mla_g_q': 1.0 + nrm(ks[9], (L, MLA_Q_RANK), 0.02),
        'mla_w_uq': nrm(ks[10], (L, MLA_Q_RANK, MLA_HEADS * (MLA_NOPE + MLA_ROPE)), MLA_Q_RANK ** -0.5),
        'mla_g_kv': 1.0 + nrm(ks[11], (L, MLA_KV_RANK), 0.02),
        'mla_w_ukv': nrm(ks[12], (L, MLA_KV_RANK, MLA_HEADS * (MLA_NOPE + MLA_V)), MLA_KV_RANK ** -0.5),
        'swa_sink': nrm(ks[13], (L, SWA_HEADS), 0.5),
        'mlstm_gate_bias': gate_bias,
        'mlstm_g_h': 1.0 + nrm(ks[16], (L, MLSTM_HEADS, MLSTM_V), 0.02),
        'w_out': nrm(ks[17], (L, D_MIX, D), D_MIX ** -0.5),
        'w_ff1': nrm(ks[18], (L, D, D_FF), D ** -0.5),
        'w_ff2': nrm(ks[19], (L, D_FF, D), D_FF ** -0.5),
        'g_final': 1.0 + nrm(ks[20], (D,), 0.02),
    }


def reference(x, c, ctx, c_ctx, w_mod, b_mod, g_norm1, g_norm2, w_in, mla_g_q, mla_w_uq,
              mla_g_kv, mla_w_ukv, swa_sink, mlstm_gate_bias, mlstm_g_h, w_out, w_ff1, w_ff2, g_final):
    B, S, D = x.shape
    rows = S // GRID_W
    row = jnp.repeat(jnp.arange(rows), GRID_W)
    col = jnp.tile(jnp.arange(GRID_W), rows)
    rope_mla = rope_tables(row, col, MLA_ROPE)
    rope_swa = rope_tables(row, col, SWA_HEAD_DIM)
    silu_c = jax.nn.silu(c)
    silu_cc = jax.nn.silu(c_ctx)
    xc = ctx
    for l in range(DEPTH):
        need_ctx = l < DEPTH - 1
        mod = silu_c @ w_mod[l] + b_mod[l]
        modc = silu_cc @ w_mod[l] + b_mod[l]
        sh1, sc1, gt1, sh2, sc2, gt2 = [m[:, None, :] for m in jnp.split(mod, 6, axis=-1)]
        sh1c, sc1c, gt1c, sh2c, sc2c, gt2c = jnp.split(modc, 6, axis=-1)
        h = modulate_norm(x, g_norm1[l], sh1, sc1)
        hc = modulate_norm(xc, g_norm1[l], sh1c, sc1c)
        y, yc = token_mixers(h, hc, w_in[l], mla_g_q[l], mla_w_uq[l], mla_g_kv[l], mla_w_ukv[l],
                             swa_sink[l], mlstm_gate_bias[l], mlstm_g_h[l], rope_mla, rope_swa, need_ctx)
        x = x + gt1 * (y @ w_out[l])
        x = x + gt2 * squared_relu_mlp(modulate_norm(x, g_norm2[l], sh2, sc2), w_ff1[l], w_ff2[l])
        if need_ctx:
            xc = xc + gt1c * (yc @ w_out[l])
            xc = xc + gt2c * squared_relu_mlp(modulate_norm(xc, g_norm2[l], sh2c, sc2c), w_ff1[l], w_ff2[l])
    return rmsnorm(x, g_final)
```

```python
import contextlib
import numpy as np
import concourse.bass as bass
import concourse.mybir as mybir
from concourse.bass_utils import run_bass_kernel_spmd

F32, BF16 = mybir.dt.float32, mybir.dt.bfloat16
AF = mybir.ActivationFunctionType
ALU = mybir.AluOpType

D = 2048; KC = 16; DEPTH = 4; SEQ = 8192; BATCH = 4; CTX = 256
T = 4096; NT = T + CTX; TT = 512
DFF = 8192
EPS = 1e-6
MLA_SCALE = 192.0 ** -0.5
NROWS_G = 8768


class Buf:
    def __init__(self, t, name):
        self.t = t; self.name = name
        self.lastw = None; self.reads = []
        self.semid = None; self.dcnt = 0

    def __getitem__(self, k):
        return self.t[k]


class Op:
    __slots__ = ("eng", "fn", "deps", "signal", "val", "dma", "idx")

    def __init__(self, eng, fn, dma=None):
        self.eng = eng; self.fn = fn; self.deps = []; self.signal = False
        self.val = None; self.dma = dma


class Prog:
    ENGS = ("pe", "act", "dve", "pool", "sp")

    def __init__(self, nc):
        self.nc = nc
        self.ops = {e: [] for e in self.ENGS}
        self.nsem = 0
        self.semholders = []
        self.extra_sems = 0

    def new_sem(self):
        self.nsem += 1
        return self.nsem - 1

    def _tok(self, op):
        if op.dma is not None:
            b = op.dma
            return ("d", b.semid, b.dcnt * 16)
        return ("o", op)

    def add(self, eng, fn, reads=(), writes=(), dma=None, cc=False):
        op = Op(eng, fn, dma)
        deps = []
        for b in reads:
            if b.lastw is not None:
                deps.append(b.lastw)
        for b in writes:
            if b.lastw is not None:
                deps.append(b.lastw)
            deps.extend(b.reads)
        res = []
        for d in deps:
            if d.dma is not None:
                res.append(("d", d.dma.semid, d.dma.dcnt * 16 if not isinstance(d.dma, CCSem) else 1))
            else:
                if d.eng == eng and eng == "pe":
                    continue
                d.signal = True
                res.append(("o", d))
        op.deps = res
        if dma is not None:
            if dma.semid is None:
                dma.semid = self.new_sem()
            dma.dcnt += 1
        for b in reads:
            b.reads.append(op)
        for b in writes:
            b.lastw = op; b.reads = []
        self.ops[eng].append(op)
        return op

    def emit(self, stack):
        nc = self.nc
        engsem = {e: self.new_sem() for e in ("pe", "act", "dve", "pool")}
        sems = [stack.enter_context(nc.semaphore(f"s{i}")) for i in range(self.nsem)]
        for e in ("pe", "act", "dve", "pool"):
            c = 0
            for op in self.ops[e]:
                if op.dma is None and op.signal:
                    c += 1; op.val = c
        block = stack.enter_context(nc.Block())
        handles = {"pe": block.tensor, "act": block.scalar, "dve": block.vector,
                   "pool": block.gpsimd, "sp": block.sync}

        def run(ename):
            def body(eng):
                waited = {}
                for op in self.ops[ename]:
                    for d in op.deps:
                        if d[0] == "d":
                            sid, v = d[1], d[2]
                        else:
                            sid, v = engsem[d[1].eng], d[1].val
                        if waited.get(sid, 0) >= v:
                            continue
                        waited[sid] = v
                        eng.wait_ge(sems[sid], v)
                    ins = op.fn(eng)
                    if op.dma is not None:
                        if isinstance(op.dma, CCSem):
                            ins.then_inc(sems[op.dma.semid])
                        else:
                            ins.then_inc(sems[op.dma.semid], 16)
                    elif op.signal:
                        ins.then_inc(sems[engsem[ename]], 1)
            return body

        for e in self.ENGS:
            if self.ops[e]:
                handles[e](run(e))


class CCSem(Buf):
    pass


def build_program(layers, first, last, debug=()):
    L = layers
    nc = bass.Bass("TRN2", target_bir_lowering=False)
    P = Prog(nc)
    stack = contextlib.ExitStack()

    def din(name, shape, dt=F32):
        return nc.dram_tensor(name, list(shape), dt, kind="ExternalInput").ap()

    def dscr(name, shape, dt):
        kind = "ExternalOutput" if name in debug else "Internal"
        return nc.dram_tensor(name, list(shape), dt, kind=kind).ap()

    xT_in = din("xT", [D, NT])
    cT_in = din("cT", [128, KC, 2])
    cos_in = din("cos2", [128, T]); sin_in = din("sin2", [128, T])
    selw_in = din("selw", [128, 2])
    cmask_in = din("cmask", [128, 8, 128])
    wmod_in = din("wmod", [L, 96, 2, 128, 8 * 128])
    bmod_in = din("bmodT", [L, 128, 96])
    g1_in = din("g1T", [L, 128, KC]); g2_in = din("g2T", [L, 128, KC])
    wina_in = din("win_a", [L, 14, 128, KC * 256]); winb_in = din("win_b", [L, 128, KC * 80])
    gq_in = din("gqT", [L, 128, 4]); gkv_in = din("gkvT", [L, 128, 2])
    wuq_in = din("wuq", [L, 128, 4 * 768]); wukv_in = din("wukv", [L, 128, 2 * 1024])
    sink_in = din("sink", [L, 16]); gbias_in = din("gbias", [L, 16])
    gh_in = din("ghT", [L, 128, 4])
    wouta_in = din("wout_a", [L, 16, 128, 8 * 128]); woutb_in = din("wout_b", [L, 16, 64, 16 * 128])
    wff1_in = din("wff1", [L, 32, 128, KC * 256]); wff2_in = din("wff2", [L, 64, 128, 16 * 128])
    gf_in = din("gfT", [128, KC])
    if last:
        out_ap = nc.dram_tensor("outT", [D, T], F32, kind="ExternalOutput").ap()
    else:
        out_ap = nc.dram_tensor("xT_out", [D, NT], F32, kind="ExternalOutput").ap()

    xT = dscr("xTs", [D, NT], F32)
    w_ina = dscr("b_win_a", [L, 14, 128, KC * 256], BF16); w_inb = dscr("b_win_b", [L, 128, KC * 80], BF16)
    w_uq = dscr("b_wuq", [L, 128, 4 * 768], BF16); w_ukv = dscr("b_wukv", [L, 128, 2 * 1024], BF16)
    w_outa = dscr("b_wouta", [L, 16, 128, 8 * 128], BF16); w_outb = dscr("b_woutb", [L, 16, 64, 16 * 128], BF16)
    w_ff1 = dscr("b_wff1", [L, 32, 128, KC * 256], BF16); w_ff2 = dscr("b_wff2", [L, 64, 128, 16 * 128], BF16)
    cmask_b = dscr("b_cmask", [128, 8, 128], BF16)
    qmla = dscr("qmla", [768, NT], BF16)
    GK = [dscr(f"gk{i}", [2048, 512], BF16) for i in range(2)]; GKo = [dscr(f"gko{i}", [4096, 512], BF16) for i in range(2)]
    GV = [dscr(f"gv{i}", [2048, 512], BF16) for i in range(2)]; GVo = [dscr(f"gvo{i}", [4096, 512], BF16) for i in range(2)]
    GR = dscr("gr", [576, 512], BF16); GRo = dscr("gro", [1152, 512], BF16)
    s_in = dscr("s_in", [256, 129], F32); s_out = dscr("s_out", [512, 129], F32)
    kc_n = dscr("kc_n", [512, CTX], BF16); kc_r = dscr("kc_r", [64, CTX], BF16); vc_m = dscr("vc_m", [CTX, 512], BF16)
    swaq = dscr("swaq", [1024, NT], BF16); swak = dscr("swak", [128, NT], BF16); swav = dscr("swav", [NT, 128], BF16)
    mq = dscr("mq", [256, NT], BF16); mk = dscr("mk", [256, NT], BF16)
    mkt = dscr("mkt", [NT, 256], BF16); mvt = dscr("mvt", [NT, 512], BF16)
    gts = dscr("gts", [NT, 16], F32); zo = dscr("zo", [512, NT], BF16)
    hf = dscr("hf", [512, NT], F32); ymT = dscr("ymT", [512, NT], BF16)

    def DB(name):
        return Buf(None, name)
    d_x = [DB(f"x{i}") for i in range(9)]
    d_w = DB("wts"); d_q = [DB(f"q{i}") for i in range(9)]
    d_gk = [DB("gk0"), DB("gk1")]; d_gv = [DB("gv0"), DB("gv1")]; d_gr = DB("gr")
    d_gko = [DB("gko0"), DB("gko1")]; d_gvo = [DB("gvo0"), DB("gvo1")]; d_gro = DB("gro")
    d_sin = DB("sin"); d_sout = DB("sout")
    d_kc = DB("kc"); d_sw = [DB(f"sw{i}") for i in range(9)]; d_m = DB("m"); d_hf = DB("hf"); d_ym = DB("ym")
    d_out = DB("out")

    def sb(name, shape, dt=F32):
        return Buf(stack.enter_context(nc.sbuf_tensor(name, list(shape), dt)), name)

    def psb(name, shape, dt=F32):
        return Buf(stack.enter_context(nc.psum_tensor(name, list(shape), dt)), name)

    xt = sb("xt", [128, KC, TT])
    hT = sb("hT", [128, KC, TT], BF16)
    big = sb("big", [128, 16, TT], BF16)
    wt = [sb(f"wt{i}", [128, 4096], BF16) for i in range(2)]
    yT = sb("yT", [128, 8, TT], BF16)
    ysw = big; qsw = hT
    wuq_s = sb("wuq_s", [128, 4, 768], BF16); wukv_s = sb("wukv_s", [128, 2, 1024], BF16)
    zf = sb("zf", [128, 4, TT]); cz = sb("cz", [128, 4, TT], BF16)
    rs = sb("rs", [128, TT]); tmp = [sb(f"tmp{i}", [128, TT]) for i in range(2)]
    stg = [sb(f"stg{i}", [128, 4, TT], BF16) for i in range(2)]
    zb = [sb(f"zb{i}", [128, TT], BF16) for i in range(2)]
    gstg = sb("gstg", [128, 4, 16])
    cosS = sb("cosS", [128, TT]); sinS = sb("sinS", [128, TT])
    cm = sb("cm", [128, 8, 128], BF16); cmf = sb("cmf", [128, 8, 128])
    cst = sb("cst", [128, 8])
    selw = sb("selw_s", [128, 2])
    scT = sb("scT", [128, KC, 2]); wm = [sb(f"wm{i}", [128, 8, 128]) for i in range(2)]
    modT = sb("modT", [128, 96, 2]); bmod = sb("bmod", [128, 96])
    g1 = sb("g1", [128, KC]); g2 = sb("g2", [128, KC]); gf = sb("gf", [128, KC])
    a1 = sb("a1", [128, KC, 2]); a2 = sb("a2", [128, KC, 2])
    gq = sb("gq", [128, 4]); gkv = sb("gkv", [128, 2]); gh = sb("gh", [128, 4])
    sk = sb("sk", [128, 16]); gb = sb("gb", [128, 16])
    qn = sb("qn", [128, 4, TT], BF16); qr = sb("qr", [64, 4, TT], BF16)
    kk = [sb(f"kk{i}", [128, 1024], BF16) for i in range(2)]
    kr = [sb(f"kr{i}", [64, 1024], BF16) for i in range(2)]
    vv = [sb(f"vv{i}", [128, 8, 128], BF16) for i in range(2)]
    pT = [sb(f"pT{i}", [128, TT], BF16) for i in range(3)]
    rec = sb("rec", [128, TT])
    ksw = sb("ksw", [64, 2, 768], BF16); vsw = sb("vsw", [128, 6, 2, 65], BF16)
    kcs = sb("kcs", [64, 2, CTX], BF16); vcs = sb("vcs", [128, 2, 2, 65], BF16)
    hk2 = sb("hk2", [64, 2, 2, 128], BF16); hv2 = sb("hv2", [128, 2, 128], BF16)
    hks = sb("hks", [64, 2, 128], BF16); hvs = sb("hvs", [128, 2, 65], BF16)
    dsw = sb("dsw", [128, TT]); rbs = sb("rbs", [64, TT])
    mqs = sb("mqs", [64, 4, 128], BF16); mks = sb("mks", [64, 4, 128], BF16)
    mkts = sb("mkts", [128, 256], BF16); mvts = sb("mvts", [128, 4, 129], BF16)
    gt = sb("gt", [128, 16]); lf = sb("lf", [128, 8]); lfr = sb("lfr", [128, 8, 64])
    bb = sb("bb", [128, 8]); gg = sb("gg", [128, 8]); bk = sb("bk", [128, 8]); kst = sb("kst", [128, 8])
    abc = sb("abc", [64, 128]); eg = sb("eg", [64, 2])
    qp = sb("qp", [64, 128], BF16); ptm = sb("ptm", [128, 128], BF16); kpp = sb("kpp", [128, 64], BF16)
    Cst = [[sb(f"C{d}{h}", [64, 129]) for h in range(4)] for d in range(2)]
    Cbf = sb("Cbf", [64, 128], BF16); nrep = sb("nrep", [64, 128], BF16)
    dm = sb("dm", [128, 128]); hdir = sb("hdir", [128, 128]); hfs = sb("hfs", [128, 128])
    zos = sb("zos", [128, 128], BF16); sg = sb("sg", [128, 128]); sqm = sb("sqm", [128, 128], BF16)
    yms = sb("yms", [128, 128], BF16); sst = sb("sst", [64, 2, 4, 129])
    psall = [psb(f"ps{i}", [128, 512]) for i in range(7)]
    acc0, acc1 = psall[0], psall[1]
    ps = psall[2:]
    ptr = psb("ptr", [128, 1024], BF16)
    cc_sems = []

    ld_q = "sp"; st_q = "pool"

    def dma(q, out, in_, reads, writes, sbuf):
        P.add(q, lambda e, o=out, i=in_: e.dma_start(out=o, in_=i), reads, writes, dma=sbuf)

    def load(sbuf, out, in_, dram=()):
        dma(ld_q, out, in_, list(dram), [sbuf], sbuf)

    def store(sbuf, out, in_, dram=()):
        dma(st_q, out, in_, [sbuf], list(dram), sbuf)

    def mm(out, lhsT, rhs, start, stop, reads, writes):
        P.add("pe", lambda e: e.matmul(out, lhsT=lhsT, rhs=rhs, start=start, stop=stop), reads, writes)

    def act(out, in_, func, reads, writes, bias=0.0, scale=1.0):
        P.add("act", lambda e: e.activation(out=out, in_=in_, func=func, bias=bias, scale=scale), reads, writes)

    def tt(eng, out, in0, in1, op, reads, writes):
        P.add(eng, lambda e: e.tensor_tensor(out=out, in0=in0, in1=in1, op=op), reads, writes)

    def ts(eng, out, in0, s1, op0, reads, writes, s2=None, op1=None):
        if op1 is None:
            P.add(eng, lambda e: e.tensor_scalar(out=out, in0=in0, scalar1=s1, scalar2=None, op0=op0), reads, writes)
        else:
            P.add(eng, lambda e: e.tensor_scalar(out=out, in0=in0, scalar1=s1, scalar2=s2, op0=op0, op1=op1), reads, writes)

    def stt(eng, out, in0, scalar, in1, op0, op1, reads, writes):
        P.add(eng, lambda e: e.scalar_tensor_tensor(out=out, in0=in0, scalar=scalar, in1=in1, op0=op0, op1=op1), reads, writes)

    def cp(eng, out, in_, reads, writes):
        if eng == "act":
            P.add("act", lambda e: e.copy(out=out, in_=in_), reads, writes)
        else:
            P.add(eng, lambda e: e.tensor_copy(out=out, in_=in_), reads, writes)

    def recip(out, in_, reads, writes):
        P.add("dve", lambda e: e.reciprocal(out=out, in_=in_), reads, writes)

    def transpose(out, in_, reads, writes):
        P.add("pe", lambda e: e.transpose(out, in_, cm[:, 6, :]), list(reads) + [cm], writes)

    rr = {"ps": 0, "wt": 0, "tmp": 0, "stg": 0, "zb": 0, "pT": 0, "wm": 0, "kk": 0}

    def nxt(key, lst):
        rr[key] = (rr[key] + 1) % len(lst)
        return lst[rr[key]]

    EPS_AP = lambda n=128, p0=0: cst[p0:p0 + n, 0:1]
    ONE_AP = lambda n=128, p0=0: cst[p0:p0 + n, 1:2]

    def rstd_from(psum, n, scale, out_buf, np_=128):
        act(out_buf[0:np_, 0:n], psum[0:np_, 0:n], AF.Ln, [psum, cst], [out_buf], bias=EPS_AP(np_), scale=scale)
        act(out_buf[0:np_, 0:n], out_buf[0:np_, 0:n], AF.Exp, [out_buf], [out_buf], scale=-0.5)

    P.add("pool", lambda e: e.memset(cst[:, 0:1], EPS), [], [cst])
    P.add("pool", lambda e: e.memset(cst[:, 1:2], 1.0), [], [cst])
    P.add("pool", lambda e: e.memset(vsw[:], 1.0), [], [vsw])
    P.add("pool", lambda e: e.memset(vcs[:], 1.0), [], [vcs])
    P.add("pool", lambda e: e.memset(mvts[:], 1.0), [], [mvts])
    P.add("pool", lambda e: e.memset(hvs[:], 1.0), [], [hvs])
    d_cm = DB("cmaskb"); castsem = Buf(None, "castsem")
    dma("pool", cmask_b, cmask_in, [], [d_cm], castsem)
    load(cm, cm[:], cmask_b, [d_cm])
    load(cmf, cmf[:], cmask_in)
    load(selw, selw[:], selw_in)
    load(scT, scT[:], cT_in)
    load(gf, gf[:], gf_in)
    for l in range(L):
        for i in range(14):
            dma("pool", w_ina[l, i], wina_in[l, i], [], [d_w], castsem)
        dma("pool", w_inb[l], winb_in[l], [], [d_w], castsem)
        dma("pool", w_uq[l], wuq_in[l], [], [d_w], castsem)
        dma("pool", w_ukv[l], wukv_in[l], [], [d_w], castsem)
        for i in range(16):
            dma("pool", w_outa[l, i], wouta_in[l, i], [], [d_w], castsem)
            dma("pool", w_outb[l, i], woutb_in[l, i], [], [d_w], castsem)
        for i in range(32):
            dma("pool", w_ff1[l, i], wff1_in[l, i], [], [d_w], castsem)
        for i in range(64):
            dma("pool", w_ff2[l, i], wff2_in[l, i], [], [d_w], castsem)
    xcp = Buf(None, "xcp")
    for i in range(9):
        c0 = i * TT; n = TT if i < 8 else CTX
        dma("sp", xT[:, c0:c0 + n], xT_in[:, c0:c0 + n], [], [d_x[i]], xcp)
    act(tmp[0][:, 0:32], scT[:].rearrange("p a b -> p (a b)"), AF.Exp, [scT], [tmp[0]], scale=-1.0)
    ts("dve", tmp[0][:, 0:32], tmp[0][:, 0:32], 1.0, ALU.add, [tmp[0]], [tmp[0]])
    recip(tmp[0][:, 0:32], tmp[0][:, 0:32], [tmp[0]], [tmp[0]])
    tt("dve", scT[:].rearrange("p a b -> p (a b)"), scT[:].rearrange("p a b -> p (a b)"), tmp[0][:, 0:32], ALU.mult, [scT, tmp[0]], [scT])

    onesb = cm[:, 7, :]
    onesf = cmf[:, 7, :]

    for l in range(L):
        need_ctx = not (last and l == L - 1)
        load(bmod, bmod[:], bmod_in[l]); load(g1, g1[:], g1_in[l]); load(g2, g2[:], g2_in[l])
        load(gq, gq[:], gq_in[l]); load(gkv, gkv[:], gkv_in[l]); load(gh, gh[:], gh_in[l])
        load(sk, sk[:], sink_in[l].partition_broadcast(128)); load(gb, gb[:], gbias_in[l].partition_broadcast(128))
        load(wuq_s, wuq_s[:].rearrange("p a b -> p (a b)"), w_uq[l], [d_w])
        load(wukv_s, wukv_s[:].rearrange("p a b -> p (a b)"), w_ukv[l], [d_w])
        act(sk[:], sk[:], AF.Exp, [sk], [sk])
        pm = acc0
        for j in range(96):
            for hf_ in range(2):
                w = nxt("wm", wm)
                load(w, w[:].rearrange("p a b -> p (a b)"), wmod_in[l, j, hf_])
                for k8 in range(8):
                    kc = hf_ * 8 + k8
                    mm(pm[:, 2 * j:2 * j + 2], w[:, k8, :], scT[:, kc, :], kc == 0, kc == KC - 1, [w, scT], [pm])
        tt("dve", modT[:], pm[:, 0:192].rearrange("p (a b) -> p a b", b=2),
           bmod[:].unsqueeze(2).to_broadcast([128, 96, 2]), ALU.add, [pm, bmod], [modT])
        for (a, g, off) in ((a1, g1, 16), (a2, g2, 64)):
            ts("dve", a[:], modT[:, off:off + 16, :], 1.0, ALU.add, [modT], [a])
            tt("dve", a[:], a[:], g[:].unsqueeze(2).to_broadcast([128, 16, 2]), ALU.mult, [a, g], [a])
        SH1, GT1, SH2, GT2 = 0, 32, 48, 80

        def norm_mod(n, r, a, shoff):
            for c in range(KC):
                act(big[:, c, 0:n], xt[:, c, 0:n], AF.Square, [xt], [big])
            p = nxt("ps", ps)
            for c in range(KC):
                mm(p[:, 0:n], onesb, big[:, c, 0:n], c == 0, c == KC - 1, [cm, big], [p])
            rstd_from(p, n, 1.0 / D, rs)
            for c in range(KC):
                t_ = nxt("tmp", tmp)
                tt("dve", t_[:, 0:n], xt[:, c, 0:n], rs[:, 0:n], ALU.mult, [xt, rs], [t_])
                act(hT[:, c, 0:n], t_[:, 0:n], AF.Identity, [t_, a, modT], [hT],
                    bias=modT[:, shoff + c, r:r + 1], scale=a[:, c, r:r + 1])

        def phaseA(ti):
            isctx = ti == 8
            c0 = ti * TT; n = CTX if isctx else TT; r = 1 if isctx else 0
            nsub = n // 128
            load(xt, xt[:, :, 0:n], xT[:, c0:c0 + n].rearrange("(c p) n -> p c n", p=128), [d_x[ti]])
            if not isctx:
                load(cosS, cosS[:], cos_in[:, c0:c0 + n]); load(sinS, sinS[:], sin_in[:, c0:c0 + n])
            norm_mod(n, r, a1, SH1)

            def rope(src_ps, np_, dst):
                z = nxt("zb", zb)
                cp("act", z[0:np_, 0:n], src_ps[0:np_, 0:n], [src_ps], [z])
                if isctx:
                    cp("dve", dst, z[0:np_, 0:n], [z], [dstbuf[0]])
                    return
                pr = nxt("ps", ps)
                mm(pr[0:np_, 0:n], cm[0:np_, 0, 0:np_], z[0:np_, 0:n], True, True, [cm, z], [pr])
                t1 = nxt("tmp", tmp)
                tt("dve", t1[0:np_, 0:n], z[0:np_, 0:n], cosS[0:np_, 0:n], ALU.mult, [z, cosS], [t1])
                t2 = nxt("tmp", tmp)
                tt("dve", t2[0:np_, 0:n], pr[0:np_, 0:n], sinS[0:np_, 0:n], ALU.mult, [pr, sinS], [t2])
                tt("dve", dst, t1[0:np_, 0:n], t2[0:np_, 0:n], ALU.add, [t1, t2], [dstbuf[0]])

            dstbuf = [None]

            def latent_norm(nch, gvec, src_chunks_done):
                p = nxt("ps", ps)
                for c in range(nch):
                    act(big[:, c, 0:n], zf[:, c, 0:n], AF.Square, [zf], [big])
                for c in range(nch):
                    mm(p[:, 0:n], onesb, big[:, c, 0:n], c == 0, c == nch - 1, [cm, big], [p])
                rstd_from(p, n, 1.0 / (nch * 128), rs)
                for c in range(nch):
                    stt("dve", cz[:, c, 0:n], zf[:, c, 0:n], gvec[:, c:c + 1], rs[:, 0:n], ALU.mult, ALU.mult,
                        [zf, gvec, rs], [cz])

            for blk in range(15):
                w = nxt("wt", wt)
                if blk < 14:
                    load(w, w[:], w_ina[l, blk], [d_w]); wv = w[:].rearrange("p (a b) -> p a b", b=256)
                else:
                    load(w, w[:, 0:KC * 80], w_inb[l], [d_w]); wv = w[:, 0:KC * 80].rearrange("p (a b) -> p a b", b=80)
                if blk in (10, 11):
                    s = nxt("stg", stg)
                    for j in range(nsub):
                        p = nxt("ps", ps)
                        for kc in range(KC):
                            mm(p[:, 0:256], hT[:, kc, j * 128:(j + 1) * 128], wv[:, kc, :], kc == 0, kc == KC - 1, [hT, w], [p])
                        cp("act" if j % 2 else "dve", s[:, j, 0:256], p[:, 0:256], [p], [s])
                    store(s, mvt[c0:c0 + n, (blk - 10) * 256:(blk - 9) * 256].rearrange("(j p) f -> p j f", p=128), s[:, 0:nsub, 0:256], [d_m])
                    continue
                if blk == 14:
                    p = nxt("ps", ps)
                    for kc in range(KC):
                        mm(p[0:64, 0:n], wv[:, kc, 0:64], hT[:, kc, 0:n], kc == 0, kc == KC - 1, [w, hT], [p])
                    s = nxt("stg", stg); dstbuf[0] = s
                    rope(p, 64, s[0:64, 0, 0:n])
                    if isctx:
                        store(s, kc_r[:, :], s[0:64, 0, 0:n], [d_kc])
                    else:
                        store(s, GR[0:512, :].rearrange("(f t) n -> f t n", t=8)[:, ti, :], s[0:64, 0, 0:n], [d_gr])
                    for j in range(nsub):
                        p = nxt("ps", ps)
                        for kc in range(KC):
                            mm(p[:, 0:16], hT[:, kc, j * 128:(j + 1) * 128], wv[:, kc, 64:80], kc == 0, kc == KC - 1, [hT, w], [p])
                        tt("dve", gstg[:, j, :], p[:, 0:16], gb[:], ALU.add, [p, gb], [gstg])
                    store(gstg, gts[c0:c0 + n, :].rearrange("(j p) f -> p j f", p=128), gstg[:, 0:nsub, :], [d_m])
                    continue
                for oc2 in range(2):
                    oc = blk * 2 + oc2
                    p = nxt("ps", ps)
                    for kc in range(KC):
                        mm(p[:, 0:n], wv[:, kc, oc2 * 128:(oc2 + 1) * 128], hT[:, kc, 0:n], kc == 0, kc == KC - 1, [w, hT], [p])
                    if oc < 4:
                        cp("act", zf[:, oc, 0:n], p[:, 0:n], [p], [zf])
                        if oc == 3:
                            latent_norm(4, gq, None)
                            s = nxt("stg", stg); s2 = nxt("stg", stg)
                            for qc in range(6):
                                p2 = nxt("ps", ps)
                                for kc in range(4):
                                    mm(p2[:, 0:n], wuq_s[:, kc, qc * 128:(qc + 1) * 128], cz[:, kc, 0:n], kc == 0, kc == 3, [wuq_s, cz], [p2])
                                if qc < 4:
                                    cp("act", s[:, qc, 0:n], p2[:, 0:n], [p2], [s])
                                else:
                                    dstbuf[0] = s2
                                    rope(p2, 128, s2[:, qc - 4, 0:n])
                            store(s, qmla[0:512, c0:c0 + n].rearrange("(c p) n -> p c n", p=128), s[:, 0:4, 0:n], [d_q[ti]])
                            store(s2, qmla[512:768, c0:c0 + n].rearrange("(c p) n -> p c n", p=128), s2[:, 0:2, 0:n], [d_q[ti]])
                    elif oc < 6:
                        cp("act", zf[:, oc - 4, 0:n], p[:, 0:n], [p], [zf])
                        if oc == 5:
                            latent_norm(2, gkv, None)
                            s = nxt("stg", stg)
                            for h in range(4):
                                p2 = nxt("ps", ps)
                                for kc in range(2):
                                    mm(p2[:, 0:n], wukv_s[:, kc, h * 128:(h + 1) * 128], cz[:, kc, 0:n], kc == 0, kc == 1, [wukv_s, cz], [p2])
                                cp("act", s[:, h, 0:n], p2[:, 0:n], [p2], [s])
                            if isctx:
                                store(s, kc_n[:, :].rearrange("(c p) n -> p c n", p=128), s[:, 0:4, 0:n], [d_kc])
                            else:
                                for i_ in range(2):
                                    store(s, GK[i_][:, :].rearrange("(c p t) n -> p c t n", p=128, t=8)[:, :, ti, :], s[:, 2 * i_:2 * i_ + 2, 0:n], [d_gk[i_]])
                            s = nxt("stg", stg)
                            for j in range(nsub):
                                p2 = nxt("ps", ps)
                                for kc in range(2):
                                    mm(p2[:, :], cz[:, kc, j * 128:(j + 1) * 128], wukv_s[:, kc, 512:1024], kc == 0, kc == 1, [cz, wukv_s], [p2])
                                cp("dve", s[:, j, :], p2[:, :], [p2], [s])
                            if isctx:
                                store(s, vc_m[:, :].rearrange("(j p) f -> p j f", p=128), s[:, 0:nsub, :], [d_kc])
                            else:
                                store(s, GV[c0 // 2048][c0 % 2048:c0 % 2048 + n, :].rearrange("(j p) f -> p j f", p=128), s[:, 0:nsub, :], [d_gv[c0 // 2048]])
                    elif oc < 15:
                        s = nxt("stg", stg); dstbuf[0] = s
                        rope(p, 128, s[:, 0, 0:n])
                        if oc < 14:
                            store(s, swaq[(oc - 6) * 128:(oc - 5) * 128, c0:c0 + n], s[:, 0, 0:n], [d_sw[ti]])
                        else:
                            store(s, swak[:, c0:c0 + n], s[:, 0, 0:n], [d_sw[ti]])
                            if ti == 7:
                                store(s, GR[512:544, :].rearrange("r (q j) -> (r q) j", q=4), s[:, 0, 384:512], [d_gr])
                    elif oc == 15 or 18 <= oc < 20:
                        z = nxt("zb", zb)
                        cp("act", z[:, 0:n], p[:, 0:n], [p], [z])
                        if oc >= 18:
                            store(z, mk[(oc - 18) * 128:(oc - 17) * 128, c0:c0 + n], z[:, 0:n], [d_m])
                        s = nxt("stg", stg)
                        for j in range(nsub):
                            transpose(ptr[:, j * 128:(j + 1) * 128], z[:, j * 128:(j + 1) * 128], [z], [ptr])
                        cp("dve", s[:, 0, 0:n], ptr[:, 0:n], [ptr], [s])
                        sv = s[:, 0, 0:n].rearrange("p (j f) -> p j f", f=128)
                        if oc == 15:
                            store(s, swav[c0:c0 + n, :].rearrange("(j p) f -> p j f", p=128), sv, [d_sw[ti]])
                            if ti == 7:
                                store(s, GR[544:576, :].rearrange("r (q j) -> (r q) j", q=4), s[:, 0, 384:512], [d_gr])
                        else:
                            store(s, mkt[c0:c0 + n, (oc - 18) * 128:(oc - 17) * 128].rearrange("(j p) f -> p j f", p=128), sv, [d_m])
                    elif oc < 18:
                        z = nxt("zb", zb)
                        cp("act", z[:, 0:n], p[:, 0:n], [p], [z])
                        store(z, mq[(oc - 16) * 128:(oc - 15) * 128, c0:c0 + n], z[:, 0:n], [d_m])
                    else:
                        z = nxt("zb", zb)
                        cp("dve", z[:, 0:n], p[:, 0:n], [p], [z])
                        store(z, zo[(oc - 24) * 128:(oc - 23) * 128, c0:c0 + n], z[:, 0:n], [d_m])

        phaseA(8)
        for ti in range(8):
            phaseA(ti)

        def mlstm_tile(tok0, direction, first_dir, init_from=None):
            d = direction
            load(gt, gt[:], gts[tok0:tok0 + 128, :], [d_m])
            load(mqs, mqs[:], mq[:, tok0:tok0 + 128].rearrange("(h d) n -> d h n", d=64), [d_m])
            load(mks, mks[:], mk[:, tok0:tok0 + 128].rearrange("(h d) n -> d h n", d=64), [d_m])
            load(mkts, mkts[:], mkt[tok0:tok0 + 128, :], [d_m])
            load(mvts, mvts[:, :, 0:128], mvt[tok0:tok0 + 128, :].rearrange("p (h f) -> p h f", f=128), [d_m])
            fo = d * 8
            act(lf[:, 0:4], gt[:, fo + 4:fo + 8], AF.Exp, [gt], [lf], scale=-1.0)
            act(lf[:, 0:4], lf[:, 0:4], AF.Ln, [lf, cst], [lf], bias=ONE_AP())
            ts("dve", lf[:, 0:4], lf[:, 0:4], -1.0, ALU.mult, [lf], [lf])
            cp("dve", lfr[:, 0:4, :], lf[:, 0:4].unsqueeze(2).to_broadcast([128, 4, 64]), [lf], [lfr])
            Tm = cmf[:, 4 + d, :]
            p = nxt("ps", ps)
            mm(p[:, 0:4], Tm, lf[:, 0:4], True, True, [cmf, lf], [p])
            cp("dve", bb[:, 0:4], p[:, 0:4], [p], [bb])
            p = nxt("ps", ps)
            mm(p[:, 0:4], cmf[:, 4, :], lf[:, 0:4], True, False, [cmf, lf], [p])
            mm(p[:, 0:4], cmf[:, 5, :], lf[:, 0:4], False, True, [cmf, lf], [p])
            tt("dve", gg[:, 0:4], p[:, 0:4], lf[:, 0:4], ALU.subtract, [p, lf], [gg])
            tt("dve", bk[:, 0:4], gt[:, fo:fo + 4], bb[:, 0:4], ALU.subtract, [gt, bb], [bk])
            tt("dve", kst[:, 0:4], bk[:, 0:4], gg[:, 0:4], ALU.add, [bk, gg], [kst])
            act(bk[:, 0:4], bk[:, 0:4], AF.Exp, [bk], [bk])
            act(kst[:, 0:4], kst[:, 0:4], AF.Exp, [kst], [kst])
            order = (0, 1) if d == 0 else (1, 0)
            for h in range(4):
                C = Cst[d][h]
                pb = nxt("ps", ps)
                mm(pb[0:64, 0:128], lfr[:, h, :], Tm, True, True, [lfr, cmf], [pb])
                act(abc[:, :], pb[0:64, 0:128], AF.Exp, [pb], [abc])
                ecol = (63, 127) if d == 0 else (0, 64)
                for c in range(2):
                    cp("dve", eg[:, c:c + 1], abc[:, ecol[c]:ecol[c] + 1], [abc], [eg])
                stt("dve", qp[:, :], mqs[:, h, :], 0.125, abc[:, :], ALU.mult, ALU.mult, [mqs, abc], [qp])
                pS = nxt("ps", ps)
                mm(pS[:, 0:128], mks[:, h, :], qp[:, :], True, True, [mks, qp], [pS])
                stt("dve", ptm[:, :], pS[:, 0:128], bk[:, h:h + 1], cm[:, 4 + d, :], ALU.mult, ALU.mult, [pS, bk, cm], [ptm])
                ts("dve", kpp[:, :], mkts[:, h * 64:(h + 1) * 64], kst[:, h:h + 1], ALU.mult, [mkts, kst], [kpp])
                pN = acc0; pD = acc1
                mm(pN[:, 0:128], mvts[:, h, 0:128], ptm[:, :], True, False, [mvts, ptm], [pN])
                mm(pD[:, 0:128], onesb, ptm[:, :], True, False, [cm, ptm], [pD])
                for ci, c in enumerate(order):
                    cs = slice(c * 64, (c + 1) * 64)
                    cp("act", Cbf[:, :], C[:, 0:128], [C], [Cbf])
                    cp("dve", nrep[:, :], C[:, 128:129].to_broadcast([64, 128]), [C], [nrep])
                    lastc = ci == 1
                    mm(pN[:, cs], Cbf[:, :], qp[:, cs], False, lastc, [Cbf, qp], [pN])
                    mm(pD[:, cs], nrep[:, :], qp[:, cs], False, lastc, [nrep, qp], [pD])
                    pC = nxt("ps", ps)
                    mm(pC[0:64, 0:129], kpp[cs, :], mvts[cs, h, :], True, True, [kpp, mvts], [pC])
                    stt("dve", C[:, :], C[:, :], eg[:, c:c + 1], pC[0:64, 0:129], ALU.mult, ALU.add, [C, eg, pC], [C])
                act(dm[:, :], pD[:, 0:128], AF.Abs, [pD], [dm])
                ts("dve", dm[:, :], dm[:, :], 1.0, ALU.max, [dm], [dm])
                recip(dm[:, :], dm[:, :], [dm], [dm])
                tt("dve", hdir[:, :], pN[:, 0:128], dm[:, :], ALU.mult, [pN, dm], [hdir])
                if first_dir:
                    store(hdir, hf[h * 128:(h + 1) * 128, tok0:tok0 + 128], hdir[:, :], [d_hf])
                else:
                    load(hfs, hfs[:, :], hf[h * 128:(h + 1) * 128, tok0:tok0 + 128], [d_hf])
                    load(zos, zos[:, :], zo[h * 128:(h + 1) * 128, tok0:tok0 + 128], [d_m])
                    tt("dve", hfs[:, :], hfs[:, :], hdir[:, :], ALU.add, [hfs, hdir], [hfs])
                    act(sqm[:, :], hfs[:, :], AF.Square, [hfs], [sqm])
                    pq = nxt("ps", ps)
                    mm(pq[:, 0:128], onesb, sqm[:, :], True, True, [cm, sqm], [pq])
                    act(sg[:, :], pq[:, 0:128], AF.Ln, [pq, cst], [sg], bias=EPS_AP(), scale=1.0 / 128)
                    act(sg[:, :], sg[:, :], AF.Exp, [sg], [sg], scale=-0.5)
                    stt("dve", hfs[:, :], hfs[:, :], gh[:, h:h + 1], sg[:, :], ALU.mult, ALU.mult, [hfs, gh, sg], [hfs])
                    act(sg[:, :], zos[:, :], AF.Exp, [zos], [sg], scale=-1.0)
                    ts("dve", sg[:, :], sg[:, :], 1.0, ALU.add, [sg], [sg])
                    recip(sg[:, :], sg[:, :], [sg], [sg])
                    tt("dve", yms[:, :], hfs[:, :], sg[:, :], ALU.mult, [hfs, sg], [yms])
                    store(yms, ymT[h * 128:(h + 1) * 128, tok0:tok0 + 128], yms[:, :], [d_ym])

        def zero_states(d):
            for h in range(4):
                P.add("pool", lambda e, t=Cst[d][h]: e.memset(t[:], 0.0), [], [Cst[d][h]])

        zero_states(0); zero_states(1)
        for tl in range(2):
            mlstm_tile(T + tl * 128, 0, True)
        if need_ctx:
            for tl in (1, 0):
                mlstm_tile(T + tl * 128, 1, False)
        for tl in range(32):
            mlstm_tile(tl * 128, 0, True)
        for h in range(4):
            store(Cst[0][h], s_in[h * 64:(h + 1) * 64, :], Cst[0][h][:, :], [d_sin])

        groups = [[0, 1], [2, 3], [4, 5], [6, 7]]
        def allgather(src, dst, dsrc, ddst):
            P.add("pool", lambda e: e.collective_compute("AllGather", ALU.bypass, replica_groups=groups,
                                                         ins=[src.opt()], outs=[dst.opt()]),
                  [dsrc], [ddst], dma=CCSem(None, "cc"))
        allgather(s_in, s_out, d_sin, d_sout)
        allgather(GR, GRo, d_gr, d_gro)
        for i_ in range(2):
            allgather(GK[i_], GKo[i_], d_gk[i_], d_gko[i_])
            allgather(GV[i_], GVo[i_], d_gv[i_], d_gvo[i_])
        load(sst, sst[:], s_out.rearrange("(r h d) f -> d r h f", r=2, h=4), [d_sout])
        for h in range(4):
            C = Cst[1][h]
            ts("dve", C[:, :], sst[:, 0, h, :], selw[0:64, 0:1], ALU.mult, [sst, selw], [C])
            stt("dve", C[:, :], sst[:, 1, h, :], selw[0:64, 1:2], C[:, :], ALU.mult, ALU.add, [sst, selw, C], [C])
        for tl in range(31, -1, -1):
            mlstm_tile(tl * 128, 1, False)
        for k_ in range(2):
            load(hk2, hk2[:, k_, :, :], bass_ap_halo_k(GRo, k_), [d_gro])
        load(hv2, hv2[:], bass_ap_halo_v(GRo), [d_gro])
        for g in range(2):
            ts("dve", hks[:, g, :], hk2[:, 0, g, :], selw[0:64, 0:1], ALU.mult, [hk2, selw], [hks])
            stt("dve", hks[:, g, :], hk2[:, 1, g, :], selw[0:64, 1:2], hks[:, g, :], ALU.mult, ALU.add, [hk2, selw, hks], [hks])
            ts("dve", hvs[:, g, 0:64], hv2[:, 0, g * 64:(g + 1) * 64], selw[:, 0:1], ALU.mult, [hv2, selw], [hvs])
            stt("dve", hvs[:, g, 0:64], hv2[:, 1, g * 64:(g + 1) * 64], selw[:, 1:2], hvs[:, g, 0:64], ALU.mult, ALU.add, [hv2, selw, hvs], [hvs])
        load(kcs, kcs[:], swak[:, T:NT].rearrange("(g d) n -> d g n", d=64), [d_sw[8]])
        for c_ in range(2):
            load(vcs, vcs[:, c_, :, 0:64], swav[T + c_ * 128:T + (c_ + 1) * 128, :].rearrange("p (g f) -> p g f", f=64), [d_sw[8]])

        def mla(ti):
            isctx = ti == 8
            c0 = ti * TT; n = CTX if isctx else TT
            load(qn, qn[:, :, 0:n], qmla[0:512, c0:c0 + n].rearrange("(h p) n -> p h n", p=128), [d_q[ti]])
            load(qr, qr[:, :, 0:n], qmla[512:768, c0:c0 + n].rearrange("(h p) n -> p h n", p=64), [d_q[ti]])
            for h in range(4):
                grps = [("c", 0)] + ([] if isctx else [("g", r_, q_) for r_ in range(2) for q_ in range(4)])
                pO = acc0; pDn = acc1
                nchunks_total = 2 + (0 if isctx else 64)
                done = 0
                for gsp in grps:
                    sl = rr["kk"] = (rr.get("kk", 0) + 1) % 2
                    K, R, V = kk[sl], kr[sl], vv[sl]
                    if gsp[0] == "c":
                        nch = 2
                        load(K, K[:, 0:CTX], kc_n[h * 128:(h + 1) * 128, :], [d_kc])
                        load(R, R[:, 0:CTX], kc_r[:, :], [d_kc])
                        load(V, V[:, 0:2, :], vc_m[:, h * 128:(h + 1) * 128].rearrange("(c p) f -> p c f", p=128), [d_kc])
                    else:
                        nch = 8
                        r_ = gsp[1]; q_ = gsp[2]
                        t0 = q_ * 2
                        load(K, K[:, :].rearrange("p (t n) -> p t n", n=512), GKo[h // 2][r_ * 2048:(r_ + 1) * 2048, :].rearrange("(f t) n -> f t n", t=8)[(h % 2) * 128:(h % 2 + 1) * 128, t0:t0 + 2, :], [d_gko[h // 2]])
                        load(R, R[:, :].rearrange("p (t n) -> p t n", n=512), GRo[r_ * 576:r_ * 576 + 512, :].rearrange("(f t) n -> f t n", t=8)[:, t0:t0 + 2, :], [d_gro])
                        vb = r_ * 2048 + (q_ % 2) * 1024
                        load(V, V[:, :, :], GVo[q_ // 2][vb:vb + 1024, h * 128:(h + 1) * 128].rearrange("(c p) f -> p c f", p=128), [d_gvo[q_ // 2]])
                    for c in range(nch):
                        pS = nxt("ps", ps)
                        mm(pS[:, 0:n], K[:, c * 128:(c + 1) * 128], qn[:, h, 0:n], True, False, [K, qn], [pS])
                        mm(pS[:, 0:n], R[:, c * 128:(c + 1) * 128], qr[:, h, 0:n], False, True, [R, qr], [pS])
                        pt = nxt("pT", pT)
                        act(pt[:, 0:n], pS[:, 0:n], AF.Exp, [pS], [pt], scale=MLA_SCALE)
                        mm(pO[:, 0:n], V[:, c, :], pt[:, 0:n], done == 0, done == nchunks_total - 1, [V, pt], [pO])
                        mm(pDn[:, 0:n], onesb, pt[:, 0:n], done == 0, done == nchunks_total - 1, [cm, pt], [pDn])
                        done += 1
                recip(rec[:, 0:n], pDn[:, 0:n], [pDn], [rec])
                tt("dve", yT[:, h, 0:n], pO[:, 0:n], rec[:, 0:n], ALU.mult, [pO, rec], [yT])

        def swa(ti):
            isctx = ti == 8
            c0 = ti * TT; n = CTX if isctx else TT
            load(qsw, qsw[0:64, :, 0:n], swaq[:, c0:c0 + n].rearrange("(h d) n -> d h n", d=64), [d_sw[ti]])
            if not isctx:
                lo = max(c0 - 128, 0); hi = min(c0 + TT + 128, T)
                o = lo - (c0 - 128)
                deps = [d_sw[i] for i in range(max(ti - 1, 0), min(ti + 2, 8))]
                load(ksw, ksw[:, :, o:o + hi - lo], swak[:, lo:hi].rearrange("(g d) n -> d g n", d=64), deps)
                for c_ in range((hi - lo) // 128):
                    load(vsw, vsw[:, o // 128 + c_, :, 0:64],
                         swav[lo + c_ * 128:lo + (c_ + 1) * 128, :].rearrange("p (g f) -> p g f", f=64), deps)
            for j in range(n // 128):
                jb = ti * 4 + j
                for g in range(2):
                    for hh in range(2):
                        h0 = g * 8 + hh * 4
                        chunks = [("c", 0, None), ("c", 1, None)]
                        if not isctx:
                            if jb > 0:
                                chunks.append(("l", j, 1))
                            chunks.append(("l", j + 1, None))
                            if jb < 31:
                                chunks.append(("l", j + 2, 2))
                            else:
                                chunks.append(("h", 0, 3))
                        pO = acc0
                        rhs = qsw[0:64, h0:h0 + 4, j * 128:(j + 1) * 128]
                        for ci, (kind, idx, msk) in enumerate(chunks):
                            if kind == "c":
                                kl = kcs[:, g, idx * 128:(idx + 1) * 128]; vl = vcs[:, idx, g, :]; kb, vb = kcs, vcs
                            elif kind == "l":
                                kl = ksw[:, g, idx * 128:(idx + 1) * 128]; vl = vsw[:, idx, g, :]; kb, vb = ksw, vsw
                            else:
                                kl = hks[:, g, :]; vl = hvs[:, g, :]; kb, vb = hks, hvs
                            pS = nxt("ps", ps)
                            mm(pS[:, :].rearrange("p (a b) -> p a b", b=128), kl, rhs, True, True, [kb, qsw], [pS])
                            pt = nxt("pT", pT)
                            act(pt[:, :], pS[:, :], AF.Exp, [pS], [pt], scale=0.125)
                            if msk is not None:
                                tt("pool", pt[:, :].rearrange("p (a b) -> p a b", b=128), pt[:, :].rearrange("p (a b) -> p a b", b=128),
                                   cm[:, msk, :].unsqueeze(1).to_broadcast([128, 4, 128]), ALU.mult, [pt, cm], [pt])
                            mm(pO[0:65, :], vl, pt[:, :], ci == 0, ci == len(chunks) - 1, [vb, pt], [pO])
                        for q in range(4):
                            ts("dve", dsw[64:65, q * 128:(q + 1) * 128], pO[64:65, q * 128:(q + 1) * 128],
                               sk[64:65, h0 + q:h0 + q + 1], ALU.add, [pO, sk], [dsw])
                        recip(dsw[64:65, :], dsw[64:65, :], [dsw], [dsw])
                        pB = nxt("ps", ps)
                        mm(pB[0:64, :], cmf[64:65, 7, 0:64], dsw[64:65, :], True, True, [cmf, dsw], [pB])
                        cp("act", rbs[:, :], pB[0:64, :], [pB], [rbs])
                        tt("dve", ysw[0:64, h0:h0 + 4, j * 128:(j + 1) * 128], pO[0:64, :].rearrange("p (a b) -> p a b", b=128),
                           rbs[:, :].rearrange("p (a b) -> p a b", b=128), ALU.mult, [pO, rbs], [ysw])

        def phaseC(ti):
            isctx = ti == 8
            c0 = ti * TT; n = CTX if isctx else TT; r = 1 if isctx else 0
            mla(ti); swa(ti)
            load(yT, yT[:, 4:8, 0:n], ymT[:, c0:c0 + n].rearrange("(c p) n -> p c n", p=128), [d_ym])
            load(xt, xt[:, :, 0:n], xT[:, c0:c0 + n].rearrange("(c p) n -> p c n", p=128), [d_x[ti]])
            for oc in range(16):
                w = nxt("wt", wt)
                load(w, w[:, 0:1024], w_outa[l, oc], [d_w])
                load(w, w[0:64, 1024:1024 + 2048], w_outb[l, oc], [d_w])
                wa = w[:, 0:1024].rearrange("p (a b) -> p a b", b=128)
                wb = w[0:64, 1024:3072].rearrange("p (a b) -> p a b", b=128)
                p = nxt("ps", ps)
                for kc in range(8):
                    mm(p[:, 0:n], wa[:, kc, :], yT[:, kc, 0:n], kc == 0, False, [w, yT], [p])
                for hd in range(16):
                    mm(p[:, 0:n], wb[:, hd, :], ysw[0:64, hd, 0:n], False, hd == 15, [w, ysw], [p])
                stt("dve", xt[:, oc, 0:n], p[:, 0:n], modT[:, GT1 + oc, r:r + 1], xt[:, oc, 0:n], ALU.mult, ALU.add,
                    [p, modT, xt], [xt])
            norm_mod(n, r, a2, SH2)
            for half in range(4):
                for blk in range(8):
                    w = nxt("wt", wt)
                    load(w, w[:], w_ff1[l, half * 8 + blk], [d_w])
                    wv = w[:].rearrange("p (a b) -> p a b", b=256)
                    for o4 in range(2):
                        hc = blk * 2 + o4
                        p = nxt("ps", ps)
                        for kc in range(KC):
                            mm(p[:, 0:n], wv[:, kc, o4 * 128:(o4 + 1) * 128], hT[:, kc, 0:n], kc == 0, kc == KC - 1, [w, hT], [p])
                        t_ = nxt("tmp", tmp)
                        ts("dve", t_[:, 0:n], p[:, 0:n], 0.0, ALU.max, [p], [t_])
                        act(big[:, hc, 0:n], t_[:, 0:n], AF.Square, [t_], [big])
                for oc in range(16):
                    w = nxt("wt", wt)
                    load(w, w[:, 0:2048], w_ff2[l, half * 16 + oc], [d_w])
                    wv = w[:, 0:2048].rearrange("p (a b) -> p a b", b=128)
                    p = nxt("ps", ps)
                    for kc in range(16):
                        mm(p[:, 0:n], wv[:, kc, :], big[:, kc, 0:n], kc == 0, kc == 15, [w, big], [p])
                    stt("dve", xt[:, oc, 0:n], p[:, 0:n], modT[:, GT2 + oc, r:r + 1], xt[:, oc, 0:n], ALU.mult, ALU.add,
                        [p, modT, xt], [xt])
            if last and l == L - 1:
                for c in range(KC):
                    act(big[:, c, 0:n], xt[:, c, 0:n], AF.Square, [xt], [big])
                p = nxt("ps", ps)
                for c in range(KC):
                    mm(p[:, 0:n], onesb, big[:, c, 0:n], c == 0, c == KC - 1, [cm, big], [p])
                rstd_from(p, n, 1.0 / D, rs)
                for c in range(KC):
                    stt("dve", xt[:, c, 0:n], xt[:, c, 0:n], gf[:, c:c + 1], rs[:, 0:n], ALU.mult, ALU.mult, [xt, gf, rs], [xt])
                store(xt, out_ap[:, c0:c0 + n].rearrange("(c p) n -> p c n", p=128), xt[:, :, 0:n], [d_out])
            else:
                store(xt, xT[:, c0:c0 + n].rearrange("(c p) n -> p c n", p=128), xt[:, :, 0:n], [d_x[ti]])
                if l == L - 1:
                    store(xt, out_ap[:, c0:c0 + n].rearrange("(c p) n -> p c n", p=128), xt[:, :, 0:n], [d_out])

        if need_ctx:
            phaseC(8)
        for ti in range(8):
            phaseC(ti)

    P.add("pool", lambda e: e.engine_nop() if hasattr(e, "engine_nop") else e.memset(cst[:, 7:8], 0.0), [d_out], [cst])
    P.emit(stack)
    stack.close()
    return nc


def bass_ap_halo_k(g_out, k_):
    return g_out[k_ * 576 + 512:k_ * 576 + 544, :].rearrange("r (q j) -> (r q) j", q=4).rearrange("(g d) j -> d g j", d=64)


def bass_ap_halo_v(g_out):
    return g_out.rearrange("(k x) n -> k x n", k=2)[:, 544:576, :].rearrange("k r (q j) -> (r q) k j", q=4)


def _const_masks():
    m = np.zeros((128, 8, 128), np.float32)
    i = np.arange(128)[:, None]; t = np.arange(128)[None, :]
    pm = np.arange(128)
    src = np.where((pm % 32) < 16, pm + 16, pm - 16)
    m[src, 0, pm] = 1.0
    m[:, 1, :] = (i >= t); m[:, 2, :] = (i <= t); m[:, 3, :] = (i + t >= 127)
    same = (i // 64) == (t // 64)
    m[:, 4, :] = same & (i <= t); m[:, 5, :] = same & (i >= t)
    m[:, 6, :] = (i == t); m[:, 7, :] = 1.0
    return m


def _rope_tables(pos):
    half = 32
    inv = (np.float32(10000.0) ** (-np.arange(0, half, 2, dtype=np.float32) / np.float32(half))).astype(np.float32)
    row = (pos // 64).astype(np.float32); col = (pos % 64).astype(np.float32)
    ar = row[:, None] * inv; ac = col[:, None] * inv
    ang = np.concatenate([ar, ar, ac, ac], axis=-1)
    cos = np.cos(ang).astype(np.float32).T; sin = np.sin(ang).astype(np.float32).T
    sgn = np.where((np.arange(64) % 32) < 16, -1.0, 1.0).astype(np.float32)[:, None]
    sin = sin * sgn
    return np.ascontiguousarray(np.concatenate([cos, cos], 0)), np.ascontiguousarray(np.concatenate([sin, sin], 0))


def _fm(v, k):
    return np.ascontiguousarray(v.reshape(k, 128).T)


def _prep_gate_parts(inp, odd):
    L = DEPTH
    gate = np.arange(3136, 3152)
    if odd:
        gate = np.concatenate([gate[8:16], gate[0:8]])
    cols = np.concatenate([np.arange(3584 - 16 + 16 + 0, 3584 - 16 + 16 + 0)[:0], np.arange(768, 832), gate])
    wb = inp["w_in"][:, :, cols].reshape(L, KC, 128, 80)
    out = {"win_b": np.ascontiguousarray(wb.transpose(0, 2, 1, 3), dtype=np.float32).reshape(L, 128, KC * 80)}
    gbv = inp["mlstm_gate_bias"].reshape(L, 16)
    if odd:
        gbv = np.concatenate([gbv[:, 8:16], gbv[:, 0:8]], axis=1)
    out["gbias"] = np.ascontiguousarray(gbv, dtype=np.float32)
    return out


def _prep_weights(inp, odd):
    L = DEPTH
    A = lambda a: np.ascontiguousarray(a, dtype=np.float32)
    w = {}
    wm = inp["w_mod"].reshape(L, 2, 8, 128, 96, 128)
    w["wmod"] = A(wm.transpose(0, 4, 1, 3, 2, 5)).reshape(L, 96, 2, 128, 1024)
    w["bmodT"] = A(inp["b_mod"].reshape(L, 96, 128).transpose(0, 2, 1))
    w["g1T"] = A(inp["g_norm1"].reshape(L, KC, 128).transpose(0, 2, 1))
    w["g2T"] = A(inp["g_norm2"].reshape(L, KC, 128).transpose(0, 2, 1))
    gate = np.arange(3136, 3152)
    if odd:
        gate = np.concatenate([gate[8:16], gate[0:8]])
    cols = np.concatenate([np.arange(0, 768), np.arange(832, 3136), np.arange(3152, 3664), np.arange(768, 832), gate])
    win = inp["w_in"][:, :, cols]
    wa = win[:, :, :3584].reshape(L, KC, 128, 14, 256)
    w["win_a"] = A(wa.transpose(0, 3, 2, 1, 4)).reshape(L, 14, 128, KC * 256)
    wb = win[:, :, 3584:].reshape(L, KC, 128, 80)
    w["win_b"] = A(wb.transpose(0, 2, 1, 3)).reshape(L, 128, KC * 80)
    w["gqT"] = A(inp["mla_g_q"].reshape(L, 4, 128).transpose(0, 2, 1))
    w["gkvT"] = A(inp["mla_g_kv"].reshape(L, 2, 128).transpose(0, 2, 1))
    qc = np.concatenate([np.concatenate([h * 192 + np.arange(128) for h in range(4)]),
                         np.concatenate([h * 192 + 128 + np.arange(64) for h in range(4)])])
    wuq = inp["mla_w_uq"][:, :, qc].reshape(L, 4, 128, 768)
    w["wuq"] = A(wuq.transpose(0, 2, 1, 3)).reshape(L, 128, 4 * 768)
    kvc = np.concatenate([np.concatenate([h * 256 + np.arange(128) for h in range(4)]),
                          np.concatenate([h * 256 + 128 + np.arange(128) for h in range(4)])])
    wukv = inp["mla_w_ukv"][:, :, kvc].reshape(L, 2, 128, 1024)
    w["wukv"] = A(wukv.transpose(0, 2, 1, 3)).reshape(L, 128, 2 * 1024)
    w["sink"] = A(inp["swa_sink"])
    gbv = inp["mlstm_gate_bias"].reshape(L, 16)
    if odd:
        gbv = np.concatenate([gbv[:, 8:16], gbv[:, 0:8]], axis=1)
    w["gbias"] = A(gbv)
    w["ghT"] = A(inp["mlstm_g_h"].transpose(0, 2, 1))
    wo = inp["w_out"]
    rows_a = np.concatenate([np.arange(0, 512), np.arange(1536, 2048)])
    woa = wo[:, rows_a, :].reshape(L, 8, 128, 16, 128)
    w["wout_a"] = A(woa.transpose(0, 3, 2, 1, 4)).reshape(L, 16, 128, 1024)
    wob = wo[:, 512:1536, :].reshape(L, 16, 64, 16, 128)
    w["wout_b"] = A(wob.transpose(0, 3, 2, 1, 4)).reshape(L, 16, 64, 2048)
    f1 = inp["w_ff1"].reshape(L, KC, 128, 32, 256)
    w["wff1"] = A(f1.transpose(0, 3, 2, 1, 4)).reshape(L, 32, 128, KC * 256)
    f2 = inp["w_ff2"].reshape(L, 4, 16, 128, 16, 128)
    w["wff2"] = A(f2.transpose(0, 1, 4, 3, 2, 5)).reshape(L, 64, 128, 16 * 128)
    w["gfT"] = _fm(np.asarray(inp["g_final"], np.float32), KC)
    w["cmask"] = _const_masks()
    return w


_PER_LAYER = ("wmod", "bmodT", "g1T", "g2T", "win_a", "win_b", "gqT", "gkvT", "wuq", "wukv", "sink", "gbias",
              "ghT", "wout_a", "wout_b", "wff1", "wff2")
_NC_CACHE = {}


def _get_nc(layers, last):
    key = (layers, last)
    if key not in _NC_CACHE:
        _NC_CACHE[key] = build_program(layers, True, last)
    return _NC_CACHE[key]


FUSED = True


def kernel(**inp):
    inp = {k: np.asarray(v) for k, v in inp.items()}
    x = inp["x"]; ctx = inp["ctx"]; c = inp["c"]; c_ctx = inp["c_ctx"]
    wts = [_prep_weights(inp, 0), None]
    wts[1] = dict(wts[0])
    wts[1].update(_prep_gate_parts(inp, 1))
    per_core = []
    for core in range(8):
        b = core // 2; odd = core % 2
        if odd:
            xs = x[b, T:SEQ][::-1]; cs = ctx[b][::-1]; pos = (SEQ - 1 - np.arange(T))
        else:
            xs = x[b, 0:T]; cs = ctx[b]; pos = np.arange(T)
        xT = np.ascontiguousarray(np.concatenate([xs, cs], 0).T.astype(np.float32))
        cT = np.stack([_fm(c[b].astype(np.float32), KC), _fm(c_ctx.astype(np.float32), KC)], axis=-1)
        cos2, sin2 = _rope_tables(pos)
        selw = np.zeros((128, 2), np.float32); selw[:, 1 - odd] = 1.0
        per_core.append({"xT": xT, "cT": np.ascontiguousarray(cT), "cos2": cos2, "sin2": sin2, "selw": selw})

    def maps(lsl, xTs):
        out = []
        for core in range(8):
            w = wts[core % 2]
            m = dict(per_core[core])
            if xTs is not None:
                m["xT"] = xTs[core]
            for k in _PER_LAYER:
                m[k] = w[k][lsl]
            m["gfT"] = w["gfT"]; m["cmask"] = w["cmask"]
            out.append(m)
        return out

    if FUSED:
        nc = _get_nc(DEPTH, True)
        res = run_bass_kernel_spmd(nc, maps(slice(0, DEPTH), None), core_ids=list(range(8)))
        outs = [np.asarray(r["outT"]) for r in res.results]
    else:
        xTs = None
        for l in range(DEPTH):
            lastl = l == DEPTH - 1
            nc = _get_nc(1, lastl)
            res = run_bass_kernel_spmd(nc, maps(slice(l, l + 1), xTs), core_ids=list(range(8)))
            if lastl:
                outs = [np.asarray(r["outT"]) for r in res.results]
            else:
                xTs = [np.asarray(r["xT_out"]) for r in res.results]
    out = np.empty((BATCH, SEQ, D), np.float32)
    for core in range(8):
        b = core // 2
        o = outs[core].T
        if core % 2:
            out[b, T:SEQ] = o[::-1]
        else:
            out[b, 0:T] = o
    return out
```

```python
import contextlib
import numpy as np
import concourse.bass as bass
import concourse.mybir as mybir
from concourse.bass_utils import run_bass_kernel_spmd

F32, BF16 = mybir.dt.float32, mybir.dt.bfloat16
AF = mybir.ActivationFunctionType
ALU = mybir.AluOpType

D = 2048; KC = 16; DEPTH = 4; SEQ = 8192; BATCH = 4; CTX = 256
T = 4096; NT = T + CTX; TT = 512
DFF = 8192
EPS = 1e-6
MLA_SCALE = 192.0 ** -0.5
NROWS_G = 8768


class Buf:
    def __init__(self, t, name):
        self.t = t; self.name = name
        self.lastw = None; self.reads = []
        self.semid = None; self.dcnt = 0
        self.sg = self

    def __getitem__(self, k):
        return self.t[k]


class Op:
    __slots__ = ("eng", "fn", "deps", "signal", "val", "dma", "idx")

    def __init__(self, eng, fn, dma=None):
        self.eng = eng; self.fn = fn; self.deps = []; self.signal = False
        self.val = None; self.dma = dma


class Prog:
    ENGS = ("pe", "act", "dve", "pool", "sp")

    def __init__(self, nc):
        self.nc = nc
        self.ops = {e: [] for e in self.ENGS}
        self.nsem = 0
        self.semholders = []
        self.extra_sems = 0

    def new_sem(self):
        self.nsem += 1
        return self.nsem - 1

    def _tok(self, op):
        if op.dma is not None:
            b = op.dma
            return ("d", b.semid, b.dcnt * 16)
        return ("o", op)

    def add(self, eng, fn, reads=(), writes=(), dma=None, cc=False):
        op = Op(eng, fn, dma)
        deps = []
        for b in reads:
            if b.lastw is not None:
                deps.append(b.lastw)
        for b in writes:
            if b.lastw is not None:
                deps.append(b.lastw)
            deps.extend(b.reads)
        res = []
        for d in deps:
            if d.dma is not None:
                res.append(("d", d.dma.semid, d.dma.dcnt * 16 if not isinstance(d.dma, CCSem) else 1))
            else:
                if d.eng == eng and eng == "pe":
                    continue
                d.signal = True
                res.append(("o", d))
        op.deps = res
        if dma is not None:
            if dma.semid is None:
                dma.semid = self.new_sem()
            dma.dcnt += 1
        for b in reads:
            b.reads.append(op)
        for b in writes:
            b.lastw = op; b.reads = []
        self.ops[eng].append(op)
        return op

    def emit(self, stack):
        nc = self.nc
        engsem = {e: self.new_sem() for e in ("pe", "act", "dve", "pool")}
        sems = [stack.enter_context(nc.semaphore(f"s{i}")) for i in range(self.nsem)]
        for e in ("pe", "act", "dve", "pool"):
            c = 0
            for op in self.ops[e]:
                if op.dma is None and op.signal:
                    c += 1; op.val = c
        block = stack.enter_context(nc.Block())
        handles = {"pe": block.tensor, "act": block.scalar, "dve": block.vector,
                   "pool": block.gpsimd, "sp": block.sync}

        def run(ename):
            def body(eng):
                waited = {}
                for op in self.ops[ename]:
                    for d in op.deps:
                        if d[0] == "d":
                            sid, v = d[1], d[2]
                        else:
                            sid, v = engsem[d[1].eng], d[1].val
                        if waited.get(sid, 0) >= v:
                            continue
                        waited[sid] = v
                        eng.wait_ge(sems[sid], v)
                    ins = op.fn(eng)
                    if op.dma is not None:
                        if isinstance(op.dma, CCSem):
                            ins.then_inc(sems[op.dma.semid])
                        else:
                            ins.then_inc(sems[op.dma.semid], 16)
                    elif op.signal:
                        ins.then_inc(sems[engsem[ename]], 1)
            return body

        for e in self.ENGS:
            if self.ops[e]:
                handles[e](run(e))


class CCSem(Buf):
    pass


def build_program(layers, first, last, debug=()):
    L = layers
    nc = bass.Bass("TRN2", target_bir_lowering=False)
    P = Prog(nc)
    stack = contextlib.ExitStack()

    def din(name, shape, dt=F32):
        return nc.dram_tensor(name, list(shape), dt, kind="ExternalInput").ap()

    def dscr(name, shape, dt):
        kind = "ExternalOutput" if name in debug else "Internal"
        return nc.dram_tensor(name, list(shape), dt, kind=kind).ap()

    xT_in = din("xT", [D, NT])
    cT_in = din("cT", [128, KC, 2])
    cos_in = din("cos2", [128, T]); sin_in = din("sin2", [128, T])
    selw_in = din("selw", [128, 2])
    cmask_in = din("cmask", [128, 8, 128])
    wmod_in = din("wmod", [L, 96, 128, KC * 128])
    bmod_in = din("bmodT", [L, 128, 96])
    g1_in = din("g1T", [L, 128, KC]); g2_in = din("g2T", [L, 128, KC])
    wina_in = din("win_a", [L, 14, 128, KC * 256]); winb_in = din("win_b", [L, 128, KC * 80])
    gq_in = din("gqT", [L, 128, 4]); gkv_in = din("gkvT", [L, 128, 2])
    wuq_in = din("wuq", [L, 128, 4 * 768]); wukv_in = din("wukv", [L, 128, 2 * 1024])
    sink_in = din("sink", [L, 16]); gbias_in = din("gbias", [L, 16])
    gh_in = din("ghT", [L, 128, 4])
    wouta_in = din("wout_a", [L, 16, 128, 8 * 128]); woutb_in = din("wout_b", [L, 16, 64, 16 * 128])
    wff1_in = din("wff1", [L, 32, 128, KC * 256]); wff2_in = din("wff2", [L, 64, 128, 16 * 128])
    gf_in = din("gfT", [128, KC])
    if last:
        out_ap = nc.dram_tensor("outT", [D, T], F32, kind="ExternalOutput").ap()
    else:
        out_ap = nc.dram_tensor("xT_out", [D, NT], F32, kind="ExternalOutput").ap()

    xT = dscr("xTs", [D, NT], F32)
    w_ina = dscr("b_win_a", [L, 14, 128, KC * 256], BF16); w_inb = dscr("b_win_b", [L, 128, KC * 80], BF16)
    w_uq = dscr("b_wuq", [L, 128, 4 * 768], BF16); w_ukv = dscr("b_wukv", [L, 128, 2 * 1024], BF16)
    w_outa = dscr("b_wouta", [L, 16, 128, 8 * 128], BF16); w_outb = dscr("b_woutb", [L, 16, 64, 16 * 128], BF16)
    w_ff1 = dscr("b_wff1", [L, 32, 128, KC * 256], BF16); w_ff2 = dscr("b_wff2", [L, 64, 128, 16 * 128], BF16)
    cmask_b = dscr("b_cmask", [128, 8, 128], BF16)
    w_modb = dscr("b_wmod", [L, 96, 128, KC * 128], BF16)
    qmla = dscr("qmla", [768, NT], BF16)
    GK = [dscr(f"gk{i}", [2048, 512], BF16) for i in range(2)]; GKo = [dscr(f"gko{i}", [4096, 512], BF16) for i in range(2)]
    GV = [dscr(f"gv{i}", [2048, 512], BF16) for i in range(2)]; GVo = [dscr(f"gvo{i}", [4096, 512], BF16) for i in range(2)]
    GR = dscr("gr", [576, 512], BF16); GRo = dscr("gro", [1152, 512], BF16)
    s_in = dscr("s_in", [256, 129], F32); s_out = dscr("s_out", [512, 129], F32)
    kc_n = dscr("kc_n", [512, CTX], BF16); kc_r = dscr("kc_r", [64, CTX], BF16); vc_m = dscr("vc_m", [CTX, 512], BF16)
    swaq = dscr("swaq", [1024, NT], BF16); swak = dscr("swak", [128, NT], BF16); swav = dscr("swav", [NT, 128], BF16)
    mq = dscr("mq", [256, NT], BF16); mk = dscr("mk", [256, NT], BF16)
    mkt = dscr("mkt", [NT, 256], BF16); mvt = dscr("mvt", [NT, 512], BF16)
    gts = dscr("gts", [NT, 16], F32); zo = dscr("zo", [512, NT], BF16)
    hf = dscr("hf", [512, NT], F32); ymT = dscr("ymT", [512, NT], BF16)

    def DB(name):
        return Buf(None, name)
    d_x = [DB(f"x{i}") for i in range(9)]
    d_w = [DB(f"wts{i}") for i in range(L)]; d_wm = [DB(f"wm{i}") for i in range(L)]; d_q = [DB(f"q{i}") for i in range(9)]
    d_gk = [DB("gk0"), DB("gk1")]; d_gv = [DB("gv0"), DB("gv1")]; d_gr = DB("gr")
    d_gko = [DB("gko0"), DB("gko1")]; d_gvo = [DB("gvo0"), DB("gvo1")]; d_gro = DB("gro")
    d_sin = DB("sin"); d_sout = DB("sout")
    d_kc = DB("kc"); d_sw = [DB(f"sw{i}") for i in range(9)]; d_m = DB("m"); d_hf = DB("hf"); d_ym = DB("ym")
    d_out = DB("out")

    def sb(name, shape, dt=F32):
        return Buf(stack.enter_context(nc.sbuf_tensor(name, list(shape), dt)), name)

    def psb(name, shape, dt=F32):
        return Buf(stack.enter_context(nc.psum_tensor(name, list(shape), dt)), name)

    xt = sb("xt", [128, KC, TT])
    hT = sb("hT", [128, KC, TT], BF16)
    big = sb("big", [128, 16, TT], BF16)
    wt = [sb(f"wt{i}", [128, 4096], BF16) for i in range(2)]
    yT = sb("yT", [128, 8, TT], BF16)
    ysw = big; qsw = hT
    wuq_s = sb("wuq_s", [128, 4, 768], BF16); wukv_s = sb("wukv_s", [128, 2, 1024], BF16)
    zf = sb("zf", [128, 4, TT]); cz = sb("cz", [128, 4, TT], BF16)
    rs = sb("rs", [128, TT]); tmp = [sb(f"tmp{i}", [128, TT]) for i in range(2)]
    stg = [sb(f"stg{i}", [128, 4, TT], BF16) for i in range(2)]
    zb = [sb(f"zb{i}", [128, TT], BF16) for i in range(2)]
    gstg = sb("gstg", [128, 4, 16])
    cosS = sb("cosS", [128, TT]); sinS = sb("sinS", [128, TT])
    cm = sb("cm", [128, 8, 128], BF16); cmf = sb("cmf", [128, 3, 128])
    cst = sb("cst", [128, 8])
    selw = sb("selw_s", [128, 2])
    scT = sb("scT", [128, KC, 2]); scb = sb("scb", [128, KC, 2], BF16); wm = [sb(f"wm{i}", [128, KC, 128], BF16) for i in range(2)]
    modT = sb("modT", [128, 96, 2]); bmod = sb("bmod", [128, 96])
    g1 = sb("g1", [128, KC]); g2 = sb("g2", [128, KC]); gf = sb("gf", [128, KC])
    a1 = sb("a1", [128, KC, 2]); a2 = sb("a2", [128, KC, 2])
    gq = sb("gq", [128, 4]); gkv = sb("gkv", [128, 2]); gh = sb("gh", [128, 4])
    sk = sb("sk", [128, 16]); gb = sb("gb", [128, 16])
    qn = sb("qn", [128, 4, TT], BF16); qr = sb("qr", [64, 4, TT], BF16)
    kk = [sb(f"kk{i}", [128, 1024], BF16) for i in range(3)]
    kr = [sb(f"kr{i}", [64, 1024], BF16) for i in range(3)]
    vv = [sb(f"vv{i}", [128, 8, 128], BF16) for i in range(3)]
    pT = [sb(f"pT{i}", [128, TT], BF16) for i in range(3)]
    rec = rs
    ksw = sb("ksw", [64, 2, 768], BF16); vsw = sb("vsw", [128, 6, 2, 65], BF16)
    kcs = sb("kcs", [64, 2, CTX], BF16); vcs = sb("vcs", [128, 2, 2, 65], BF16)
    hk2 = sb("hk2", [64, 2, 2, 128], BF16); hv2 = sb("hv2", [128, 2, 128], BF16)
    hks = sb("hks", [64, 2, 128], BF16); hvs = sb("hvs", [128, 2, 65], BF16)
    rbs = zf
    mqs = sb("mqs", [64, 4, 128], BF16); mks = sb("mks", [64, 4, 128], BF16)
    mkts = sb("mkts", [128, 256], BF16); mvts = sb("mvts", [128, 4, 129], BF16)
    gt = sb("gt", [128, 16]); lf = sb("lf", [128, 8]); lfr = sb("lfr", [128, 4, 64])
    bb = sb("bb", [128, 8]); gg = sb("gg", [128, 8]); bk = sb("bk", [128, 8]); kst = sb("kst", [128, 8])
    Cst = [[sb(f"C{d}{h}", [64, 129]) for h in range(4)] for d in range(2)]
    msets = []
    for si in range(2):
        msets.append((sb(f"abc{si}", [64, 128]), sb(f"eg{si}", [64, 2]), sb(f"qp{si}", [64, 128], BF16),
                      sb(f"ptm{si}", [128, 128], BF16), sb(f"kpp{si}", [128, 64], BF16),
                      sb(f"Cbf{si}", [64, 128], BF16), sb(f"nrep{si}", [64, 128], BF16),
                      sb(f"dm{si}", [128, 128]), sb(f"hdir{si}", [128, 128]), sb(f"hfs{si}", [128, 128]),
                      sb(f"zos{si}", [128, 128], BF16), sb(f"sg{si}", [128, 128]), sb(f"sqm{si}", [128, 128], BF16),
                      sb(f"yms{si}", [128, 128], BF16)))
    sst = sb("sst", [64, 2, 129])
    psall = [psb(f"ps{i}", [128, 512]) for i in range(7)]
    acc0, acc1 = psall[0], psall[1]
    ps = psall[2:]
    ptr = psb("ptr", [128, 1024], BF16)
    cc_sems = []

    ld_q = "sp"; st_q = "pool"

    def dma(q, out, in_, reads, writes, sbuf):
        P.add(q, lambda e, o=out, i=in_: e.dma_start(out=o, in_=i), reads, writes, dma=sbuf)

    def load(sbuf, out, in_, dram=()):
        dma(ld_q, out, in_, list(dram), [sbuf], sbuf.sg)

    def store(sbuf, out, in_, dram=()):
        dma(st_q, out, in_, [sbuf], list(dram), sbuf.sg)

    def share(bufs):
        for b_ in bufs[1:]:
            b_.sg = bufs[0]

    share([bmod, g1, g2, gq, gkv, gh, sk, gb, gf, selw, scT, cm, cmf])
    share([gt, mqs, mks, mkts, mvts])
    share([hk2, hv2, kcs, vcs, sst])
    share([Cst[0][0], Cst[0][1], Cst[0][2], Cst[0][3]])
    share([cosS, sinS])

    def mm(out, lhsT, rhs, start, stop, reads, writes):
        P.add("pe", lambda e: e.matmul(out, lhsT=lhsT, rhs=rhs, start=start, stop=stop), reads, writes)

    def act(out, in_, func, reads, writes, bias=0.0, scale=1.0):
        P.add("act", lambda e: e.activation(out=out, in_=in_, func=func, bias=bias, scale=scale), reads, writes)

    def tt(eng, out, in0, in1, op, reads, writes):
        P.add(eng, lambda e: e.tensor_tensor(out=out, in0=in0, in1=in1, op=op), reads, writes)

    def ts(eng, out, in0, s1, op0, reads, writes, s2=None, op1=None):
        if op1 is None:
            P.add(eng, lambda e: e.tensor_scalar(out=out, in0=in0, scalar1=s1, scalar2=None, op0=op0), reads, writes)
        else:
            P.add(eng, lambda e: e.tensor_scalar(out=out, in0=in0, scalar1=s1, scalar2=s2, op0=op0, op1=op1), reads, writes)

    def stt(eng, out, in0, scalar, in1, op0, op1, reads, writes):
        P.add(eng, lambda e: e.scalar_tensor_tensor(out=out, in0=in0, scalar=scalar, in1=in1, op0=op0, op1=op1), reads, writes)

    def cp(eng, out, in_, reads, writes):
        if eng == "act":
            P.add("act", lambda e: e.copy(out=out, in_=in_), reads, writes)
        else:
            P.add(eng, lambda e: e.tensor_copy(out=out, in_=in_), reads, writes)

    def recip(out, in_, reads, writes):
        P.add("dve", lambda e: e.reciprocal(out=out, in_=in_), reads, writes)

    def transpose(out, in_, reads, writes):
        P.add("pe", lambda e: e.transpose(out, in_, cm[:, 6, :]), list(reads) + [cm], writes)

    rr = {"ps": 0, "wt": 0, "tmp": 0, "stg": 0, "zb": 0, "pT": 0, "wm": 0, "kk": 0}

    def nxt(key, lst):
        rr[key] = (rr[key] + 1) % len(lst)
        return lst[rr[key]]

    EPS_AP = lambda n=128, p0=0: cst[p0:p0 + n, 0:1]
    ONE_AP = lambda n=128, p0=0: cst[p0:p0 + n, 1:2]

    def rstd_from(psum, n, scale, out_buf, np_=128):
        act(out_buf[0:np_, 0:n], psum[0:np_, 0:n], AF.Ln, [psum, cst], [out_buf], bias=EPS_AP(np_), scale=scale)
        act(out_buf[0:np_, 0:n], out_buf[0:np_, 0:n], AF.Exp, [out_buf], [out_buf], scale=-0.5)

    P.add("pool", lambda e: e.memset(cst[:, 0:1], EPS), [], [cst])
    P.add("pool", lambda e: e.memset(cst[:, 1:2], 1.0), [], [cst])
    P.add("pool", lambda e: e.memset(vsw[:], 1.0), [], [vsw])
    P.add("pool", lambda e: e.memset(vcs[:], 1.0), [], [vcs])
    P.add("pool", lambda e: e.memset(mvts[:], 1.0), [], [mvts])
    P.add("pool", lambda e: e.memset(hvs[:], 1.0), [], [hvs])
    d_cm = DB("cmaskb"); castsem = Buf(None, "castsem")
    dma("pool", cmask_b, cmask_in, [], [d_cm], castsem)
    load(cm, cm[:], cmask_b, [d_cm])
    load(cmf, cmf[:, 0:2, :], cmask_in[:, 4:6, :])
    load(cmf, cmf[:, 2:3, :], cmask_in[:, 7:8, :])
    load(selw, selw[:], selw_in)
    load(scT, scT[:], cT_in)
    load(gf, gf[:], gf_in)
    def emit_casts(l):
        cs_m = Buf(None, f"castm{l}")
        cs_ = Buf(None, f"cast{l}")
        for i in range(8):
            dma("pool", w_modb[l, i * 12:(i + 1) * 12], wmod_in[l, i * 12:(i + 1) * 12], [], [d_wm[l]], cs_m)
        for i in range(14):
            dma("pool", w_ina[l, i], wina_in[l, i], [], [d_w[l]], cs_)
        dma("pool", w_inb[l], winb_in[l], [], [d_w[l]], cs_)
        dma("pool", w_uq[l], wuq_in[l], [], [d_w[l]], cs_)
        dma("pool", w_ukv[l], wukv_in[l], [], [d_w[l]], cs_)
        for i in range(4):
            dma("pool", w_outa[l, i * 4:(i + 1) * 4], wouta_in[l, i * 4:(i + 1) * 4], [], [d_w[l]], cs_)
            dma("pool", w_outb[l, i * 4:(i + 1) * 4], woutb_in[l, i * 4:(i + 1) * 4], [], [d_w[l]], cs_)
        for i in range(8):
            dma("pool", w_ff1[l, i * 4:(i + 1) * 4], wff1_in[l, i * 4:(i + 1) * 4], [], [d_w[l]], cs_)
        for i in range(8):
            dma("pool", w_ff2[l, i * 8:(i + 1) * 8], wff2_in[l, i * 8:(i + 1) * 8], [], [d_w[l]], cs_)
    emit_casts(0)
    xcp = Buf(None, "xcp")
    for i in range(9):
        c0 = i * TT; n = TT if i < 8 else CTX
        dma("sp", xT[:, c0:c0 + n], xT_in[:, c0:c0 + n], [], [d_x[i]], xcp)
    act(tmp[0][:, 0:32], scT[:].rearrange("p a b -> p (a b)"), AF.Exp, [scT], [tmp[0]], scale=-1.0)
    ts("dve", tmp[0][:, 0:32], tmp[0][:, 0:32], 1.0, ALU.add, [tmp[0]], [tmp[0]])
    recip(tmp[0][:, 0:32], tmp[0][:, 0:32], [tmp[0]], [tmp[0]])
    tt("dve", scT[:].rearrange("p a b -> p (a b)"), scT[:].rearrange("p a b -> p (a b)"), tmp[0][:, 0:32], ALU.mult, [scT, tmp[0]], [scT])

    cp("dve", scb[:], scT[:], [scT], [scb])
    onesb = cm[:, 7, :]

    for l in range(L):
        need_ctx = not (last and l == L - 1)
        load(bmod, bmod[:], bmod_in[l]); load(g1, g1[:], g1_in[l]); load(g2, g2[:], g2_in[l])
        load(gq, gq[:], gq_in[l]); load(gkv, gkv[:], gkv_in[l]); load(gh, gh[:], gh_in[l])
        load(sk, sk[:], sink_in[l].partition_broadcast(128)); load(gb, gb[:], gbias_in[l].partition_broadcast(128))
        load(wuq_s, wuq_s[:].rearrange("p a b -> p (a b)"), w_uq[l], [d_w[l]])
        load(wukv_s, wukv_s[:].rearrange("p a b -> p (a b)"), w_ukv[l], [d_w[l]])
        act(sk[:], sk[:], AF.Exp, [sk], [sk])
        pm = acc0
        for j in range(96):
            w = nxt("wm", wm)
            load(w, w[:].rearrange("p a b -> p (a b)"), w_modb[l, j], [d_wm[l]])
            for kc in range(KC):
                mm(pm[:, 2 * j:2 * j + 2], w[:, kc, :], scb[:, kc, :], kc == 0, kc == KC - 1, [w, scb], [pm])
        tt("dve", modT[:], pm[:, 0:192].rearrange("p (a b) -> p a b", b=2),
           bmod[:].unsqueeze(2).to_broadcast([128, 96, 2]), ALU.add, [pm, bmod], [modT])
        for (a, g, off) in ((a1, g1, 16), (a2, g2, 64)):
            ts("dve", a[:], modT[:, off:off + 16, :], 1.0, ALU.add, [modT], [a])
            tt("dve", a[:], a[:], g[:].unsqueeze(2).to_broadcast([128, 16, 2]), ALU.mult, [a, g], [a])
        SH1, GT1, SH2, GT2 = 0, 32, 48, 80

        def norm_mod(n, r, a, shoff):
            for c in range(KC):
                act(big[:, c, 0:n], xt[:, c, 0:n], AF.Square, [xt], [big])
            p = nxt("ps", ps)
            for c in range(KC):
                mm(p[:, 0:n], onesb, big[:, c, 0:n], c == 0, c == KC - 1, [cm, big], [p])
            rstd_from(p, n, 1.0 / D, rs)
            for c in range(KC):
                t_ = nxt("tmp", tmp)
                tt("dve", t_[:, 0:n], xt[:, c, 0:n], rs[:, 0:n], ALU.mult, [xt, rs], [t_])
                act(hT[:, c, 0:n], t_[:, 0:n], AF.Identity, [t_, a, modT], [hT],
                    bias=modT[:, shoff + c, r:r + 1], scale=a[:, c, r:r + 1])

        def phaseA(ti):
            isctx = ti == 8
            c0 = ti * TT; n = CTX if isctx else TT; r = 1 if isctx else 0
            nsub = n // 128
            load(xt, xt[:, :, 0:n], xT[:, c0:c0 + n].rearrange("(c p) n -> p c n", p=128), [d_x[ti]])
            if not isctx:
                load(cosS, cosS[:], cos_in[:, c0:c0 + n]); load(sinS, sinS[:], sin_in[:, c0:c0 + n])
            norm_mod(n, r, a1, SH1)

            def rope(src_ps, np_, dst):
                z = nxt("zb", zb)
                cp("act", z[0:np_, 0:n], src_ps[0:np_, 0:n], [src_ps], [z])
                if isctx:
                    cp("dve", dst, z[0:np_, 0:n], [z], [dstbuf[0]])
                    return
                pr = nxt("ps", ps)
                mm(pr[0:np_, 0:n], cm[0:np_, 0, 0:np_], z[0:np_, 0:n], True, True, [cm, z], [pr])
                t1 = nxt("tmp", tmp)
                tt("dve", t1[0:np_, 0:n], z[0:np_, 0:n], cosS[0:np_, 0:n], ALU.mult, [z, cosS], [t1])
                t2 = nxt("tmp", tmp)
                tt("dve", t2[0:np_, 0:n], pr[0:np_, 0:n], sinS[0:np_, 0:n], ALU.mult, [pr, sinS], [t2])
                tt("dve", dst, t1[0:np_, 0:n], t2[0:np_, 0:n], ALU.add, [t1, t2], [dstbuf[0]])

            dstbuf = [None]

            def latent_norm(nch, gvec, src_chunks_done):
                p = nxt("ps", ps)
                for c in range(nch):
                    act(big[:, c, 0:n], zf[:, c, 0:n], AF.Square, [zf], [big])
                for c in range(nch):
                    mm(p[:, 0:n], onesb, big[:, c, 0:n], c == 0, c == nch - 1, [cm, big], [p])
                rstd_from(p, n, 1.0 / (nch * 128), rs)
                for c in range(nch):
                    stt("dve", cz[:, c, 0:n], zf[:, c, 0:n], gvec[:, c:c + 1], rs[:, 0:n], ALU.mult, ALU.mult,
                        [zf, gvec, rs], [cz])

            for blk in range(15):
                w = nxt("wt", wt)
                if blk < 14:
                    load(w, w[:], w_ina[l, blk], [d_w[l]]); wv = w[:].rearrange("p (a b) -> p a b", b=256)
                else:
                    load(w, w[:, 0:KC * 80], w_inb[l], [d_w[l]]); wv = w[:, 0:KC * 80].rearrange("p (a b) -> p a b", b=80)
                if blk in (10, 11):
                    s = nxt("stg", stg)
                    for j in range(nsub):
                        p = nxt("ps", ps)
                        for kc in range(KC):
                            mm(p[:, 0:256], hT[:, kc, j * 128:(j + 1) * 128], wv[:, kc, :], kc == 0, kc == KC - 1, [hT, w], [p])
                        cp("act" if j % 2 else "dve", s[:, j, 0:256], p[:, 0:256], [p], [s])
                    store(s, mvt[c0:c0 + n, (blk - 10) * 256:(blk - 9) * 256].rearrange("(j p) f -> p j f", p=128), s[:, 0:nsub, 0:256], [d_m])
                    continue
                if blk == 14:
                    p = nxt("ps", ps)
                    for kc in range(KC):
                        mm(p[0:64, 0:n], wv[:, kc, 0:64], hT[:, kc, 0:n], kc == 0, kc == KC - 1, [w, hT], [p])
                    s = nxt("stg", stg); dstbuf[0] = s
                    rope(p, 64, s[0:64, 0, 0:n])
                    if isctx:
                        store(s, kc_r[:, :], s[0:64, 0, 0:n], [d_kc])
                    else:
                        store(s, GR[0:512, :].rearrange("(f t) n -> f t n", t=8)[:, ti, :], s[0:64, 0, 0:n], [d_gr])
                    for j in range(nsub):
                        p = nxt("ps", ps)
                        for kc in range(KC):
                            mm(p[:, 0:16], hT[:, kc, j * 128:(j + 1) * 128], wv[:, kc, 64:80], kc == 0, kc == KC - 1, [hT, w], [p])
                        tt("dve", gstg[:, j, :], p[:, 0:16], gb[:], ALU.add, [p, gb], [gstg])
                    store(gstg, gts[c0:c0 + n, :].rearrange("(j p) f -> p j f", p=128), gstg[:, 0:nsub, :], [d_m])
                    continue
                for oc2 in range(2):
                    oc = blk * 2 + oc2
                    p = nxt("ps", ps)
                    for kc in range(KC):
                        mm(p[:, 0:n], wv[:, kc, oc2 * 128:(oc2 + 1) * 128], hT[:, kc, 0:n], kc == 0, kc == KC - 1, [w, hT], [p])
                    if oc < 4:
                        cp("act", zf[:, oc, 0:n], p[:, 0:n], [p], [zf])
                        if oc == 3:
                            latent_norm(4, gq, None)
                            s = nxt("stg", stg); s2 = nxt("stg", stg)
                            for qc in range(6):
                                p2 = nxt("ps", ps)
                                for kc in range(4):
                                    mm(p2[:, 0:n], wuq_s[:, kc, qc * 128:(qc + 1) * 128], cz[:, kc, 0:n], kc == 0, kc == 3, [wuq_s, cz], [p2])
                                if qc < 4:
                                    cp("act", s[:, qc, 0:n], p2[:, 0:n], [p2], [s])
                                else:
                                    dstbuf[0] = s2
                                    rope(p2, 128, s2[:, qc - 4, 0:n])
                            store(s, qmla[0:512, c0:c0 + n].rearrange("(c p) n -> p c n", p=128), s[:, 0:4, 0:n], [d_q[ti]])
                            store(s2, qmla[512:768, c0:c0 + n].rearrange("(c p) n -> p c n", p=128), s2[:, 0:2, 0:n], [d_q[ti]])
                    elif oc < 6:
                        cp("act", zf[:, oc - 4, 0:n], p[:, 0:n], [p], [zf])
                        if oc == 5:
                            latent_norm(2, gkv, None)
                            s = nxt("stg", stg)
                            for h in range(4):
                                p2 = nxt("ps", ps)
                                for kc in range(2):
                                    mm(p2[:, 0:n], wukv_s[:, kc, h * 128:(h + 1) * 128], cz[:, kc, 0:n], kc == 0, kc == 1, [wukv_s, cz], [p2])
                                cp("act", s[:, h, 0:n], p2[:, 0:n], [p2], [s])
                            if isctx:
                                store(s, kc_n[:, :].rearrange("(c p) n -> p c n", p=128), s[:, 0:4, 0:n], [d_kc])
                            else:
                                for i_ in range(2):
                                    store(s, GK[i_][:, :].rearrange("(c p t) n -> p c t n", p=128, t=8)[:, :, ti, :], s[:, 2 * i_:2 * i_ + 2, 0:n], [d_gk[i_]])
                            s = nxt("stg", stg)
                            for j in range(nsub):
                                p2 = nxt("ps", ps)
                                for kc in range(2):
                                    mm(p2[:, :], cz[:, kc, j * 128:(j + 1) * 128], wukv_s[:, kc, 512:1024], kc == 0, kc == 1, [cz, wukv_s], [p2])
                                cp("dve", s[:, j, :], p2[:, :], [p2], [s])
                            if isctx:
                                store(s, vc_m[:, :].rearrange("(j p) f -> p j f", p=128), s[:, 0:nsub, :], [d_kc])
                            else:
                                store(s, GV[c0 // 2048][c0 % 2048:c0 % 2048 + n, :].rearrange("(j p) f -> p j f", p=128), s[:, 0:nsub, :], [d_gv[c0 // 2048]])
                    elif oc < 15:
                        s = nxt("stg", stg); dstbuf[0] = s
                        rope(p, 128, s[:, 0, 0:n])
                        if oc < 14:
                            store(s, swaq[(oc - 6) * 128:(oc - 5) * 128, c0:c0 + n], s[:, 0, 0:n], [d_sw[ti]])
                        else:
                            store(s, swak[:, c0:c0 + n], s[:, 0, 0:n], [d_sw[ti]])
                            if ti == 7:
                                store(s, GR[512:544, :].rearrange("r (q j) -> (r q) j", q=4), s[:, 0, 384:512], [d_gr])
                    elif oc == 15 or 18 <= oc < 20:
                        z = nxt("zb", zb)
                        cp("act", z[:, 0:n], p[:, 0:n], [p], [z])
                        if oc >= 18:
                            store(z, mk[(oc - 18) * 128:(oc - 17) * 128, c0:c0 + n], z[:, 0:n], [d_m])
                        s = nxt("stg", stg)
                        for j in range(nsub):
                            transpose(ptr[:, j * 128:(j + 1) * 128], z[:, j * 128:(j + 1) * 128], [z], [ptr])
                        cp("dve", s[:, 0, 0:n], ptr[:, 0:n], [ptr], [s])
                        sv = s[:, 0, 0:n].rearrange("p (j f) -> p j f", f=128)
                        if oc == 15:
                            store(s, swav[c0:c0 + n, :].rearrange("(j p) f -> p j f", p=128), sv, [d_sw[ti]])
                            if ti == 7:
                                store(s, GR[544:576, :].rearrange("r (q j) -> (r q) j", q=4), s[:, 0, 384:512], [d_gr])
                        else:
                            store(s, mkt[c0:c0 + n, (oc - 18) * 128:(oc - 17) * 128].rearrange("(j p) f -> p j f", p=128), sv, [d_m])
                    elif oc < 18:
                        z = nxt("zb", zb)
                        cp("act", z[:, 0:n], p[:, 0:n], [p], [z])
                        store(z, mq[(oc - 16) * 128:(oc - 15) * 128, c0:c0 + n], z[:, 0:n], [d_m])
                    else:
                        z = nxt("zb", zb)
                        cp("dve", z[:, 0:n], p[:, 0:n], [p], [z])
                        store(z, zo[(oc - 24) * 128:(oc - 23) * 128, c0:c0 + n], z[:, 0:n], [d_m])

        phaseA(8)
        for ti in range(8):
            phaseA(ti)
        if l + 1 < L:
            emit_casts(l + 1)

        def mlstm_tile(tok0, direction, first_dir, init_from=None):
            d = direction
            load(gt, gt[:], gts[tok0:tok0 + 128, :], [d_m])
            load(mqs, mqs[:], mq[:, tok0:tok0 + 128].rearrange("(h d) n -> d h n", d=64), [d_m])
            load(mks, mks[:], mk[:, tok0:tok0 + 128].rearrange("(h d) n -> d h n", d=64), [d_m])
            load(mkts, mkts[:], mkt[tok0:tok0 + 128, :], [d_m])
            load(mvts, mvts[:, :, 0:128], mvt[tok0:tok0 + 128, :].rearrange("p (h f) -> p h f", f=128), [d_m])
            fo = d * 8
            act(lf[:, 0:4], gt[:, fo + 4:fo + 8], AF.Exp, [gt], [lf], scale=-1.0)
            act(lf[:, 0:4], lf[:, 0:4], AF.Ln, [lf, cst], [lf], bias=ONE_AP())
            ts("dve", lf[:, 0:4], lf[:, 0:4], -1.0, ALU.mult, [lf], [lf])
            cp("dve", lfr[:, 0:4, :], lf[:, 0:4].unsqueeze(2).to_broadcast([128, 4, 64]), [lf], [lfr])
            Tm = cmf[:, d, :]
            p = ps[4]
            mm(p[:, 0:4], Tm, lf[:, 0:4], True, True, [cmf, lf], [p])
            cp("dve", bb[:, 0:4], p[:, 0:4], [p], [bb])
            mm(p[:, 8:12], cmf[:, 0, :], lf[:, 0:4], True, False, [cmf, lf], [p])
            mm(p[:, 8:12], cmf[:, 1, :], lf[:, 0:4], False, True, [cmf, lf], [p])
            tt("dve", gg[:, 0:4], p[:, 8:12], lf[:, 0:4], ALU.subtract, [p, lf], [gg])
            tt("dve", bk[:, 0:4], gt[:, fo:fo + 4], bb[:, 0:4], ALU.subtract, [gt, bb], [bk])
            tt("dve", kst[:, 0:4], bk[:, 0:4], gg[:, 0:4], ALU.add, [bk, gg], [kst])
            act(bk[:, 0:4], bk[:, 0:4], AF.Exp, [bk], [bk])
            act(kst[:, 0:4], kst[:, 0:4], AF.Exp, [kst], [kst])
            order = (0, 1) if d == 0 else (1, 0)

            def head_stages(h, si):
                C = Cst[d][h]
                B_ = msets[si]
                abc, eg, qp, ptm, kpp, Cbf, nrep, dm, hdir, hfs, zos, sg, sqm, yms = B_
                pN, pD, ptmp = ((acc0, acc1, ps[0]), (ps[2], ps[3], ps[1]))[si]
                ecol = (63, 127) if d == 0 else (0, 64)
                st = []

                def s1():
                    mm(ptmp[0:64, 0:128], lfr[:, h, :], Tm, True, True, [lfr, cmf], [ptmp])
                st.append(s1)

                def s2():
                    act(abc[:, :], ptmp[0:64, 0:128], AF.Exp, [ptmp], [abc])
                    for c in range(2):
                        cp("pool", eg[:, c:c + 1], abc[:, ecol[c]:ecol[c] + 1], [abc], [eg])
                    stt("dve", qp[:, :], mqs[:, h, :], 0.125, abc[:, :], ALU.mult, ALU.mult, [mqs, abc], [qp])
                st.append(s2)

                def s3():
                    mm(ptmp[:, 0:128], mks[:, h, :], qp[:, :], True, True, [mks, qp], [ptmp])
                st.append(s3)

                def s4():
                    stt("dve", ptm[:, :], ptmp[:, 0:128], bk[:, h:h + 1], cm[:, 4 + d, :], ALU.mult, ALU.mult, [ptmp, bk, cm], [ptm])
                    ts("pool", kpp[:, :], mkts[:, h * 64:(h + 1) * 64], kst[:, h:h + 1], ALU.mult, [mkts, kst], [kpp])
                st.append(s4)

                def s5():
                    mm(pN[:, 0:128], mvts[:, h, 0:128], ptm[:, :], True, False, [mvts, ptm], [pN])
                    mm(pD[:, 0:128], onesb, ptm[:, :], True, False, [cm, ptm], [pD])
                st.append(s5)
                for ci, c in enumerate(order):
                    cs = slice(c * 64, (c + 1) * 64)
                    lastc = ci == 1

                    def s6a(c=c):
                        cp("act", Cbf[:, :], C[:, 0:128], [C], [Cbf])
                        cp("pool", nrep[:, :], C[:, 128:129].to_broadcast([64, 128]), [C], [nrep])
                    st.append(s6a)

                    def s6b(c=c, cs=cs, lastc=lastc):
                        mm(pN[:, cs], Cbf[:, :], qp[:, cs], False, lastc, [Cbf, qp], [pN])
                        mm(pD[:, cs], nrep[:, :], qp[:, cs], False, lastc, [nrep, qp], [pD])
                        mm(ptmp[0:64, 0:129], kpp[cs, :], mvts[cs, h, :], True, True, [kpp, mvts], [ptmp])
                    st.append(s6b)

                    def s6c(c=c):
                        stt("dve", C[:, :], C[:, :], eg[:, c:c + 1], ptmp[0:64, 0:129], ALU.mult, ALU.add, [C, eg, ptmp], [C])
                    st.append(s6c)

                def s7():
                    act(dm[:, :], pD[:, 0:128], AF.Abs, [pD], [dm])
                    ts("dve", dm[:, :], dm[:, :], 1.0, ALU.max, [dm], [dm])
                    recip(dm[:, :], dm[:, :], [dm], [dm])
                    tt("dve", hdir[:, :], pN[:, 0:128], dm[:, :], ALU.mult, [pN, dm], [hdir])
                    if first_dir:
                        store(hdir, hf[h * 128:(h + 1) * 128, tok0:tok0 + 128], hdir[:, :], [d_hf])
                    else:
                        load(hfs, hfs[:, :], hf[h * 128:(h + 1) * 128, tok0:tok0 + 128], [d_hf])
                        load(zos, zos[:, :], zo[h * 128:(h + 1) * 128, tok0:tok0 + 128], [d_m])
                        tt("dve", hfs[:, :], hfs[:, :], hdir[:, :], ALU.add, [hfs, hdir], [hfs])
                        act(sqm[:, :], hfs[:, :], AF.Square, [hfs], [sqm])
                st.append(s7)
                if not first_dir:
                    def s8():
                        mm(ptmp[:, 0:128], onesb, sqm[:, :], True, True, [cm, sqm], [ptmp])
                    st.append(s8)

                    def s9():
                        act(sg[:, :], ptmp[:, 0:128], AF.Ln, [ptmp, cst], [sg], bias=EPS_AP(), scale=1.0 / 128)
                        act(sg[:, :], sg[:, :], AF.Exp, [sg], [sg], scale=-0.5)
                        stt("dve", hfs[:, :], hfs[:, :], gh[:, h:h + 1], sg[:, :], ALU.mult, ALU.mult, [hfs, gh, sg], [hfs])
                        act(sg[:, :], zos[:, :], AF.Exp, [zos], [sg], scale=-1.0)
                        ts("dve", sg[:, :], sg[:, :], 1.0, ALU.add, [sg], [sg])
                        recip(sg[:, :], sg[:, :], [sg], [sg])
                        tt("dve", yms[:, :], hfs[:, :], sg[:, :], ALU.mult, [hfs, sg], [yms])
                        store(yms, ymT[h * 128:(h + 1) * 128, tok0:tok0 + 128], yms[:, :], [d_ym])
                    st.append(s9)
                return st

            for hp in range(2):
                sa = head_stages(2 * hp, 0); sb_ = head_stages(2 * hp + 1, 1)
                for k in range(len(sa)):
                    sa[k](); sb_[k]()

        def zero_states(d):
            for h in range(4):
                P.add("pool", lambda e, t=Cst[d][h]: e.memset(t[:], 0.0), [], [Cst[d][h]])

        zero_states(0); zero_states(1)
        for tl in range(2):
            mlstm_tile(T + tl * 128, 0, True)
        if need_ctx:
            for tl in (1, 0):
                mlstm_tile(T + tl * 128, 1, False)
        for tl in range(32):
            mlstm_tile(tl * 128, 0, True)
        for h in range(4):
            store(Cst[0][h], s_in[h * 64:(h + 1) * 64, :], Cst[0][h][:, :], [d_sin])

        groups = [[0, 1], [2, 3], [4, 5], [6, 7]]
        def allgather(src, dst, dsrc, ddst):
            P.add("pool", lambda e: e.collective_compute("AllGather", ALU.bypass, replica_groups=groups,
                                                         ins=[src.opt()], outs=[dst.opt()]),
                  [dsrc], [ddst], dma=CCSem(None, "cc"))
        allgather(s_in, s_out, d_sin, d_sout)
        allgather(GR, GRo, d_gr, d_gro)
        for i_ in range(2):
            allgather(GK[i_], GKo[i_], d_gk[i_], d_gko[i_])
            allgather(GV[i_], GVo[i_], d_gv[i_], d_gvo[i_])
        for h in range(4):
            C = Cst[1][h]
            load(sst, sst[:], s_out.rearrange("(r h d) f -> h d r f", r=2, h=4)[h], [d_sout])
            ts("dve", C[:, :], sst[:, 0, :], selw[0:64, 0:1], ALU.mult, [sst, selw], [C])
            stt("dve", C[:, :], sst[:, 1, :], selw[0:64, 1:2], C[:, :], ALU.mult, ALU.add, [sst, selw, C], [C])
        for tl in range(31, -1, -1):
            mlstm_tile(tl * 128, 1, False)
        for k_ in range(2):
            load(hk2, hk2[:, k_, :, :], bass_ap_halo_k(GRo, k_), [d_gro])
        load(hv2, hv2[:], bass_ap_halo_v(GRo), [d_gro])
        for g in range(2):
            ts("dve", hks[:, g, :], hk2[:, 0, g, :], selw[0:64, 0:1], ALU.mult, [hk2, selw], [hks])
            stt("dve", hks[:, g, :], hk2[:, 1, g, :], selw[0:64, 1:2], hks[:, g, :], ALU.mult, ALU.add, [hk2, selw, hks], [hks])
            ts("dve", hvs[:, g, 0:64], hv2[:, 0, g * 64:(g + 1) * 64], selw[:, 0:1], ALU.mult, [hv2, selw], [hvs])
            stt("dve", hvs[:, g, 0:64], hv2[:, 1, g * 64:(g + 1) * 64], selw[:, 1:2], hvs[:, g, 0:64], ALU.mult, ALU.add, [hv2, selw, hvs], [hvs])
        load(kcs, kcs[:], swak[:, T:NT].rearrange("(g d) n -> d g n", d=64), [d_sw[8]])
        for c_ in range(2):
            load(vcs, vcs[:, c_, :, 0:64], swav[T + c_ * 128:T + (c_ + 1) * 128, :].rearrange("p (g f) -> p g f", f=64), [d_sw[8]])

        def mla(ti):
            isctx = ti == 8
            c0 = ti * TT; n = CTX if isctx else TT
            load(qn, qn[:, :, 0:n], qmla[0:512, c0:c0 + n].rearrange("(h p) n -> p h n", p=128), [d_q[ti]])
            load(qr, qr[:, :, 0:n], qmla[512:768, c0:c0 + n].rearrange("(h p) n -> p h n", p=64), [d_q[ti]])
            groups = []
            for h in range(4):
                groups.append((h, ("c", 0)))
                if not isctx:
                    groups += [(h, ("g", r_, q_)) for r_ in range(2) for q_ in range(4)]
            items = []
            for gi, (h, gsp) in enumerate(groups):
                for c in range(2 if gsp[0] == "c" else 8):
                    items.append((gi, c, h))
            nper = 2 + (0 if isctx else 64)
            loaded = set()

            def ensure(gi):
                if gi >= len(groups) or gi in loaded:
                    return
                loaded.add(gi)
                h, gsp = groups[gi]; sl = gi % 3
                K, R, V = kk[sl], kr[sl], vv[sl]
                if gsp[0] == "c":
                    load(K, K[:, 0:CTX], kc_n[h * 128:(h + 1) * 128, :], [d_kc])
                    load(R, R[:, 0:CTX], kc_r[:, :], [d_kc])
                    load(V, V[:, 0:2, :], vc_m[:, h * 128:(h + 1) * 128].rearrange("(c p) f -> p c f", p=128), [d_kc])
                else:
                    r_ = gsp[1]; q_ = gsp[2]
                    t0 = q_ * 2
                    load(K, K[:, :].rearrange("p (t n) -> p t n", n=512), GKo[h // 2][r_ * 2048:(r_ + 1) * 2048, :].rearrange("(f t) n -> f t n", t=8)[(h % 2) * 128:(h % 2 + 1) * 128, t0:t0 + 2, :], [d_gko[h // 2]])
                    load(R, R[:, :].rearrange("p (t n) -> p t n", n=512), GRo[r_ * 576:r_ * 576 + 512, :].rearrange("(f t) n -> f t n", t=8)[:, t0:t0 + 2, :], [d_gro])
                    vb = r_ * 2048 + (q_ % 2) * 1024
                    load(V, V[:, :, :], GVo[q_ // 2][vb:vb + 1024, h * 128:(h + 1) * 128].rearrange("(c p) f -> p c f", p=128), [d_gvo[q_ // 2]])

            def emit_S(i):
                gi, c, h = items[i]
                ensure(gi); ensure(gi + 1)
                K, R = kk[gi % 3], kr[gi % 3]
                pS = ps[i % 3]; pt = pT[i % 3]
                mm(pS[:, 0:n], K[:, c * 128:(c + 1) * 128], qn[:, h, 0:n], True, False, [K, qn], [pS])
                mm(pS[:, 0:n], R[:, c * 128:(c + 1) * 128], qr[:, h, 0:n], False, True, [R, qr], [pS])
                act(pt[:, 0:n], pS[:, 0:n], AF.Exp, [pS], [pt], scale=MLA_SCALE)

            def emit_PV(i):
                gi, c, h = items[i]
                V = vv[gi % 3]; pt = pT[i % 3]
                idx = i - h * nper
                pO, pDn = (acc0, acc1) if h % 2 == 0 else (ps[3], ps[4])
                mm(pO[:, 0:n], V[:, c, :], pt[:, 0:n], idx == 0, idx == nper - 1, [V, pt], [pO])
                mm(pDn[:, 0:n], onesb, pt[:, 0:n], idx == 0, idx == nper - 1, [cm, pt], [pDn])
                if idx == nper - 1:
                    recip(rec[:, 0:n], pDn[:, 0:n], [pDn], [rec])
                    tt("dve", yT[:, h, 0:n], pO[:, 0:n], rec[:, 0:n], ALU.mult, [pO, rec], [yT])

            LA = 2
            for i in range(min(LA, len(items))):
                emit_S(i)
            for i in range(len(items)):
                if i + LA < len(items):
                    emit_S(i + LA)
                emit_PV(i)

        def swa(ti):
            isctx = ti == 8
            c0 = ti * TT; n = CTX if isctx else TT
            load(qsw, qsw[0:64, :, 0:n], swaq[:, c0:c0 + n].rearrange("(h d) n -> d h n", d=64), [d_sw[ti]])
            if not isctx:
                lo = max(c0 - 128, 0); hi = min(c0 + TT + 128, T)
                o = lo - (c0 - 128)
                deps = [d_sw[i] for i in range(max(ti - 1, 0), min(ti + 2, 8))]
                load(ksw, ksw[:, :, o:o + hi - lo], swak[:, lo:hi].rearrange("(g d) n -> d g n", d=64), deps)
                for c_ in range((hi - lo) // 128):
                    load(vsw, vsw[:, o // 128 + c_, :, 0:64],
                         swav[lo + c_ * 128:lo + (c_ + 1) * 128, :].rearrange("p (g f) -> p g f", f=64), deps)
            units = [(j, g, hh) for j in range(n // 128) for g in range(2) for hh in range(2)]
            items = []
            for u, (j, g, hh) in enumerate(units):
                jb = ti * 4 + j
                chunks = [("c", 0, None), ("c", 1, None)]
                if not isctx:
                    if jb > 0:
                        chunks.append(("l", j, 1))
                    chunks.append(("l", j + 1, None))
                    if jb < 31:
                        chunks.append(("l", j + 2, 2))
                    else:
                        chunks.append(("h", 0, 3))
                for ci, ch in enumerate(chunks):
                    items.append((u, ci, ch, len(chunks)))
            dsws = (tmp[0], tmp[1]); pBs = (ps[3], ps[4]); accs = (acc0, acc1)

            def operands(u, ch):
                j, g, hh = units[u]
                kind, idx, msk = ch
                if kind == "c":
                    return kcs[:, g, idx * 128:(idx + 1) * 128], vcs[:, idx, g, :], kcs, vcs
                if kind == "l":
                    return ksw[:, g, idx * 128:(idx + 1) * 128], vsw[:, idx, g, :], ksw, vsw
                return hks[:, g, :], hvs[:, g, :], hks, hvs

            def emit_S(i):
                u, ci, ch, nchk = items[i]
                j, g, hh = units[u]; h0 = g * 8 + hh * 4
                kl, vl, kb, vb = operands(u, ch)
                rhs = qsw[0:64, h0:h0 + 4, j * 128:(j + 1) * 128]
                pS = ps[i % 3]; pt = pT[i % 3]
                mm(pS[:, :].rearrange("p (a b) -> p a b", b=128), kl, rhs, True, True, [kb, qsw], [pS])
                act(pt[:, :], pS[:, :], AF.Exp, [pS], [pt], scale=0.125)
                if ch[2] is not None:
                    tt("dve", pt[:, :].rearrange("p (a b) -> p a b", b=128), pt[:, :].rearrange("p (a b) -> p a b", b=128),
                       cm[:, ch[2], :].unsqueeze(1).to_broadcast([128, 4, 128]), ALU.mult, [pt, cm], [pt])

            def tailA(u):
                j, g, hh = units[u]; h0 = g * 8 + hh * 4
                pO = accs[u % 2]; dsw_ = dsws[u % 2]
                for q in range(4):
                    ts("dve", dsw_[64:65, q * 128:(q + 1) * 128], pO[64:65, q * 128:(q + 1) * 128],
                       sk[64:65, h0 + q:h0 + q + 1], ALU.add, [pO, sk], [dsw_])
                recip(dsw_[64:65, :], dsw_[64:65, :], [dsw_], [dsw_])

            def tailB(u):
                j, g, hh = units[u]; h0 = g * 8 + hh * 4
                pO = accs[u % 2]; dsw_ = dsws[u % 2]; pB = pBs[u % 2]
                mm(pB[0:64, :], cmf[64:65, 2, 0:64], dsw_[64:65, :], True, True, [cmf, dsw_], [pB])
                cp("act", rbs[0:64, 0, :], pB[0:64, :], [pB], [rbs])
                tt("dve", ysw[0:64, h0:h0 + 4, j * 128:(j + 1) * 128], pO[0:64, :].rearrange("p (a b) -> p a b", b=128),
                   rbs[0:64, 0, :].rearrange("p (a b) -> p a b", b=128), ALU.mult, [pO, rbs], [ysw])

            def emit_PV(i):
                u, ci, ch, nchk = items[i]
                kl, vl, kb, vb = operands(u, ch)
                pO = accs[u % 2]; pt = pT[i % 3]
                mm(pO[0:65, :], vl, pt[:, :], ci == 0, ci == nchk - 1, [vb, pt], [pO])
                if ci == nchk - 1:
                    tailA(u)
                if ci == 0 and u > 0:
                    tailB(u - 1)

            LA = 2
            for i in range(min(LA, len(items))):
                emit_S(i)
            for i in range(len(items)):
                if i + LA < len(items):
                    emit_S(i + LA)
                emit_PV(i)
            tailB(len(units) - 1)

        def phaseC(ti):
            isctx = ti == 8
            c0 = ti * TT; n = CTX if isctx else TT; r = 1 if isctx else 0
            mla(ti); swa(ti)
            load(yT, yT[:, 4:8, 0:n], ymT[:, c0:c0 + n].rearrange("(c p) n -> p c n", p=128), [d_ym])
            load(xt, xt[:, :, 0:n], xT[:, c0:c0 + n].rearrange("(c p) n -> p c n", p=128), [d_x[ti]])
            for oc in range(16):
                w = nxt("wt", wt)
                load(w, w[:, 0:1024], w_outa[l, oc], [d_w[l]])
                load(w, w[0:64, 1024:1024 + 2048], w_outb[l, oc], [d_w[l]])
                wa = w[:, 0:1024].rearrange("p (a b) -> p a b", b=128)
                wb = w[0:64, 1024:3072].rearrange("p (a b) -> p a b", b=128)
                p = nxt("ps", ps)
                for kc in range(8):
                    mm(p[:, 0:n], wa[:, kc, :], yT[:, kc, 0:n], kc == 0, False, [w, yT], [p])
                for hd in range(16):
                    mm(p[:, 0:n], wb[:, hd, :], ysw[0:64, hd, 0:n], False, hd == 15, [w, ysw], [p])
                stt("dve", xt[:, oc, 0:n], p[:, 0:n], modT[:, GT1 + oc, r:r + 1], xt[:, oc, 0:n], ALU.mult, ALU.add,
                    [p, modT, xt], [xt])
            norm_mod(n, r, a2, SH2)
            for half in range(4):
                for blk in range(8):
                    w = nxt("wt", wt)
                    load(w, w[:], w_ff1[l, half * 8 + blk], [d_w[l]])
                    wv = w[:].rearrange("p (a b) -> p a b", b=256)
                    for o4 in range(2):
                        hc = blk * 2 + o4
                        p = nxt("ps", ps)
                        for kc in range(KC):
                            mm(p[:, 0:n], wv[:, kc, o4 * 128:(o4 + 1) * 128], hT[:, kc, 0:n], kc == 0, kc == KC - 1, [w, hT], [p])
                        t_ = nxt("tmp", tmp)
                        ts("dve", t_[:, 0:n], p[:, 0:n], 0.0, ALU.max, [p], [t_])
                        act(big[:, hc, 0:n], t_[:, 0:n], AF.Square, [t_], [big])
                for oc in range(16):
                    w = nxt("wt", wt)
                    load(w, w[:, 0:2048], w_ff2[l, half * 16 + oc], [d_w[l]])
                    wv = w[:, 0:2048].rearrange("p (a b) -> p a b", b=128)
                    p = nxt("ps", ps)
                    for kc in range(16):
                        mm(p[:, 0:n], wv[:, kc, :], big[:, kc, 0:n], kc == 0, kc == 15, [w, big], [p])
                    stt("dve", xt[:, oc, 0:n], p[:, 0:n], modT[:, GT2 + oc, r:r + 1], xt[:, oc, 0:n], ALU.mult, ALU.add,
                        [p, modT, xt], [xt])
            if last and l == L - 1:
                for c in range(KC):
                    act(big[:, c, 0:n], xt[:, c, 0:n], AF.Square, [xt], [big])
                p = nxt("ps", ps)
                for c in range(KC):
                    mm(p[:, 0:n], onesb, big[:, c, 0:n], c == 0, c == KC - 1, [cm, big], [p])
                rstd_from(p, n, 1.0 / D, rs)
                for c in range(KC):
                    stt("dve", xt[:, c, 0:n], xt[:, c, 0:n], gf[:, c:c + 1], rs[:, 0:n], ALU.mult, ALU.mult, [xt, gf, rs], [xt])
                store(xt, out_ap[:, c0:c0 + n].rearrange("(c p) n -> p c n", p=128), xt[:, :, 0:n], [d_out])
            else:
                store(xt, xT[:, c0:c0 + n].rearrange("(c p) n -> p c n", p=128), xt[:, :, 0:n], [d_x[ti]])
                if l == L - 1:
                    store(xt, out_ap[:, c0:c0 + n].rearrange("(c p) n -> p c n", p=128), xt[:, :, 0:n], [d_out])

        if need_ctx:
            phaseC(8)
        for ti in range(8):
            phaseC(ti)

    P.add("pool", lambda e: e.engine_nop() if hasattr(e, "engine_nop") else e.memset(cst[:, 7:8], 0.0), [d_out], [cst])
    P.emit(stack)
    stack.close()
    return nc


def bass_ap_halo_k(g_out, k_):
    return g_out[k_ * 576 + 512:k_ * 576 + 544, :].rearrange("r (q j) -> (r q) j", q=4).rearrange("(g d) j -> d g j", d=64)


def bass_ap_halo_v(g_out):
    return g_out.rearrange("(k x) n -> k x n", k=2)[:, 544:576, :].rearrange("k r (q j) -> (r q) k j", q=4)


def _const_masks():
    m = np.zeros((128, 8, 128), np.float32)
    i = np.arange(128)[:, None]; t = np.arange(128)[None, :]
    pm = np.arange(128)
    src = np.where((pm % 32) < 16, pm + 16, pm - 16)
    m[src, 0, pm] = 1.0
    m[:, 1, :] = (i >= t); m[:, 2, :] = (i <= t); m[:, 3, :] = (i + t >= 127)
    same = (i // 64) == (t // 64)
    m[:, 4, :] = same & (i <= t); m[:, 5, :] = same & (i >= t)
    m[:, 6, :] = (i == t); m[:, 7, :] = 1.0
    return m


def _rope_tables(pos):
    half = 32
    inv = (np.float32(10000.0) ** (-np.arange(0, half, 2, dtype=np.float32) / np.float32(half))).astype(np.float32)
    row = (pos // 64).astype(np.float32); col = (pos % 64).astype(np.float32)
    ar = row[:, None] * inv; ac = col[:, None] * inv
    ang = np.concatenate([ar, ar, ac, ac], axis=-1)
    cos = np.cos(ang).astype(np.float32).T; sin = np.sin(ang).astype(np.float32).T
    sgn = np.where((np.arange(64) % 32) < 16, -1.0, 1.0).astype(np.float32)[:, None]
    sin = sin * sgn
    return np.ascontiguousarray(np.concatenate([cos, cos], 0)), np.ascontiguousarray(np.concatenate([sin, sin], 0))


def _fm(v, k):
    return np.ascontiguousarray(v.reshape(k, 128).T)


def _prep_gate_parts(inp, odd):
    L = DEPTH
    gate = np.arange(3136, 3152)
    if odd:
        gate = np.concatenate([gate[8:16], gate[0:8]])
    cols = np.concatenate([np.arange(3584 - 16 + 16 + 0, 3584 - 16 + 16 + 0)[:0], np.arange(768, 832), gate])
    wb = inp["w_in"][:, :, cols].reshape(L, KC, 128, 80)
    out = {"win_b": np.ascontiguousarray(wb.transpose(0, 2, 1, 3), dtype=np.float32).reshape(L, 128, KC * 80)}
    gbv = inp["mlstm_gate_bias"].reshape(L, 16)
    if odd:
        gbv = np.concatenate([gbv[:, 8:16], gbv[:, 0:8]], axis=1)
    out["gbias"] = np.ascontiguousarray(gbv, dtype=np.float32)
    return out


def _prep_weights(inp, odd):
    L = DEPTH
    A = lambda a: np.ascontiguousarray(a, dtype=np.float32)
    w = {}
    wm = inp["w_mod"].reshape(L, KC, 128, 96, 128)
    w["wmod"] = A(wm.transpose(0, 3, 2, 1, 4)).reshape(L, 96, 128, KC * 128)
    w["bmodT"] = A(inp["b_mod"].reshape(L, 96, 128).transpose(0, 2, 1))
    w["g1T"] = A(inp["g_norm1"].reshape(L, KC, 128).transpose(0, 2, 1))
    w["g2T"] = A(inp["g_norm2"].reshape(L, KC, 128).transpose(0, 2, 1))
    gate = np.arange(3136, 3152)
    if odd:
        gate = np.concatenate([gate[8:16], gate[0:8]])
    cols = np.concatenate([np.arange(0, 768), np.arange(832, 3136), np.arange(3152, 3664), np.arange(768, 832), gate])
    win = inp["w_in"][:, :, cols]
    wa = win[:, :, :3584].reshape(L, KC, 128, 14, 256)
    w["win_a"] = A(wa.transpose(0, 3, 2, 1, 4)).reshape(L, 14, 128, KC * 256)
    wb = win[:, :, 3584:].reshape(L, KC, 128, 80)
    w["win_b"] = A(wb.transpose(0, 2, 1, 3)).reshape(L, 128, KC * 80)
    w["gqT"] = A(inp["mla_g_q"].reshape(L, 4, 128).transpose(0, 2, 1))
    w["gkvT"] = A(inp["mla_g_kv"].reshape(L, 2, 128).transpose(0, 2, 1))
    qc = np.concatenate([np.concatenate([h * 192 + np.arange(128) for h in range(4)]),
                         np.concatenate([h * 192 + 128 + np.arange(64) for h in range(4)])])
    wuq = inp["mla_w_uq"][:, :, qc].reshape(L, 4, 128, 768)
    w["wuq"] = A(wuq.transpose(0, 2, 1, 3)).reshape(L, 128, 4 * 768)
    kvc = np.concatenate([np.concatenate([h * 256 + np.arange(128) for h in range(4)]),
                          np.concatenate([h * 256 + 128 + np.arange(128) for h in range(4)])])
    wukv = inp["mla_w_ukv"][:, :, kvc].reshape(L, 2, 128, 1024)
    w["wukv"] = A(wukv.transpose(0, 2, 1, 3)).reshape(L, 128, 2 * 1024)
    w["sink"] = A(inp["swa_sink"])
    gbv = inp["mlstm_gate_bias"].reshape(L, 16)
    if odd:
        gbv = np.concatenate([gbv[:, 8:16], gbv[:, 0:8]], axis=1)
    w["gbias"] = A(gbv)
    w["ghT"] = A(inp["mlstm_g_h"].transpose(0, 2, 1))
    wo = inp["w_out"]
    rows_a = np.concatenate([np.arange(0, 512), np.arange(1536, 2048)])
    woa = wo[:, rows_a, :].reshape(L, 8, 128, 16, 128)
    w["wout_a"] = A(woa.transpose(0, 3, 2, 1, 4)).reshape(L, 16, 128, 1024)
    wob = wo[:, 512:1536, :].reshape(L, 16, 64, 16, 128)
    w["wout_b"] = A(wob.transpose(0, 3, 2, 1, 4)).reshape(L, 16, 64, 2048)
    f1 = inp["w_ff1"].reshape(L, KC, 128, 32, 256)
    w["wff1"] = A(f1.transpose(0, 3, 2, 1, 4)).reshape(L, 32, 128, KC * 256)
    f2 = inp["w_ff2"].reshape(L, 4, 16, 128, 16, 128)
    w["wff2"] = A(f2.transpose(0, 1, 4, 3, 2, 5)).reshape(L, 64, 128, 16 * 128)
    w["gfT"] = _fm(np.asarray(inp["g_final"], np.float32), KC)
    w["cmask"] = _const_masks()
    return w


_PER_LAYER = ("wmod", "bmodT", "g1T", "g2T", "win_a", "win_b", "gqT", "gkvT", "wuq", "wukv", "sink", "gbias",
              "ghT", "wout_a", "wout_b", "wff1", "wff2")
_NC_CACHE = {}


def _get_nc(layers, last):
    key = (layers, last)
    if key not in _NC_CACHE:
        _NC_CACHE[key] = build_program(layers, True, last)
    return _NC_CACHE[key]


FUSED = True


def kernel(**inp):
    inp = {k: np.asarray(v) for k, v in inp.items()}
    x = inp["x"]; ctx = inp["ctx"]; c = inp["c"]; c_ctx = inp["c_ctx"]
    wts = [_prep_weights(inp, 0), None]
    wts[1] = dict(wts[0])
    wts[1].update(_prep_gate_parts(inp, 1))
    per_core = []
    for core in range(8):
        b = core // 2; odd = core % 2
        if odd:
            xs = x[b, T:SEQ][::-1]; cs = ctx[b][::-1]; pos = (SEQ - 1 - np.arange(T))
        else:
            xs = x[b, 0:T]; cs = ctx[b]; pos = np.arange(T)
        xT = np.ascontiguousarray(np.concatenate([xs, cs], 0).T.astype(np.float32))
        cT = np.stack([_fm(c[b].astype(np.float32), KC), _fm(c_ctx.astype(np.float32), KC)], axis=-1)
        cos2, sin2 = _rope_tables(pos)
        selw = np.zeros((128, 2), np.float32); selw[:, 1 - odd] = 1.0
        per_core.append({"xT": xT, "cT": np.ascontiguousarray(cT), "cos2": cos2, "sin2": sin2, "selw": selw})

    def maps(lsl, xTs):
        out = []
        for core in range(8):
            w = wts[core % 2]
            m = dict(per_core[core])
            if xTs is not None:
                m["xT"] = xTs[core]
            for k in _PER_LAYER:
                m[k] = w[k][lsl]
            m["gfT"] = w["gfT"]; m["cmask"] = w["cmask"]
            out.append(m)
        return out

    if FUSED:
        nc = _get_nc(DEPTH, True)
        res = run_bass_kernel_spmd(nc, maps(slice(0, DEPTH), None), core_ids=list(range(8)))
        outs = [np.asarray(r["outT"]) for r in res.results]
    else:
        xTs = None
        for l in range(DEPTH):
            lastl = l == DEPTH - 1
            nc = _get_nc(1, lastl)
            res = run_bass_kernel_spmd(nc, maps(slice(l, l + 1), xTs), core_ids=list(range(8)))
            if lastl:
                outs = [np.asarray(r["outT"]) for r in res.results]
            else:
                xTs = [np.asarray(r["xT_out"]) for r in res.results]
    out = np.empty((BATCH, SEQ, D), np.float32)
    for core in range(8):
        b = core // 2
        o = outs[core].T
        if core % 2:
            out[b, T:SEQ] = o[::-1]
        else:
            out[b, 0:T] = o
    return out
```

```python
import contextlib
import numpy as np
import concourse.bass as bass
import concourse.mybir as mybir
from concourse.bass_utils import run_bass_kernel_spmd

F32, BF16 = mybir.dt.float32, mybir.dt.bfloat16
AF = mybir.ActivationFunctionType
ALU = mybir.AluOpType

D = 2048; KC = 16; DEPTH = 4; SEQ = 8192; BATCH = 4; CTX = 256
T = 4096; NT = T + CTX; TT = 512
DFF = 8192
EPS = 1e-6
MLA_SCALE = 192.0 ** -0.5
NROWS_G = 8768


class Buf:
    def __init__(self, t, name):
        self.t = t; self.name = name
        self.lastw = None; self.reads = []
        self.semid = None; self.dcnt = 0
        self.sg = self

    def __getitem__(self, k):
        return self.t[k]


class Op:
    __slots__ = ("eng", "fn", "deps", "signal", "val", "dma", "idx")

    def __init__(self, eng, fn, dma=None):
        self.eng = eng; self.fn = fn; self.deps = []; self.signal = False
        self.val = None; self.dma = dma


class Prog:
    ENGS = ("pe", "act", "dve", "pool", "sp")

    def __init__(self, nc):
        self.nc = nc
        self.ops = {e: [] for e in self.ENGS}
        self.nsem = 0
        self.semholders = []
        self.extra_sems = 0

    def new_sem(self):
        self.nsem += 1
        return self.nsem - 1

    def _tok(self, op):
        if op.dma is not None:
            b = op.dma
            return ("d", b.semid, b.dcnt * 16)
        return ("o", op)

    def add(self, eng, fn, reads=(), writes=(), dma=None, cc=False):
        op = Op(eng, fn, dma)
        deps = []
        for b in reads:
            if b.lastw is not None:
                deps.append(b.lastw)
        for b in writes:
            if b.lastw is not None:
                deps.append(b.lastw)
            deps.extend(b.reads)
        res = []
        for d in deps:
            if d.dma is not None:
                res.append(("d", d.dma.semid, d.dma.dcnt * 16 if not isinstance(d.dma, CCSem) else 1))
            else:
                if d.eng == eng and eng == "pe":
                    continue
                d.signal = True
                res.append(("o", d))
        op.deps = res
        if dma is not None:
            if dma.semid is None:
                dma.semid = self.new_sem()
            dma.dcnt += 1
        for b in reads:
            b.reads.append(op)
        for b in writes:
            b.lastw = op; b.reads = []
        self.ops[eng].append(op)
        return op

    def emit(self, stack):
        nc = self.nc
        engsem = {e: self.new_sem() for e in ("pe", "act", "dve", "pool")}
        sems = [stack.enter_context(nc.semaphore(f"s{i}")) for i in range(self.nsem)]
        for e in ("pe", "act", "dve", "pool"):
            c = 0
            for op in self.ops[e]:
                if op.dma is None and op.signal:
                    c += 1; op.val = c
        block = stack.enter_context(nc.Block())
        handles = {"pe": block.tensor, "act": block.scalar, "dve": block.vector,
                   "pool": block.gpsimd, "sp": block.sync}

        def run(ename):
            def body(eng):
                waited = {}
                for op in self.ops[ename]:
                    for d in op.deps:
                        if d[0] == "d":
                            sid, v = d[1], d[2]
                        else:
                            sid, v = engsem[d[1].eng], d[1].val
                        if waited.get(sid, 0) >= v:
                            continue
                        waited[sid] = v
                        eng.wait_ge(sems[sid], v)
                    ins = op.fn(eng)
                    if op.dma is not None:
                        if isinstance(op.dma, CCSem):
                            ins.then_inc(sems[op.dma.semid])
                        else:
                            ins.then_inc(sems[op.dma.semid], 16)
                    elif op.signal:
                        ins.then_inc(sems[engsem[ename]], 1)
            return body

        for e in self.ENGS:
            if self.ops[e]:
                handles[e](run(e))


class CCSem(Buf):
    pass


def build_program(layers, first, last, debug=()):
    L = layers
    nc = bass.Bass("TRN2", target_bir_lowering=False)
    P = Prog(nc)
    stack = contextlib.ExitStack()

    def din(name, shape, dt=F32):
        return nc.dram_tensor(name, list(shape), dt, kind="ExternalInput").ap()

    def dscr(name, shape, dt):
        kind = "ExternalOutput" if name in debug else "Internal"
        return nc.dram_tensor(name, list(shape), dt, kind=kind).ap()

    xT_in = din("xT", [D, NT])
    cT_in = din("cT", [128, KC, 2])
    cos_in = din("cos2", [128, T]); sin_in = din("sin2", [128, T])
    selw_in = din("selw", [128, 2])
    cmask_in = din("cmask", [128, 8, 128])
    wmod_in = din("wmod", [L, 96, 128, KC * 128])
    bmod_in = din("bmodT", [L, 128, 96])
    g1_in = din("g1T", [L, 128, KC]); g2_in = din("g2T", [L, 128, KC])
    wina_in = din("win_a", [L, 14, 128, KC * 256]); winb_in = din("win_b", [L, 128, KC * 80])
    gq_in = din("gqT", [L, 128, 4]); gkv_in = din("gkvT", [L, 128, 2])
    wuq_in = din("wuq", [L, 128, 4 * 768]); wukv_in = din("wukv", [L, 128, 2 * 1024])
    sink_in = din("sink", [L, 16]); gbias_in = din("gbias", [L, 16])
    gh_in = din("ghT", [L, 128, 4])
    wouta_in = din("wout_a", [L, 16, 128, 8 * 128]); woutb_in = din("wout_b", [L, 16, 64, 16 * 128])
    wff1_in = din("wff1", [L, 32, 128, KC * 256]); wff2_in = din("wff2", [L, 64, 128, 16 * 128])
    gf_in = din("gfT", [128, KC])
    if last:
        out_ap = nc.dram_tensor("outT", [D, T], F32, kind="ExternalOutput").ap()
    else:
        out_ap = nc.dram_tensor("xT_out", [D, NT], F32, kind="ExternalOutput").ap()

    xT = dscr("xTs", [D, NT], F32)
    w_ina = dscr("b_win_a", [L, 14, 128, KC * 256], BF16); w_inb = dscr("b_win_b", [L, 128, KC * 80], BF16)
    w_uq = dscr("b_wuq", [L, 128, 4 * 768], BF16); w_ukv = dscr("b_wukv", [L, 128, 2 * 1024], BF16)
    w_outa = dscr("b_wouta", [L, 16, 128, 8 * 128], BF16); w_outb = dscr("b_woutb", [L, 16, 64, 16 * 128], BF16)
    w_ff1 = dscr("b_wff1", [L, 32, 128, KC * 256], BF16); w_ff2 = dscr("b_wff2", [L, 64, 128, 16 * 128], BF16)
    cmask_b = dscr("b_cmask", [128, 8, 128], BF16)
    w_modb = dscr("b_wmod", [L, 96, 128, KC * 128], BF16)
    qmla = dscr("qmla", [768, NT], BF16)
    GK = [dscr(f"gk{i}", [2048, 512], BF16) for i in range(2)]; GKo = [dscr(f"gko{i}", [4096, 512], BF16) for i in range(2)]
    GV = [dscr(f"gv{i}", [2048, 512], BF16) for i in range(2)]; GVo = [dscr(f"gvo{i}", [4096, 512], BF16) for i in range(2)]
    GR = dscr("gr", [576, 512], BF16); GRo = dscr("gro", [1152, 512], BF16)
    s_in = dscr("s_in", [256, 129], F32); s_out = dscr("s_out", [512, 129], F32)
    kc_n = dscr("kc_n", [512, CTX], BF16); kc_r = dscr("kc_r", [64, CTX], BF16); vc_m = dscr("vc_m", [CTX, 512], BF16)
    swaq = dscr("swaq", [1024, NT], BF16); swak = dscr("swak", [128, NT], BF16); swav = dscr("swav", [NT, 128], BF16)
    mq = dscr("mq", [256, NT], BF16); mk = dscr("mk", [256, NT], BF16)
    mkt = dscr("mkt", [NT, 256], BF16); mvt = dscr("mvt", [NT, 512], BF16)
    gts = dscr("gts", [NT, 16], F32); zo = dscr("zo", [512, NT], BF16)
    hf = dscr("hf", [512, NT], F32); ymT = dscr("ymT", [512, NT], BF16)

    def DB(name):
        return Buf(None, name)
    d_x = [DB(f"x{i}") for i in range(9)]
    d_w = [DB(f"wts{i}") for i in range(L)]; d_wm = [DB(f"wm{i}") for i in range(L)]; d_win = [DB(f"win{i}") for i in range(L)]; d_q = [DB(f"q{i}") for i in range(9)]
    d_gk = [DB("gk0"), DB("gk1")]; d_gv = [DB("gv0"), DB("gv1")]; d_gr = DB("gr")
    d_gko = [DB("gko0"), DB("gko1")]; d_gvo = [DB("gvo0"), DB("gvo1")]; d_gro = DB("gro")
    d_sin = DB("sin"); d_sout = DB("sout")
    d_kc = DB("kc"); d_sw = [DB(f"sw{i}") for i in range(9)]; d_mt = [DB(f"m{i}") for i in range(9)]; d_hf = DB("hf"); d_ymt = [DB(f"ym{i}") for i in range(9)]
    d_out = DB("out")

    def sb(name, shape, dt=F32):
        return Buf(stack.enter_context(nc.sbuf_tensor(name, list(shape), dt)), name)

    def psb(name, shape, dt=F32):
        return Buf(stack.enter_context(nc.psum_tensor(name, list(shape), dt)), name)

    xt = sb("xt", [128, KC, TT])
    hT = sb("hT", [128, KC, TT], BF16)
    big = sb("big", [128, 16, TT], BF16)
    wt = [sb(f"wt{i}", [128, 4096], BF16) for i in range(2)]
    yT = sb("yT", [128, 8, TT], BF16)
    ysw = big; qsw = hT
    wuq_s = sb("wuq_s", [128, 4, 768], BF16); wukv_s = sb("wukv_s", [128, 2, 1024], BF16)
    zf = sb("zf", [128, 4, TT]); cz = sb("cz", [128, 4, TT], BF16)
    rs = sb("rs", [128, TT]); tmp = [sb(f"tmp{i}", [128, TT]) for i in range(2)]
    stg = [sb(f"stg{i}", [128, 4, TT], BF16) for i in range(2)]
    zb = [sb(f"zb{i}", [128, TT], BF16) for i in range(2)]
    gstg = sb("gstg", [128, 4, 16])
    cosS = sb("cosS", [128, TT]); sinS = sb("sinS", [128, TT])
    cm = sb("cm", [128, 8, 128], BF16); cmf = sb("cmf", [128, 3, 128])
    cst = sb("cst", [128, 8])
    selw = sb("selw_s", [128, 2])
    scT = sb("scT", [128, KC, 2]); scb = sb("scb", [128, KC, 2], BF16); wm = [sb(f"wm{i}", [128, KC, 128], BF16) for i in range(2)]
    modT = sb("modT", [128, 96, 2]); bmod = sb("bmod", [128, 96])
    g1 = sb("g1", [128, KC]); g2 = sb("g2", [128, KC]); gf = sb("gf", [128, KC])
    a1 = sb("a1", [128, KC, 2]); a2 = sb("a2", [128, KC, 2])
    gq = sb("gq", [128, 4]); gkv = sb("gkv", [128, 2]); gh = sb("gh", [128, 4])
    sk = sb("sk", [128, 16]); gb = sb("gb", [128, 16])
    qn = sb("qn", [128, 4, TT], BF16); qr = sb("qr", [64, 4, TT], BF16)
    kk = [sb(f"kk{i}", [128, 1024], BF16) for i in range(3)]
    kr = [sb(f"kr{i}", [64, 1024], BF16) for i in range(3)]
    vv = [sb(f"vv{i}", [128, 8, 128], BF16) for i in range(3)]
    pT = [sb(f"pT{i}", [128, TT], BF16) for i in range(3)]
    rec = rs
    ksw = sb("ksw", [64, 2, 768], BF16); vsw = sb("vsw", [128, 6, 2, 65], BF16)
    kcs = sb("kcs", [64, 2, CTX], BF16); vcs = sb("vcs", [128, 2, 2, 65], BF16)
    hk2 = sb("hk2", [64, 2, 2, 128], BF16); hv2 = sb("hv2", [128, 2, 128], BF16)
    hks = sb("hks", [64, 2, 128], BF16); hvs = sb("hvs", [128, 2, 65], BF16)
    rbs = zf
    mqs = sb("mqs", [64, 4, 128], BF16); mks = sb("mks", [64, 4, 128], BF16)
    mkts = sb("mkts", [128, 256], BF16); mvts = sb("mvts", [128, 4, 129], BF16)
    gt = sb("gt", [128, 16]); lf = sb("lf", [128, 8]); lfr = sb("lfr", [128, 4, 64])
    bb = sb("bb", [128, 8]); gg = sb("gg", [128, 8]); bk = sb("bk", [128, 8]); kst = sb("kst", [128, 8])
    Cst = [[sb(f"C{d}{h}", [64, 129]) for h in range(4)] for d in range(2)]
    msets = []
    for si in range(2):
        msets.append((sb(f"abc{si}", [64, 128]), sb(f"eg{si}", [64, 2]), sb(f"qp{si}", [64, 128], BF16),
                      sb(f"ptm{si}", [128, 128], BF16), sb(f"kpp{si}", [128, 64], BF16),
                      sb(f"Cbf{si}", [64, 128], BF16), sb(f"nrep{si}", [64, 128], BF16),
                      sb(f"dm{si}", [128, 128]), sb(f"hdir{si}", [128, 128]), sb(f"hfs{si}", [128, 128]),
                      sb(f"zos{si}", [128, 128], BF16), sb(f"sg{si}", [128, 128]), sb(f"sqm{si}", [128, 128], BF16),
                      sb(f"yms{si}", [128, 128], BF16)))
    sst = sb("sst", [64, 2, 129])
    psall = [psb(f"ps{i}", [128, 512]) for i in range(8)]
    acc0, acc1 = psall[0], psall[1]
    pS3 = psall[2:5]
    ps = [acc0, acc1] + psall[2:5]
    mviews = [(psall[5], psall[6], psall[7])] * 2
    mgate = psall[7]
    cc_sems = []

    ld_q = "sp"; st_q = "pool"

    def dma(q, out, in_, reads, writes, sbuf):
        P.add(q, lambda e, o=out, i=in_: e.dma_start(out=o, in_=i), reads, writes, dma=sbuf)

    def load(sbuf, out, in_, dram=()):
        dma(ld_q, out, in_, list(dram), [sbuf], sbuf.sg)

    def store(sbuf, out, in_, dram=()):
        dma(st_q, out, in_, [sbuf], list(dram), sbuf.sg)

    def share(bufs):
        for b_ in bufs[1:]:
            b_.sg = bufs[0]

    share([bmod, g1, g2, gq, gkv, gh, sk, gb, gf, selw, scT, cm, cmf])
    share([gt, mqs, mks, mkts, mvts])
    share([hk2, hv2, kcs, vcs, sst])
    share([Cst[0][0], Cst[0][1], Cst[0][2], Cst[0][3]])
    share([cosS, sinS])

    def mm(out, lhsT, rhs, start, stop, reads, writes):
        P.add("pe", lambda e: e.matmul(out, lhsT=lhsT, rhs=rhs, start=start, stop=stop), reads, writes)

    def act(out, in_, func, reads, writes, bias=0.0, scale=1.0):
        P.add("act", lambda e: e.activation(out=out, in_=in_, func=func, bias=bias, scale=scale), reads, writes)

    def tt(eng, out, in0, in1, op, reads, writes):
        P.add(eng, lambda e: e.tensor_tensor(out=out, in0=in0, in1=in1, op=op), reads, writes)

    def ts(eng, out, in0, s1, op0, reads, writes, s2=None, op1=None):
        if op1 is None:
            P.add(eng, lambda e: e.tensor_scalar(out=out, in0=in0, scalar1=s1, scalar2=None, op0=op0), reads, writes)
        else:
            P.add(eng, lambda e: e.tensor_scalar(out=out, in0=in0, scalar1=s1, scalar2=s2, op0=op0, op1=op1), reads, writes)

    def stt(eng, out, in0, scalar, in1, op0, op1, reads, writes):
        P.add(eng, lambda e: e.scalar_tensor_tensor(out=out, in0=in0, scalar=scalar, in1=in1, op0=op0, op1=op1), reads, writes)

    def cp(eng, out, in_, reads, writes):
        if eng == "act":
            P.add("act", lambda e: e.copy(out=out, in_=in_), reads, writes)
        else:
            P.add(eng, lambda e: e.tensor_copy(out=out, in_=in_), reads, writes)

    def recip(out, in_, reads, writes):
        P.add("dve", lambda e: e.reciprocal(out=out, in_=in_), reads, writes)

    def transpose(out, in_, reads, writes):
        P.add("pe", lambda e: e.transpose(out, in_, cm[:, 6, :]), list(reads) + [cm], writes)

    rr = {"ps": 0, "wt": 0, "tmp": 0, "stg": 0, "zb": 0, "pT": 0, "wm": 0, "kk": 0}

    def nxt(key, lst):
        rr[key] = (rr[key] + 1) % len(lst)
        return lst[rr[key]]

    EPS_AP = lambda n=128, p0=0: cst[p0:p0 + n, 0:1]
    ONE_AP = lambda n=128, p0=0: cst[p0:p0 + n, 1:2]

    def rstd_from(psum, n, scale, out_buf, np_=128):
        act(out_buf[0:np_, 0:n], psum[0:np_, 0:n], AF.Ln, [psum, cst], [out_buf], bias=EPS_AP(np_), scale=scale)
        act(out_buf[0:np_, 0:n], out_buf[0:np_, 0:n], AF.Exp, [out_buf], [out_buf], scale=-0.5)

    P.add("pool", lambda e: e.memset(cst[:, 0:1], EPS), [], [cst])
    P.add("pool", lambda e: e.memset(cst[:, 1:2], 1.0), [], [cst])
    P.add("pool", lambda e: e.memset(vsw[:], 1.0), [], [vsw])
    P.add("pool", lambda e: e.memset(vcs[:], 1.0), [], [vcs])
    P.add("pool", lambda e: e.memset(mvts[:], 1.0), [], [mvts])
    P.add("pool", lambda e: e.memset(hvs[:], 1.0), [], [hvs])
    d_cm = DB("cmaskb"); castsem = Buf(None, "castsem")
    dma("pool", cmask_b, cmask_in, [], [d_cm], castsem)
    load(cm, cm[:], cmask_b, [d_cm])
    load(cmf, cmf[:, 0:2, :], cmask_in[:, 4:6, :])
    load(cmf, cmf[:, 2:3, :], cmask_in[:, 7:8, :])
    load(selw, selw[:], selw_in)
    load(scT, scT[:], cT_in)
    load(gf, gf[:], gf_in)
    def cast_thunks(l):
        cs_m = Buf(None, f"castm{l}"); cs_i = Buf(None, f"casti{l}"); cs_ = Buf(None, f"cast{l}")
        th = []

        def C(out, in_, dep, sem):
            th.append(lambda: dma("pool", out, in_, [], [dep], sem))
        for i in range(8):
            C(w_modb[l, i * 12:(i + 1) * 12], wmod_in[l, i * 12:(i + 1) * 12], d_wm[l], cs_m)
        for i in range(7):
            C(w_ina[l, 2 * i:2 * i + 2], wina_in[l, 2 * i:2 * i + 2], d_win[l], cs_i)
        C(w_inb[l], winb_in[l], d_win[l], cs_i)
        C(w_uq[l], wuq_in[l], d_win[l], cs_i)
        C(w_ukv[l], wukv_in[l], d_win[l], cs_i)
        for i in range(4):
            C(w_outa[l, i * 4:(i + 1) * 4], wouta_in[l, i * 4:(i + 1) * 4], d_w[l], cs_)
            C(w_outb[l, i * 4:(i + 1) * 4], woutb_in[l, i * 4:(i + 1) * 4], d_w[l], cs_)
        for i in range(8):
            C(w_ff1[l, i * 4:(i + 1) * 4], wff1_in[l, i * 4:(i + 1) * 4], d_w[l], cs_)
        for i in range(8):
            C(w_ff2[l, i * 8:(i + 1) * 8], wff2_in[l, i * 8:(i + 1) * 8], d_w[l], cs_)
        return th

    for t_ in cast_thunks(0):
        t_()
    xcp = Buf(None, "xcp")
    for i in range(9):
        c0 = i * TT; n = TT if i < 8 else CTX
        dma("sp", xT[:, c0:c0 + n], xT_in[:, c0:c0 + n], [], [d_x[i]], xcp)
    act(tmp[0][:, 0:32], scT[:].rearrange("p a b -> p (a b)"), AF.Exp, [scT], [tmp[0]], scale=-1.0)
    ts("dve", tmp[0][:, 0:32], tmp[0][:, 0:32], 1.0, ALU.add, [tmp[0]], [tmp[0]])
    recip(tmp[0][:, 0:32], tmp[0][:, 0:32], [tmp[0]], [tmp[0]])
    tt("dve", scT[:].rearrange("p a b -> p (a b)"), scT[:].rearrange("p a b -> p (a b)"), tmp[0][:, 0:32], ALU.mult, [scT, tmp[0]], [scT])

    cp("dve", scb[:], scT[:], [scT], [scb])
    def run(g):
        for _ in g:
            pass

    def chain(gs):
        for g in gs:
            yield from g

    def interleave(main, side, km, ks):
        done = False
        while not done:
            for _ in range(km):
                try:
                    next(main)
                except StopIteration:
                    done = True
                    break
            if done:
                break
            for _ in range(ks):
                try:
                    next(side)
                except StopIteration:
                    side = iter(())
                    break
        for _ in side:
            pass

    onesb = cm[:, 7, :]

    for l in range(L):
        need_ctx = not (last and l == L - 1)
        load(bmod, bmod[:], bmod_in[l]); load(g1, g1[:], g1_in[l]); load(g2, g2[:], g2_in[l])
        load(gq, gq[:], gq_in[l]); load(gkv, gkv[:], gkv_in[l]); load(gh, gh[:], gh_in[l])
        load(sk, sk[:], sink_in[l].partition_broadcast(128)); load(gb, gb[:], gbias_in[l].partition_broadcast(128))
        load(wuq_s, wuq_s[:].rearrange("p a b -> p (a b)"), w_uq[l], [d_win[l]])
        load(wukv_s, wukv_s[:].rearrange("p a b -> p (a b)"), w_ukv[l], [d_win[l]])
        act(sk[:], sk[:], AF.Exp, [sk], [sk])
        pm = acc0
        for j in range(96):
            w = nxt("wm", wm)
            load(w, w[:].rearrange("p a b -> p (a b)"), w_modb[l, j], [d_wm[l]])
            for kc in range(KC):
                mm(pm[:, 2 * j:2 * j + 2], w[:, kc, :], scb[:, kc, :], kc == 0, kc == KC - 1, [w, scb], [pm])
        tt("dve", modT[:], pm[:, 0:192].rearrange("p (a b) -> p a b", b=2),
           bmod[:].unsqueeze(2).to_broadcast([128, 96, 2]), ALU.add, [pm, bmod], [modT])
        for (a, g, off) in ((a1, g1, 16), (a2, g2, 64)):
            ts("dve", a[:], modT[:, off:off + 16, :], 1.0, ALU.add, [modT], [a])
            tt("dve", a[:], a[:], g[:].unsqueeze(2).to_broadcast([128, 16, 2]), ALU.mult, [a, g], [a])
        SH1, GT1, SH2, GT2 = 0, 32, 48, 80

        def norm_mod(n, r, a, shoff):
            for c in range(KC):
                act(big[:, c, 0:n], xt[:, c, 0:n], AF.Square, [xt], [big])
            p = nxt("ps", ps)
            for c in range(KC):
                mm(p[:, 0:n], onesb, big[:, c, 0:n], c == 0, c == KC - 1, [cm, big], [p])
            rstd_from(p, n, 1.0 / D, rs)
            for c in range(KC):
                t_ = nxt("tmp", tmp)
                tt("dve", t_[:, 0:n], xt[:, c, 0:n], rs[:, 0:n], ALU.mult, [xt, rs], [t_])
                act(hT[:, c, 0:n], t_[:, 0:n], AF.Identity, [t_, a, modT], [hT],
                    bias=modT[:, shoff + c, r:r + 1], scale=a[:, c, r:r + 1])

        def phaseA(ti):
            isctx = ti == 8
            c0 = ti * TT; n = CTX if isctx else TT; r = 1 if isctx else 0
            nsub = n // 128
            load(xt, xt[:, :, 0:n], xT[:, c0:c0 + n].rearrange("(c p) n -> p c n", p=128), [d_x[ti]])
            if not isctx:
                load(cosS, cosS[:], cos_in[:, c0:c0 + n]); load(sinS, sinS[:], sin_in[:, c0:c0 + n])
            norm_mod(n, r, a1, SH1)

            def rope(src_ps, np_, dst):
                z = nxt("zb", zb)
                cp("act", z[0:np_, 0:n], src_ps[0:np_, 0:n], [src_ps], [z])
                if isctx:
                    cp("dve", dst, z[0:np_, 0:n], [z], [dstbuf[0]])
                    return
                pr = nxt("ps", ps)
                mm(pr[0:np_, 0:n], cm[0:np_, 0, 0:np_], z[0:np_, 0:n], True, True, [cm, z], [pr])
                t1 = nxt("tmp", tmp)
                tt("dve", t1[0:np_, 0:n], z[0:np_, 0:n], cosS[0:np_, 0:n], ALU.mult, [z, cosS], [t1])
                t2 = nxt("tmp", tmp)
                tt("dve", t2[0:np_, 0:n], pr[0:np_, 0:n], sinS[0:np_, 0:n], ALU.mult, [pr, sinS], [t2])
                tt("dve", dst, t1[0:np_, 0:n], t2[0:np_, 0:n], ALU.add, [t1, t2], [dstbuf[0]])

            dstbuf = [None]

            def latent_norm(nch, gvec, src_chunks_done):
                p = nxt("ps", ps)
                for c in range(nch):
                    act(big[:, c, 0:n], zf[:, c, 0:n], AF.Square, [zf], [big])
                for c in range(nch):
                    mm(p[:, 0:n], onesb, big[:, c, 0:n], c == 0, c == nch - 1, [cm, big], [p])
                rstd_from(p, n, 1.0 / (nch * 128), rs)
                for c in range(nch):
                    stt("dve", cz[:, c, 0:n], zf[:, c, 0:n], gvec[:, c:c + 1], rs[:, 0:n], ALU.mult, ALU.mult,
                        [zf, gvec, rs], [cz])

            for blk in range(15):
                w = nxt("wt", wt)
                if blk < 14:
                    load(w, w[:], w_ina[l, blk], [d_win[l]]); wv = w[:].rearrange("p (a b) -> p a b", b=256)
                else:
                    load(w, w[:, 0:KC * 80], w_inb[l], [d_win[l]]); wv = w[:, 0:KC * 80].rearrange("p (a b) -> p a b", b=80)
                if blk in (10, 11):
                    s = nxt("stg", stg)
                    for j in range(nsub):
                        p = nxt("ps", ps)
                        for kc in range(KC):
                            mm(p[:, 0:256], hT[:, kc, j * 128:(j + 1) * 128], wv[:, kc, :], kc == 0, kc == KC - 1, [hT, w], [p])
                        cp("act" if j % 2 else "dve", s[:, j, 0:256], p[:, 0:256], [p], [s])
                    store(s, mvt[c0:c0 + n, (blk - 10) * 256:(blk - 9) * 256].rearrange("(j p) f -> p j f", p=128), s[:, 0:nsub, 0:256], [d_mt[ti]])
                    yield
                    continue
                if blk == 14:
                    p = nxt("ps", ps)
                    for kc in range(KC):
                        mm(p[0:64, 0:n], wv[:, kc, 0:64], hT[:, kc, 0:n], kc == 0, kc == KC - 1, [w, hT], [p])
                    s = nxt("stg", stg); dstbuf[0] = s
                    rope(p, 64, s[0:64, 0, 0:n])
                    if isctx:
                        store(s, kc_r[:, :], s[0:64, 0, 0:n], [d_kc])
                    else:
                        store(s, GR[0:512, :].rearrange("(f t) n -> f t n", t=8)[:, ti, :], s[0:64, 0, 0:n], [d_gr])
                    for j in range(nsub):
                        p = nxt("ps", ps)
                        for kc in range(KC):
                            mm(p[:, 0:16], hT[:, kc, j * 128:(j + 1) * 128], wv[:, kc, 64:80], kc == 0, kc == KC - 1, [hT, w], [p])
                        tt("dve", gstg[:, j, :], p[:, 0:16], gb[:], ALU.add, [p, gb], [gstg])
                    store(gstg, gts[c0:c0 + n, :].rearrange("(j p) f -> p j f", p=128), gstg[:, 0:nsub, :], [d_mt[ti]])
                    yield
                    continue
                if blk in (7, 9):
                    s = nxt("stg", stg)
                    c_lo, c_n = (128, 128) if blk == 7 else (0, 256)
                    for j in range(nsub):
                        p = nxt("ps", ps)
                        for kc in range(KC):
                            mm(p[:, 0:c_n], hT[:, kc, j * 128:(j + 1) * 128], wv[:, kc, c_lo:c_lo + c_n], kc == 0, kc == KC - 1, [hT, w], [p])
                        cp("act" if j % 2 else "dve", s[:, j, 0:c_n], p[:, 0:c_n], [p], [s])
                    if blk == 7:
                        store(s, swav[c0:c0 + n, :].rearrange("(j p) f -> p j f", p=128), s[:, 0:nsub, 0:128], [d_sw[ti]])
                        if ti == 7:
                            store(s, GR[544:576, :].rearrange("r (q j) -> (r q) j", q=4), s[:, 3, 0:128], [d_gr])
                    else:
                        store(s, mkt[c0:c0 + n, :].rearrange("(j p) f -> p j f", p=128), s[:, 0:nsub, 0:256], [d_mt[ti]])
                    yield
                for oc2 in range(2):
                    oc = blk * 2 + oc2
                    if oc == 15:
                        continue
                    p = nxt("ps", ps)
                    for kc in range(KC):
                        mm(p[:, 0:n], wv[:, kc, oc2 * 128:(oc2 + 1) * 128], hT[:, kc, 0:n], kc == 0, kc == KC - 1, [w, hT], [p])
                    if oc < 4:
                        cp("act", zf[:, oc, 0:n], p[:, 0:n], [p], [zf])
                        if oc == 3:
                            latent_norm(4, gq, None)
                            s = nxt("stg", stg); s2 = nxt("stg", stg)
                            for qc in range(6):
                                p2 = nxt("ps", ps)
                                for kc in range(4):
                                    mm(p2[:, 0:n], wuq_s[:, kc, qc * 128:(qc + 1) * 128], cz[:, kc, 0:n], kc == 0, kc == 3, [wuq_s, cz], [p2])
                                if qc < 4:
                                    cp("act", s[:, qc, 0:n], p2[:, 0:n], [p2], [s])
                                else:
                                    dstbuf[0] = s2
                                    rope(p2, 128, s2[:, qc - 4, 0:n])
                            store(s, qmla[0:512, c0:c0 + n].rearrange("(c p) n -> p c n", p=128), s[:, 0:4, 0:n], [d_q[ti]])
                            store(s2, qmla[512:768, c0:c0 + n].rearrange("(c p) n -> p c n", p=128), s2[:, 0:2, 0:n], [d_q[ti]])
                    elif oc < 6:
                        cp("act", zf[:, oc - 4, 0:n], p[:, 0:n], [p], [zf])
                        if oc == 5:
                            latent_norm(2, gkv, None)
                            s = nxt("stg", stg)
                            for h in range(4):
                                p2 = nxt("ps", ps)
                                for kc in range(2):
                                    mm(p2[:, 0:n], wukv_s[:, kc, h * 128:(h + 1) * 128], cz[:, kc, 0:n], kc == 0, kc == 1, [wukv_s, cz], [p2])
                                cp("act", s[:, h, 0:n], p2[:, 0:n], [p2], [s])
                            if isctx:
                                store(s, kc_n[:, :].rearrange("(c p) n -> p c n", p=128), s[:, 0:4, 0:n], [d_kc])
                            else:
                                for i_ in range(2):
                                    store(s, GK[i_][:, :].rearrange("(c p t) n -> p c t n", p=128, t=8)[:, :, ti, :], s[:, 2 * i_:2 * i_ + 2, 0:n], [d_gk[i_]])
                            s = nxt("stg", stg)
                            for j in range(nsub):
                                p2 = nxt("ps", ps)
                                for kc in range(2):
                                    mm(p2[:, :], cz[:, kc, j * 128:(j + 1) * 128], wukv_s[:, kc, 512:1024], kc == 0, kc == 1, [cz, wukv_s], [p2])
                                cp("dve", s[:, j, :], p2[:, :], [p2], [s])
                            if isctx:
                                store(s, vc_m[:, :].rearrange("(j p) f -> p j f", p=128), s[:, 0:nsub, :], [d_kc])
                            else:
                                store(s, GV[c0 // 2048][c0 % 2048:c0 % 2048 + n, :].rearrange("(j p) f -> p j f", p=128), s[:, 0:nsub, :], [d_gv[c0 // 2048]])
                    elif oc < 15:
                        s = nxt("stg", stg); dstbuf[0] = s
                        rope(p, 128, s[:, 0, 0:n])
                        if oc < 14:
                            store(s, swaq[(oc - 6) * 128:(oc - 5) * 128, c0:c0 + n], s[:, 0, 0:n], [d_sw[ti]])
                        else:
                            store(s, swak[:, c0:c0 + n], s[:, 0, 0:n], [d_sw[ti]])
                            if ti == 7:
                                store(s, GR[512:544, :].rearrange("r (q j) -> (r q) j", q=4), s[:, 0, 384:512], [d_gr])
                    elif 18 <= oc < 20:
                        z = nxt("zb", zb)
                        cp("act", z[:, 0:n], p[:, 0:n], [p], [z])
                        store(z, mk[(oc - 18) * 128:(oc - 17) * 128, c0:c0 + n], z[:, 0:n], [d_mt[ti]])
                    elif oc == 15:
                        pass
                    elif oc < 18:
                        z = nxt("zb", zb)
                        cp("act", z[:, 0:n], p[:, 0:n], [p], [z])
                        store(z, mq[(oc - 16) * 128:(oc - 15) * 128, c0:c0 + n], z[:, 0:n], [d_mt[ti]])
                    else:
                        z = nxt("zb", zb)
                        cp("dve", z[:, 0:n], p[:, 0:n], [p], [z])
                        store(z, zo[(oc - 24) * 128:(oc - 23) * 128, c0:c0 + n], z[:, 0:n], [d_mt[ti]])
                    yield

        def mlstm_tile(tok0, direction, first_dir, init_from=None):
            d = direction
            ti = tok0 // TT
            load(gt, gt[:], gts[tok0:tok0 + 128, :], [d_mt[ti]])
            load(mqs, mqs[:], mq[:, tok0:tok0 + 128].rearrange("(h d) n -> d h n", d=64), [d_mt[ti]])
            load(mks, mks[:], mk[:, tok0:tok0 + 128].rearrange("(h d) n -> d h n", d=64), [d_mt[ti]])
            load(mkts, mkts[:], mkt[tok0:tok0 + 128, :], [d_mt[ti]])
            load(mvts, mvts[:, :, 0:128], mvt[tok0:tok0 + 128, :].rearrange("p (h f) -> p h f", f=128), [d_mt[ti]])
            fo = d * 8
            act(lf[:, 0:4], gt[:, fo + 4:fo + 8], AF.Exp, [gt], [lf], scale=-1.0)
            act(lf[:, 0:4], lf[:, 0:4], AF.Ln, [lf, cst], [lf], bias=ONE_AP())
            ts("dve", lf[:, 0:4], lf[:, 0:4], -1.0, ALU.mult, [lf], [lf])
            cp("dve", lfr[:, 0:4, :], lf[:, 0:4].unsqueeze(2).to_broadcast([128, 4, 64]), [lf], [lfr])
            Tm = cmf[:, d, :]
            p = mgate
            mm(p[:, 0:4], Tm, lf[:, 0:4], True, True, [cmf, lf], [p])
            cp("dve", bb[:, 0:4], p[:, 0:4], [p], [bb])
            mm(p[:, 8:12], cmf[:, 0, :], lf[:, 0:4], True, False, [cmf, lf], [p])
            mm(p[:, 8:12], cmf[:, 1, :], lf[:, 0:4], False, True, [cmf, lf], [p])
            tt("dve", gg[:, 0:4], p[:, 8:12], lf[:, 0:4], ALU.subtract, [p, lf], [gg])
            tt("dve", bk[:, 0:4], gt[:, fo:fo + 4], bb[:, 0:4], ALU.subtract, [gt, bb], [bk])
            tt("dve", kst[:, 0:4], bk[:, 0:4], gg[:, 0:4], ALU.add, [bk, gg], [kst])
            act(bk[:, 0:4], bk[:, 0:4], AF.Exp, [bk], [bk])
            act(kst[:, 0:4], kst[:, 0:4], AF.Exp, [kst], [kst])
            order = (0, 1) if d == 0 else (1, 0)

            def head_stages(h, si):
                C = Cst[d][h]
                B_ = msets[si]
                abc, eg, qp, ptm, kpp, Cbf, nrep, dm, hdir, hfs, zos, sg, sqm, yms = B_
                pN, pD, ptmp = mviews[si]
                ecol = (63, 127) if d == 0 else (0, 64)
                st = []

                def s1():
                    mm(ptmp[0:64, 0:128], lfr[:, h, :], Tm, True, True, [lfr, cmf], [ptmp])
                st.append(s1)

                def s2():
                    act(abc[:, :], ptmp[0:64, 0:128], AF.Exp, [ptmp], [abc])
                    for c in range(2):
                        cp("pool", eg[:, c:c + 1], abc[:, ecol[c]:ecol[c] + 1], [abc], [eg])
                    stt("dve", qp[:, :], mqs[:, h, :], 0.125, abc[:, :], ALU.mult, ALU.mult, [mqs, abc], [qp])
                st.append(s2)

                def s3():
                    mm(ptmp[:, 0:128], mks[:, h, :], qp[:, :], True, True, [mks, qp], [ptmp])
                st.append(s3)

                def s4():
                    stt("dve", ptm[:, :], ptmp[:, 0:128], bk[:, h:h + 1], cm[:, 4 + d, :], ALU.mult, ALU.mult, [ptmp, bk, cm], [ptm])
                    ts("pool", kpp[:, :], mkts[:, h * 64:(h + 1) * 64], kst[:, h:h + 1], ALU.mult, [mkts, kst], [kpp])
                st.append(s4)

                def s5():
                    mm(pN[:, 0:128], mvts[:, h, 0:128], ptm[:, :], True, False, [mvts, ptm], [pN])
                    mm(pD[:, 0:128], onesb, ptm[:, :], True, False, [cm, ptm], [pD])
                st.append(s5)
                for ci, c in enumerate(order):
                    cs = slice(c * 64, (c + 1) * 64)
                    lastc = ci == 1

                    def s6a(c=c):
                        cp("act", Cbf[:, :], C[:, 0:128], [C], [Cbf])
                        cp("pool", nrep[:, :], C[:, 128:129].to_broadcast([64, 128]), [C], [nrep])
                    st.append(s6a)

                    def s6b(c=c, cs=cs, lastc=lastc):
                        mm(pN[:, cs], Cbf[:, :], qp[:, cs], False, lastc, [Cbf, qp], [pN])
                        mm(pD[:, cs], nrep[:, :], qp[:, cs], False, lastc, [nrep, qp], [pD])
                        mm(ptmp[0:64, 0:129], kpp[cs, :], mvts[cs, h, :], True, True, [kpp, mvts], [ptmp])
                    st.append(s6b)

                    def s6c(c=c):
                        stt("dve", C[:, :], C[:, :], eg[:, c:c + 1], ptmp[0:64, 0:129], ALU.mult, ALU.add, [C, eg, ptmp], [C])
                    st.append(s6c)

                def s7():
                    act(dm[:, :], pD[:, 0:128], AF.Abs, [pD], [dm])
                    ts("dve", dm[:, :], dm[:, :], 1.0, ALU.max, [dm], [dm])
                    recip(dm[:, :], dm[:, :], [dm], [dm])
                    tt("dve", hdir[:, :], pN[:, 0:128], dm[:, :], ALU.mult, [pN, dm], [hdir])
                    if first_dir:
                        store(hdir, hf[h * 128:(h + 1) * 128, tok0:tok0 + 128], hdir[:, :], [d_hf])
                    else:
                        load(hfs, hfs[:, :], hf[h * 128:(h + 1) * 128, tok0:tok0 + 128], [d_hf])
                        load(zos, zos[:, :], zo[h * 128:(h + 1) * 128, tok0:tok0 + 128], [d_mt[ti]])
                        tt("dve", hfs[:, :], hfs[:, :], hdir[:, :], ALU.add, [hfs, hdir], [hfs])
                        act(sqm[:, :], hfs[:, :], AF.Square, [hfs], [sqm])
                st.append(s7)
                if not first_dir:
                    def s8():
                        mm(ptmp[:, 0:128], onesb, sqm[:, :], True, True, [cm, sqm], [ptmp])
                    st.append(s8)

                    def s9():
                        act(sg[:, :], ptmp[:, 0:128], AF.Ln, [ptmp, cst], [sg], bias=EPS_AP(), scale=1.0 / 128)
                        act(sg[:, :], sg[:, :], AF.Exp, [sg], [sg], scale=-0.5)
                        stt("dve", hfs[:, :], hfs[:, :], gh[:, h:h + 1], sg[:, :], ALU.mult, ALU.mult, [hfs, gh, sg], [hfs])
                        act(sg[:, :], zos[:, :], AF.Exp, [zos], [sg], scale=-1.0)
                        ts("dve", sg[:, :], sg[:, :], 1.0, ALU.add, [sg], [sg])
                        recip(sg[:, :], sg[:, :], [sg], [sg])
                        tt("dve", yms[:, :], hfs[:, :], sg[:, :], ALU.mult, [hfs, sg], [yms])
                        store(yms, ymT[h * 128:(h + 1) * 128, tok0:tok0 + 128], yms[:, :], [d_ymt[ti]])
                    st.append(s9)
                return st

            yield
            for h_ in range(4):
                for st_ in head_stages(h_, h_ % 2):
                    st_()
                    yield

        def zero_states(d):
            for h in range(4):
                P.add("pool", lambda e, t=Cst[d][h]: e.memset(t[:], 0.0), [], [Cst[d][h]])

        zero_states(0); zero_states(1)
        run(phaseA(8))
        ctx_side = [mlstm_tile(T + tl * 128, 0, True) for tl in range(2)]
        if need_ctx:
            ctx_side += [mlstm_tile(T + tl * 128, 1, False) for tl in (1, 0)]
        side = chain(ctx_side)
        for ti in range(8):
            interleave(phaseA(ti), side, 1, 4)
            side = chain([mlstm_tile((ti * 4 + q_) * 128, 0, True) for q_ in range(4)])
        run(side)
        for h in range(4):
            store(Cst[0][h], s_in[h * 64:(h + 1) * 64, :], Cst[0][h][:, :], [d_sin])

        groups = [[0, 1], [2, 3], [4, 5], [6, 7]]
        def allgather(src, dst, dsrc, ddst):
            P.add("pool", lambda e: e.collective_compute("AllGather", ALU.bypass, replica_groups=groups,
                                                         ins=[src.opt()], outs=[dst.opt()]),
                  [dsrc], [ddst], dma=CCSem(None, "cc"))
        allgather(s_in, s_out, d_sin, d_sout)
        allgather(GR, GRo, d_gr, d_gro)
        for i_ in range(2):
            allgather(GK[i_], GKo[i_], d_gk[i_], d_gko[i_])
            allgather(GV[i_], GVo[i_], d_gv[i_], d_gvo[i_])
        for h in range(4):
            C = Cst[1][h]
            load(sst, sst[:], s_out.rearrange("(r h d) f -> h d r f", r=2, h=4)[h], [d_sout])
            ts("dve", C[:, :], sst[:, 0, :], selw[0:64, 0:1], ALU.mult, [sst, selw], [C])
            stt("dve", C[:, :], sst[:, 1, :], selw[0:64, 1:2], C[:, :], ALU.mult, ALU.add, [sst, selw, C], [C])
        for k_ in range(2):
            load(hk2, hk2[:, k_, :, :], bass_ap_halo_k(GRo, k_), [d_gro])
        load(hv2, hv2[:], bass_ap_halo_v(GRo), [d_gro])
        for g in range(2):
            ts("dve", hks[:, g, :], hk2[:, 0, g, :], selw[0:64, 0:1], ALU.mult, [hk2, selw], [hks])
            stt("dve", hks[:, g, :], hk2[:, 1, g, :], selw[0:64, 1:2], hks[:, g, :], ALU.mult, ALU.add, [hk2, selw, hks], [hks])
            ts("dve", hvs[:, g, 0:64], hv2[:, 0, g * 64:(g + 1) * 64], selw[:, 0:1], ALU.mult, [hv2, selw], [hvs])
            stt("dve", hvs[:, g, 0:64], hv2[:, 1, g * 64:(g + 1) * 64], selw[:, 1:2], hvs[:, g, 0:64], ALU.mult, ALU.add, [hv2, selw, hvs], [hvs])
        load(kcs, kcs[:], swak[:, T:NT].rearrange("(g d) n -> d g n", d=64), [d_sw[8]])
        for c_ in range(2):
            load(vcs, vcs[:, c_, :, 0:64], swav[T + c_ * 128:T + (c_ + 1) * 128, :].rearrange("p (g f) -> p g f", f=64), [d_sw[8]])

        def mla(ti):
            isctx = ti == 8
            c0 = ti * TT; n = CTX if isctx else TT
            load(qn, qn[:, :, 0:n], qmla[0:512, c0:c0 + n].rearrange("(h p) n -> p h n", p=128), [d_q[ti]])
            load(qr, qr[:, :, 0:n], qmla[512:768, c0:c0 + n].rearrange("(h p) n -> p h n", p=64), [d_q[ti]])
            groups = []
            for h in range(4):
                groups.append((h, ("c", 0)))
                if not isctx:
                    groups += [(h, ("g", r_, q_)) for r_ in range(2) for q_ in range(4)]
            items = []
            for gi, (h, gsp) in enumerate(groups):
                for c in range(2 if gsp[0] == "c" else 8):
                    items.append((gi, c, h))
            nper = 2 + (0 if isctx else 64)
            loaded = set()

            def ensure(gi):
                if gi >= len(groups) or gi in loaded:
                    return
                loaded.add(gi)
                h, gsp = groups[gi]; sl = gi % 3
                K, R, V = kk[sl], kr[sl], vv[sl]
                if gsp[0] == "c":
                    load(K, K[:, 0:CTX], kc_n[h * 128:(h + 1) * 128, :], [d_kc])
                    load(R, R[:, 0:CTX], kc_r[:, :], [d_kc])
                    load(V, V[:, 0:2, :], vc_m[:, h * 128:(h + 1) * 128].rearrange("(c p) f -> p c f", p=128), [d_kc])
                else:
                    r_ = gsp[1]; q_ = gsp[2]
                    t0 = q_ * 2
                    load(K, K[:, :].rearrange("p (t n) -> p t n", n=512), GKo[h // 2][r_ * 2048:(r_ + 1) * 2048, :].rearrange("(f t) n -> f t n", t=8)[(h % 2) * 128:(h % 2 + 1) * 128, t0:t0 + 2, :], [d_gko[h // 2]])
                    load(R, R[:, :].rearrange("p (t n) -> p t n", n=512), GRo[r_ * 576:r_ * 576 + 512, :].rearrange("(f t) n -> f t n", t=8)[:, t0:t0 + 2, :], [d_gro])
                    vb = r_ * 2048 + (q_ % 2) * 1024
                    load(V, V[:, :, :], GVo[q_ // 2][vb:vb + 1024, h * 128:(h + 1) * 128].rearrange("(c p) f -> p c f", p=128), [d_gvo[q_ // 2]])

            def emit_S(i):
                gi, c, h = items[i]
                ensure(gi); ensure(gi + 1)
                K, R = kk[gi % 3], kr[gi % 3]
                pS = pS3[i % 3]; pt = pT[i % 3]
                mm(pS[:, 0:n], K[:, c * 128:(c + 1) * 128], qn[:, h, 0:n], True, False, [K, qn], [pS])
                mm(pS[:, 0:n], R[:, c * 128:(c + 1) * 128], qr[:, h, 0:n], False, True, [R, qr], [pS])
                act(pt[:, 0:n], pS[:, 0:n], AF.Exp, [pS], [pt], scale=MLA_SCALE)

            def emit_PV(i):
                gi, c, h = items[i]
                V = vv[gi % 3]; pt = pT[i % 3]
                idx = i - h * nper
                pO, pDn = acc0, acc1
                mm(pO[:, 0:n], V[:, c, :], pt[:, 0:n], idx == 0, idx == nper - 1, [V, pt], [pO])
                mm(pDn[:, 0:n], onesb, pt[:, 0:n], idx == 0, idx == nper - 1, [cm, pt], [pDn])
                if idx == nper - 1:
                    recip(rec[:, 0:n], pDn[:, 0:n], [pDn], [rec])
                    tt("dve", yT[:, h, 0:n], pO[:, 0:n], rec[:, 0:n], ALU.mult, [pO, rec], [yT])

            LA = 2
            for i in range(min(LA, len(items))):
                emit_S(i)
            for i in range(len(items)):
                if i + LA < len(items):
                    emit_S(i + LA)
                emit_PV(i)
                yield

        def swa(ti):
            isctx = ti == 8
            c0 = ti * TT; n = CTX if isctx else TT
            load(qsw, qsw[0:64, :, 0:n], swaq[:, c0:c0 + n].rearrange("(h d) n -> d h n", d=64), [d_sw[ti]])
            if not isctx:
                lo = max(c0 - 128, 0); hi = min(c0 + TT + 128, T)
                o = lo - (c0 - 128)
                deps = [d_sw[i] for i in range(max(ti - 1, 0), min(ti + 2, 8))]
                load(ksw, ksw[:, :, o:o + hi - lo], swak[:, lo:hi].rearrange("(g d) n -> d g n", d=64), deps)
                for c_ in range((hi - lo) // 128):
                    load(vsw, vsw[:, o // 128 + c_, :, 0:64],
                         swav[lo + c_ * 128:lo + (c_ + 1) * 128, :].rearrange("p (g f) -> p g f", f=64), deps)
            units = [(j, g, hh) for j in range(n // 128) for g in range(2) for hh in range(2)]
            items = []
            for u, (j, g, hh) in enumerate(units):
                jb = ti * 4 + j
                chunks = [("c", 0, None), ("c", 1, None)]
                if not isctx:
                    if jb > 0:
                        chunks.append(("l", j, 1))
                    chunks.append(("l", j + 1, None))
                    if jb < 31:
                        chunks.append(("l", j + 2, 2))
                    else:
                        chunks.append(("h", 0, 3))
                for ci, ch in enumerate(chunks):
                    items.append((u, ci, ch, len(chunks)))
            dsws = (tmp[0], tmp[1]); accs = (acc0, acc1); sctr = [0]

            def operands(u, ch):
                j, g, hh = units[u]
                kind, idx, msk = ch
                if kind == "c":
                    return kcs[:, g, idx * 128:(idx + 1) * 128], vcs[:, idx, g, :], kcs, vcs
                if kind == "l":
                    return ksw[:, g, idx * 128:(idx + 1) * 128], vsw[:, idx, g, :], ksw, vsw
                return hks[:, g, :], hvs[:, g, :], hks, hvs

            def emit_S(i):
                u, ci, ch, nchk = items[i]
                j, g, hh = units[u]; h0 = g * 8 + hh * 4
                kl, vl, kb, vb = operands(u, ch)
                rhs = qsw[0:64, h0:h0 + 4, j * 128:(j + 1) * 128]
                pS = pS3[sctr[0] % 3]; sctr[0] += 1; pt = pT[i % 3]
                mm(pS[:, :].rearrange("p (a b) -> p a b", b=128), kl, rhs, True, True, [kb, qsw], [pS])
                act(pt[:, :], pS[:, :], AF.Exp, [pS], [pt], scale=0.125)
                if ch[2] is not None:
                    tt("dve", pt[:, :].rearrange("p (a b) -> p a b", b=128), pt[:, :].rearrange("p (a b) -> p a b", b=128),
                       cm[:, ch[2], :].unsqueeze(1).to_broadcast([128, 4, 128]), ALU.mult, [pt, cm], [pt])

            def tailA(u):
                j, g, hh = units[u]; h0 = g * 8 + hh * 4
                pO = accs[u % 2]; dsw_ = dsws[u % 2]
                for q in range(4):
                    ts("dve", dsw_[64:65, q * 128:(q + 1) * 128], pO[64:65, q * 128:(q + 1) * 128],
                       sk[64:65, h0 + q:h0 + q + 1], ALU.add, [pO, sk], [dsw_])
                recip(dsw_[64:65, :], dsw_[64:65, :], [dsw_], [dsw_])

            def tailB(u):
                j, g, hh = units[u]; h0 = g * 8 + hh * 4
                pO = accs[u % 2]; dsw_ = dsws[u % 2]; pB = pS3[sctr[0] % 3]; sctr[0] += 1
                mm(pB[0:64, :], cmf[64:65, 2, 0:64], dsw_[64:65, :], True, True, [cmf, dsw_], [pB])
                cp("act", rbs[0:64, 0, :], pB[0:64, :], [pB], [rbs])
                tt("dve", ysw[0:64, h0:h0 + 4, j * 128:(j + 1) * 128], pO[0:64, :].rearrange("p (a b) -> p a b", b=128),
                   rbs[0:64, 0, :].rearrange("p (a b) -> p a b", b=128), ALU.mult, [pO, rbs], [ysw])

            def emit_PV(i):
                u, ci, ch, nchk = items[i]
                kl, vl, kb, vb = operands(u, ch)
                pO = accs[u % 2]; pt = pT[i % 3]
                mm(pO[0:65, :], vl, pt[:, :], ci == 0, ci == nchk - 1, [vb, pt], [pO])
                if ci == nchk - 1:
                    tailA(u)
                if ci == 0 and u > 0:
                    tailB(u - 1)

            LA = 2
            for i in range(min(LA, len(items))):
                emit_S(i)
            for i in range(len(items)):
                if i + LA < len(items):
                    emit_S(i + LA)
                emit_PV(i)
                yield
            tailB(len(units) - 1)

        def phaseC(ti):
            isctx = ti == 8
            c0 = ti * TT; n = CTX if isctx else TT; r = 1 if isctx else 0
            yield from mla(ti)
            yield from swa(ti)
            load(yT, yT[:, 4:8, 0:n], ymT[:, c0:c0 + n].rearrange("(c p) n -> p c n", p=128), [d_ymt[ti]])
            load(xt, xt[:, :, 0:n], xT[:, c0:c0 + n].rearrange("(c p) n -> p c n", p=128), [d_x[ti]])
            for oc in range(16):
                w = nxt("wt", wt)
                load(w, w[:, 0:1024], w_outa[l, oc], [d_w[l]])
                load(w, w[0:64, 1024:1024 + 2048], w_outb[l, oc], [d_w[l]])
                wa = w[:, 0:1024].rearrange("p (a b) -> p a b", b=128)
                wb = w[0:64, 1024:3072].rearrange("p (a b) -> p a b", b=128)
                p = nxt("ps", ps)
                for kc in range(8):
                    mm(p[:, 0:n], wa[:, kc, :], yT[:, kc, 0:n], kc == 0, False, [w, yT], [p])
                for hd in range(16):
                    mm(p[:, 0:n], wb[:, hd, :], ysw[0:64, hd, 0:n], False, hd == 15, [w, ysw], [p])
                stt("dve", xt[:, oc, 0:n], p[:, 0:n], modT[:, GT1 + oc, r:r + 1], xt[:, oc, 0:n], ALU.mult, ALU.add,
                    [p, modT, xt], [xt])
                yield
            norm_mod(n, r, a2, SH2)
            for half in range(4):
                for blk in range(8):
                    w = nxt("wt", wt)
                    load(w, w[:], w_ff1[l, half * 8 + blk], [d_w[l]])
                    wv = w[:].rearrange("p (a b) -> p a b", b=256)
                    for o4 in range(2):
                        hc = blk * 2 + o4
                        p = nxt("ps", ps)
                        for kc in range(KC):
                            mm(p[:, 0:n], wv[:, kc, o4 * 128:(o4 + 1) * 128], hT[:, kc, 0:n], kc == 0, kc == KC - 1, [w, hT], [p])
                        t_ = nxt("tmp", tmp)
                        ts("dve", t_[:, 0:n], p[:, 0:n], 0.0, ALU.max, [p], [t_])
                        act(big[:, hc, 0:n], t_[:, 0:n], AF.Square, [t_], [big])
                        yield
                for oc in range(16):
                    w = nxt("wt", wt)
                    load(w, w[:, 0:2048], w_ff2[l, half * 16 + oc], [d_w[l]])
                    wv = w[:, 0:2048].rearrange("p (a b) -> p a b", b=128)
                    p = nxt("ps", ps)
                    for kc in range(16):
                        mm(p[:, 0:n], wv[:, kc, :], big[:, kc, 0:n], kc == 0, kc == 15, [w, big], [p])
                    stt("dve", xt[:, oc, 0:n], p[:, 0:n], modT[:, GT2 + oc, r:r + 1], xt[:, oc, 0:n], ALU.mult, ALU.add,
                        [p, modT, xt], [xt])
                    yield
            if last and l == L - 1:
                for c in range(KC):
                    act(big[:, c, 0:n], xt[:, c, 0:n], AF.Square, [xt], [big])
                p = nxt("ps", ps)
                for c in range(KC):
                    mm(p[:, 0:n], onesb, big[:, c, 0:n], c == 0, c == KC - 1, [cm, big], [p])
                rstd_from(p, n, 1.0 / D, rs)
                for c in range(KC):
                    stt("dve", xt[:, c, 0:n], xt[:, c, 0:n], gf[:, c:c + 1], rs[:, 0:n], ALU.mult, ALU.mult, [xt, gf, rs], [xt])
                store(xt, out_ap[:, c0:c0 + n].rearrange("(c p) n -> p c n", p=128), xt[:, :, 0:n], [d_out])
            else:
                store(xt, xT[:, c0:c0 + n].rearrange("(c p) n -> p c n", p=128), xt[:, :, 0:n], [d_x[ti]])
                if l == L - 1:
                    store(xt, out_ap[:, c0:c0 + n].rearrange("(c p) n -> p c n", p=128), xt[:, :, 0:n], [d_out])

        def bwd_side(t):
            return chain([mlstm_tile((t * 4 + q_) * 128, 1, False) for q_ in (3, 2, 1, 0)])

        nxt_casts = cast_thunks(l + 1) if l + 1 < L else []
        per = (len(nxt_casts) + 6) // 7
        if need_ctx:
            interleave(phaseC(8), bwd_side(7), 3, 1)
        else:
            run(bwd_side(7))
        for ti in range(7, -1, -1):
            if ti > 0:
                interleave(phaseC(ti), bwd_side(ti - 1), 3, 1)
            else:
                run(phaseC(0))
            for t_ in nxt_casts[(7 - ti) * per:(8 - ti) * per]:
                t_()

    P.add("pool", lambda e: e.engine_nop() if hasattr(e, "engine_nop") else e.memset(cst[:, 7:8], 0.0), [d_out], [cst])
    P.emit(stack)
    stack.close()
    return nc


def bass_ap_halo_k(g_out, k_):
    return g_out[k_ * 576 + 512:k_ * 576 + 544, :].rearrange("r (q j) -> (r q) j", q=4).rearrange("(g d) j -> d g j", d=64)


def bass_ap_halo_v(g_out):
    return g_out.rearrange("(k x) n -> k x n", k=2)[:, 544:576, :].rearrange("k r (q j) -> (r q) k j", q=4)


def _const_masks():
    m = np.zeros((128, 8, 128), np.float32)
    i = np.arange(128)[:, None]; t = np.arange(128)[None, :]
    pm = np.arange(128)
    src = np.where((pm % 32) < 16, pm + 16, pm - 16)
    m[src, 0, pm] = 1.0
    m[:, 1, :] = (i >= t); m[:, 2, :] = (i <= t); m[:, 3, :] = (i + t >= 127)
    same = (i // 64) == (t // 64)
    m[:, 4, :] = same & (i <= t); m[:, 5, :] = same & (i >= t)
    m[:, 6, :] = (i == t); m[:, 7, :] = 1.0
    return m


def _rope_tables(pos):
    half = 32
    inv = (np.float32(10000.0) ** (-np.arange(0, half, 2, dtype=np.float32) / np.float32(half))).astype(np.float32)
    row = (pos // 64).astype(np.float32); col = (pos % 64).astype(np.float32)
    ar = row[:, None] * inv; ac = col[:, None] * inv
    ang = np.concatenate([ar, ar, ac, ac], axis=-1)
    cos = np.cos(ang).astype(np.float32).T; sin = np.sin(ang).astype(np.float32).T
    sgn = np.where((np.arange(64) % 32) < 16, -1.0, 1.0).astype(np.float32)[:, None]
    sin = sin * sgn
    return np.ascontiguousarray(np.concatenate([cos, cos], 0)), np.ascontiguousarray(np.concatenate([sin, sin], 0))


def _fm(v, k):
    return np.ascontiguousarray(v.reshape(k, 128).T)


def _prep_gate_parts(inp, odd):
    L = DEPTH
    gate = np.arange(3136, 3152)
    if odd:
        gate = np.concatenate([gate[8:16], gate[0:8]])
    cols = np.concatenate([np.arange(3584 - 16 + 16 + 0, 3584 - 16 + 16 + 0)[:0], np.arange(768, 832), gate])
    wb = inp["w_in"][:, :, cols].reshape(L, KC, 128, 80)
    out = {"win_b": np.ascontiguousarray(wb.transpose(0, 2, 1, 3), dtype=np.float32).reshape(L, 128, KC * 80)}
    gbv = inp["mlstm_gate_bias"].reshape(L, 16)
    if odd:
        gbv = np.concatenate([gbv[:, 8:16], gbv[:, 0:8]], axis=1)
    out["gbias"] = np.ascontiguousarray(gbv, dtype=np.float32)
    return out


def _prep_weights(inp, odd):
    L = DEPTH
    A = lambda a: np.ascontiguousarray(a, dtype=np.float32)
    w = {}
    wm = inp["w_mod"].reshape(L, KC, 128, 96, 128)
    w["wmod"] = A(wm.transpose(0, 3, 2, 1, 4)).reshape(L, 96, 128, KC * 128)
    w["bmodT"] = A(inp["b_mod"].reshape(L, 96, 128).transpose(0, 2, 1))
    w["g1T"] = A(inp["g_norm1"].reshape(L, KC, 128).transpose(0, 2, 1))
    w["g2T"] = A(inp["g_norm2"].reshape(L, KC, 128).transpose(0, 2, 1))
    gate = np.arange(3136, 3152)
    if odd:
        gate = np.concatenate([gate[8:16], gate[0:8]])
    cols = np.concatenate([np.arange(0, 768), np.arange(832, 3136), np.arange(3152, 3664), np.arange(768, 832), gate])
    win = inp["w_in"][:, :, cols]
    wa = win[:, :, :3584].reshape(L, KC, 128, 14, 256)
    w["win_a"] = A(wa.transpose(0, 3, 2, 1, 4)).reshape(L, 14, 128, KC * 256)
    wb = win[:, :, 3584:].reshape(L, KC, 128, 80)
    w["win_b"] = A(wb.transpose(0, 2, 1, 3)).reshape(L, 128, KC * 80)
    w["gqT"] = A(inp["mla_g_q"].reshape(L, 4, 128).transpose(0, 2, 1))
    w["gkvT"] = A(inp["mla_g_kv"].reshape(L, 2, 128).transpose(0, 2, 1))
    qc = np.concatenate([np.concatenate([h * 192 + np.arange(128) for h in range(4)]),
                         np.concatenate([h * 192 + 128 + np.arange(64) for h in range(4)])])
    wuq = inp["mla_w_uq"][:, :, qc].reshape(L, 4, 128, 768)
    w["wuq"] = A(wuq.transpose(0, 2, 1, 3)).reshape(L, 128, 4 * 768)
    kvc = np.concatenate([np.concatenate([h * 256 + np.arange(128) for h in range(4)]),
                          np.concatenate([h * 256 + 128 + np.arange(128) for h in range(4)])])
    wukv = inp["mla_w_ukv"][:, :, kvc].reshape(L, 2, 128, 1024)
    w["wukv"] = A(wukv.transpose(0, 2, 1, 3)).reshape(L, 128, 2 * 1024)
    w["sink"] = A(inp["swa_sink"])
    gbv = inp["mlstm_gate_bias"].reshape(L, 16)
    if odd:
        gbv = np.concatenate([gbv[:, 8:16], gbv[:, 0:8]], axis=1)
    w["gbias"] = A(gbv)
    w["ghT"] = A(inp["mlstm_g_h"].transpose(0, 2, 1))
    wo = inp["w_out"]
    rows_a = np.concatenate([np.arange(0, 512), np.arange(1536, 2048)])
    woa = wo[:, rows_a, :].reshape(L, 8, 128, 16, 128)
    w["wout_a"] = A(woa.transpose(0, 3, 2, 1, 4)).reshape(L, 16, 128, 1024)
    wob = wo[:, 512:1536, :].reshape(L, 16, 64, 16, 128)
    w["wout_b"] = A(wob.transpose(0, 3, 2, 1, 4)).reshape(L, 16, 64, 2048)
    f1 = inp["w_ff1"].reshape(L, KC, 128, 32, 256)
    w["wff1"] = A(f1.transpose(0, 3, 2, 1, 4)).reshape(L, 32, 128, KC * 256)
    f2 = inp["w_ff2"].reshape(L, 4, 16, 128, 16, 128)
    w["wff2"] = A(f2.transpose(0, 1, 4, 3, 2, 5)).reshape(L, 64, 128, 16 * 128)
    w["gfT"] = _fm(np.asarray(inp["g_final"], np.float32), KC)
    w["cmask"] = _const_masks()
    return w


_PER_LAYER = ("wmod", "bmodT", "g1T", "g2T", "win_a", "win_b", "gqT", "gkvT", "wuq", "wukv", "sink", "gbias",
              "ghT", "wout_a", "wout_b", "wff1", "wff2")
_NC_CACHE = {}


def _get_nc(layers, last):
    key = (layers, last)
    if key not in _NC_CACHE:
        _NC_CACHE[key] = build_program(layers, True, last)
    return _NC_CACHE[key]


FUSED = True


def kernel(**inp):
    inp = {k: np.asarray(v) for k, v in inp.items()}
    x = inp["x"]; ctx = inp["ctx"]; c = inp["c"]; c_ctx = inp["c_ctx"]
    wts = [_prep_weights(inp, 0), None]
    wts[1] = dict(wts[0])
    wts[1].update(_prep_gate_parts(inp, 1))
    per_core = []
    for core in range(8):
        b = core // 2; odd = core % 2
        if odd:
            xs = x[b, T:SEQ][::-1]; cs = ctx[b][::-1]; pos = (SEQ - 1 - np.arange(T))
        else:
            xs = x[b, 0:T]; cs = ctx[b]; pos = np.arange(T)
        xT = np.ascontiguousarray(np.concatenate([xs, cs], 0).T.astype(np.float32))
        cT = np.stack([_fm(c[b].astype(np.float32), KC), _fm(c_ctx.astype(np.float32), KC)], axis=-1)
        cos2, sin2 = _rope_tables(pos)
        selw = np.zeros((128, 2), np.float32); selw[:, 1 - odd] = 1.0
        per_core.append({"xT": xT, "cT": np.ascontiguousarray(cT), "cos2": cos2, "sin2": sin2, "selw": selw})

    def maps(lsl, xTs):
        out = []
        for core in range(8):
            w = wts[core % 2]
            m = dict(per_core[core])
            if xTs is not None:
                m["xT"] = xTs[core]
            for k in _PER_LAYER:
                m[k] = w[k][lsl]
            m["gfT"] = w["gfT"]; m["cmask"] = w["cmask"]
            out.append(m)
        return out

    if FUSED:
        nc = _get_nc(DEPTH, True)
        res = run_bass_kernel_spmd(nc, maps(slice(0, DEPTH), None), core_ids=list(range(8)))
        outs = [np.asarray(r["outT"]) for r in res.results]
    else:
        xTs = None
        for l in range(DEPTH):
            lastl = l == DEPTH - 1
            nc = _get_nc(1, lastl)
            res = run_bass_kernel_spmd(nc, maps(slice(l, l + 1), xTs), core_ids=list(range(8)))
            if lastl:
                outs = [np.asarray(r["outT"]) for r in res.results]
            else:
                xTs = [np.asarray(r["xT_out"]) for r in res.results]
    out = np.empty((BATCH, SEQ, D), np.float32)
    for core in range(8):
        b = core // 2
        o = outs[core].T
        if core % 2:
            out[b, T:SEQ] = o[::-1]
        else:
            out[b, 0:T] = o
    return out
```

```python
import contextlib
import numpy as np
import concourse.bass as bass
import concourse.mybir as mybir
from concourse.bass_utils import run_bass_kernel_spmd

F32, BF16 = mybir.dt.float32, mybir.dt.bfloat16
AF = mybir.ActivationFunctionType
ALU = mybir.AluOpType

D = 2048; KC = 16; DEPTH = 4; SEQ = 8192; BATCH = 4; CTX = 256
T = 4096; NT = T + CTX; TT = 512
DFF = 8192
EPS = 1e-6
MLA_SCALE = 192.0 ** -0.5
NROWS_G = 8768


class Buf:
    def __init__(self, t, name):
        self.t = t; self.name = name
        self.lastw = None; self.reads = []
        self.semid = None; self.dcnt = 0
        self.sg = self

    def __getitem__(self, k):
        return self.t[k]


class Op:
    __slots__ = ("eng", "fn", "deps", "signal", "val", "dma", "idx")

    def __init__(self, eng, fn, dma=None):
        self.eng = eng; self.fn = fn; self.deps = []; self.signal = False
        self.val = None; self.dma = dma


class Prog:
    ENGS = ("pe", "act", "dve", "pool", "sp")

    def __init__(self, nc):
        self.nc = nc
        self.ops = {e: [] for e in self.ENGS}
        self.nsem = 0
        self.semholders = []
        self.extra_sems = 0

    def new_sem(self):
        self.nsem += 1
        return self.nsem - 1

    def _tok(self, op):
        if op.dma is not None:
            b = op.dma
            return ("d", b.semid, b.dcnt * 16)
        return ("o", op)

    def add(self, eng, fn, reads=(), writes=(), dma=None, cc=False):
        op = Op(eng, fn, dma)
        deps = []
        for b in reads:
            if b.lastw is not None:
                deps.append(b.lastw)
        for b in writes:
            if b.lastw is not None:
                deps.append(b.lastw)
            deps.extend(b.reads)
        res = []
        for d in deps:
            if d.dma is not None:
                res.append(("d", d.dma.semid, d.dma.dcnt * 16 if not isinstance(d.dma, CCSem) else 1))
            else:
                if d.eng == eng and eng == "pe":
                    continue
                d.signal = True
                res.append(("o", d))
        op.deps = res
        if dma is not None:
            if dma.semid is None:
                dma.semid = self.new_sem()
            dma.dcnt += 1
        for b in reads:
            b.reads.append(op)
        for b in writes:
            b.lastw = op; b.reads = []
        self.ops[eng].append(op)
        return op

    def emit(self, stack):
        nc = self.nc
        engsem = {e: self.new_sem() for e in ("pe", "act", "dve", "pool")}
        sems = [stack.enter_context(nc.semaphore(f"s{i}")) for i in range(self.nsem)]
        for e in ("pe", "act", "dve", "pool"):
            c = 0
            for op in self.ops[e]:
                if op.dma is None and op.signal:
                    c += 1; op.val = c
        block = stack.enter_context(nc.Block())
        handles = {"pe": block.tensor, "act": block.scalar, "dve": block.vector,
                   "pool": block.gpsimd, "sp": block.sync}

        def run(ename):
            def body(eng):
                waited = {}
                for op in self.ops[ename]:
                    for d in op.deps:
                        if d[0] == "d":
                            sid, v = d[1], d[2]
                        else:
                            sid, v = engsem[d[1].eng], d[1].val
                        if waited.get(sid, 0) >= v:
                            continue
                        waited[sid] = v
                        eng.wait_ge(sems[sid], v)
                    ins = op.fn(eng)
                    if op.dma is not None:
                        if isinstance(op.dma, CCSem):
                            ins.then_inc(sems[op.dma.semid])
                        else:
                            ins.then_inc(sems[op.dma.semid], 16)
                    elif op.signal:
                        ins.then_inc(sems[engsem[ename]], 1)
            return body

        for e in self.ENGS:
            if self.ops[e]:
                handles[e](run(e))


class CCSem(Buf):
    pass


def build_program(layers, first, last, debug=()):
    L = layers
    nc = bass.Bass("TRN2", target_bir_lowering=False)
    P = Prog(nc)
    stack = contextlib.ExitStack()

    def din(name, shape, dt=F32):
        return nc.dram_tensor(name, list(shape), dt, kind="ExternalInput").ap()

    def dscr(name, shape, dt):
        kind = "ExternalOutput" if name in debug else "Internal"
        return nc.dram_tensor(name, list(shape), dt, kind=kind).ap()

    xT_in = din("xT", [D, NT])
    cT_in = din("cT", [128, KC, 2])
    cos_in = din("cos2", [128, T]); sin_in = din("sin2", [128, T])
    selw_in = din("selw", [128, 2])
    cmask_in = din("cmask", [128, 8, 128])
    wmod_in = din("wmod", [L, 96, 128, KC * 128])
    bmod_in = din("bmodT", [L, 128, 96])
    g1_in = din("g1T", [L, 128, KC]); g2_in = din("g2T", [L, 128, KC])
    wina_in = din("win_a", [L, 14, 128, KC * 256]); winb_in = din("win_b", [L, 128, KC * 80])
    gq_in = din("gqT", [L, 128, 4]); gkv_in = din("gkvT", [L, 128, 2])
    wuq_in = din("wuq", [L, 128, 4 * 768]); wukv_in = din("wukv", [L, 128, 2 * 1024])
    sink_in = din("sink", [L, 16]); gbias_in = din("gbias", [L, 16])
    gh_in = din("ghT", [L, 128, 4])
    wouta_in = din("wout_a", [L, 16, 128, 8 * 128]); woutb_in = din("wout_b", [L, 16, 64, 16 * 128])
    wff1_in = din("wff1", [L, 32, 128, KC * 256]); wff2_in = din("wff2", [L, 64, 128, 16 * 128])
    gf_in = din("gfT", [128, KC])
    if last:
        out_ap = nc.dram_tensor("outT", [D, T], F32, kind="ExternalOutput").ap()
    else:
        out_ap = nc.dram_tensor("xT_out", [D, NT], F32, kind="ExternalOutput").ap()

    xT = dscr("xTs", [D, NT], F32)
    w_ina = dscr("b_win_a", [L, 14, 128, KC * 256], BF16); w_inb = dscr("b_win_b", [L, 128, KC * 80], BF16)
    w_uq = dscr("b_wuq", [L, 128, 4 * 768], BF16); w_ukv = dscr("b_wukv", [L, 128, 2 * 1024], BF16)
    w_outa = dscr("b_wouta", [L, 16, 128, 8 * 128], BF16); w_outb = dscr("b_woutb", [L, 16, 64, 16 * 128], BF16)
    w_ff1 = dscr("b_wff1", [L, 32, 128, KC * 256], BF16); w_ff2 = dscr("b_wff2", [L, 64, 128, 16 * 128], BF16)
    cmask_b = dscr("b_cmask", [128, 8, 128], BF16)
    w_modb = dscr("b_wmod", [L, 96, 128, KC * 128], BF16)
    qmla = dscr("qmla", [768, NT], BF16)
    GK = [dscr(f"gk{i}", [2048, 512], BF16) for i in range(2)]; GKo = [dscr(f"gko{i}", [4096, 512], BF16) for i in range(2)]
    GV = [dscr(f"gv{i}", [2048, 512], BF16) for i in range(2)]; GVo = [dscr(f"gvo{i}", [4096, 512], BF16) for i in range(2)]
    GR = dscr("gr", [576, 512], BF16); GRo = dscr("gro", [1152, 512], BF16)
    s_in = dscr("s_in", [256, 129], F32); s_out = dscr("s_out", [512, 129], F32)
    kc_n = dscr("kc_n", [512, CTX], BF16); kc_r = dscr("kc_r", [64, CTX], BF16); vc_m = dscr("vc_m", [CTX, 512], BF16)
    swaq = dscr("swaq", [1024, NT], BF16); swak = dscr("swak", [128, NT], BF16); swav = dscr("swav", [NT, 128], BF16)
    mq = dscr("mq", [256, NT], BF16); mk = dscr("mk", [256, NT], BF16)
    mkt = dscr("mkt", [NT, 256], BF16); mvt = dscr("mvt", [NT, 512], BF16)
    gts = dscr("gts", [NT, 16], F32); zo = dscr("zo", [512, NT], BF16)
    hf = dscr("hf", [512, NT], F32); ymT = dscr("ymT", [512, NT], BF16)

    def DB(name):
        return Buf(None, name)
    d_x = [DB(f"x{i}") for i in range(9)]
    d_w = [DB(f"wts{i}") for i in range(L)]; d_wm = [DB(f"wm{i}") for i in range(L)]; d_win = [DB(f"win{i}") for i in range(L)]; d_q = [DB(f"q{i}") for i in range(9)]
    d_gk = [DB("gk0"), DB("gk1")]; d_gv = [DB("gv0"), DB("gv1")]; d_gr = DB("gr")
    d_gko = [DB("gko0"), DB("gko1")]; d_gvo = [DB("gvo0"), DB("gvo1")]; d_gro = DB("gro")
    d_sin = DB("sin"); d_sout = DB("sout")
    d_kc = DB("kc"); d_sw = [DB(f"sw{i}") for i in range(9)]; d_mt = [DB(f"m{i}") for i in range(9)]; d_hf = DB("hf"); d_ymt = [DB(f"ym{i}") for i in range(9)]
    d_out = DB("out")

    def sb(name, shape, dt=F32):
        return Buf(stack.enter_context(nc.sbuf_tensor(name, list(shape), dt)), name)

    def psb(name, shape, dt=F32):
        return Buf(stack.enter_context(nc.psum_tensor(name, list(shape), dt)), name)

    xt = sb("xt", [128, KC, TT])
    hT = sb("hT", [128, KC, TT], BF16)
    big = sb("big", [128, 16, TT], BF16)
    wt = [sb(f"wt{i}", [128, 4096], BF16) for i in range(2)]
    yT = sb("yT", [128, 8, TT], BF16)
    ysw = big; qsw = hT
    wuq_s = sb("wuq_s", [128, 4, 768], BF16); wukv_s = sb("wukv_s", [128, 2, 1024], BF16)
    zf = sb("zf", [128, 4, TT]); cz = sb("cz", [128, 4, TT], BF16)
    rs = sb("rs", [128, TT]); tmp = [sb(f"tmp{i}", [128, TT]) for i in range(2)]
    stg = [sb(f"stg{i}", [128, 4, TT], BF16) for i in range(2)]
    zb = [sb(f"zb{i}", [128, TT], BF16) for i in range(2)]
    gstg = sb("gstg", [128, 4, 16])
    cosS = sb("cosS", [128, TT]); sinS = sb("sinS", [128, TT])
    cm = sb("cm", [128, 8, 128], BF16); cmf = sb("cmf", [128, 3, 128])
    cst = sb("cst", [128, 8])
    selw = sb("selw_s", [128, 2])
    scT = sb("scT", [128, KC, 2]); scb = sb("scb", [128, KC, 2], BF16); wm = [sb(f"wm{i}", [128, KC, 128], BF16) for i in range(2)]
    modT = sb("modT", [128, 96, 2]); bmod = sb("bmod", [128, 96])
    g1 = sb("g1", [128, KC]); g2 = sb("g2", [128, KC]); gf = sb("gf", [128, KC])
    a1 = sb("a1", [128, KC, 2]); a2 = sb("a2", [128, KC, 2])
    gq = sb("gq", [128, 4]); gkv = sb("gkv", [128, 2]); gh = sb("gh", [128, 4])
    sk = sb("sk", [128, 16]); gb = sb("gb", [128, 16])
    qn = sb("qn", [128, 4, TT], BF16); qr = sb("qr", [64, 4, TT], BF16)
    kk = [sb(f"kk{i}", [128, 1024], BF16) for i in range(3)]
    kr = [sb(f"kr{i}", [64, 1024], BF16) for i in range(3)]
    vv = [sb(f"vv{i}", [128, 8, 128], BF16) for i in range(3)]
    pT = [sb(f"pT{i}", [128, TT], BF16) for i in range(3)]
    rec = rs
    ksw = sb("ksw", [64, 2, 768], BF16); vsw = sb("vsw", [128, 6, 2, 65], BF16)
    kcs = sb("kcs", [64, 2, CTX], BF16); vcs = sb("vcs", [128, 2, 2, 65], BF16)
    hk2 = sb("hk2", [64, 2, 2, 128], BF16); hv2 = sb("hv2", [128, 2, 128], BF16)
    hks = sb("hks", [64, 2, 128], BF16); hvs = sb("hvs", [128, 2, 65], BF16)
    rbs = zf
    mqs = sb("mqs", [64, 4, 128], BF16); mks = sb("mks", [64, 4, 128], BF16)
    mkts = sb("mkts", [128, 256], BF16); mvts = sb("mvts", [128, 4, 129], BF16)
    gt = sb("gt", [128, 16]); lf = sb("lf", [128, 8]); lfr = sb("lfr", [128, 4, 64])
    bb = sb("bb", [128, 8]); gg = sb("gg", [128, 8]); bk = sb("bk", [128, 8]); kst = sb("kst", [128, 8])
    Cst = [[sb(f"C{d}{h}", [64, 129]) for h in range(4)] for d in range(2)]
    msets = []
    for si in range(2):
        msets.append((sb(f"abc{si}", [64, 128]), sb(f"eg{si}", [64, 2]), sb(f"qp{si}", [64, 128], BF16),
                      sb(f"ptm{si}", [128, 128], BF16), sb(f"kpp{si}", [128, 64], BF16),
                      sb(f"Cbf{si}", [64, 128], BF16), sb(f"nrep{si}", [64, 128], BF16),
                      sb(f"dm{si}", [128, 128]), sb(f"hdir{si}", [128, 128]), sb(f"hfs{si}", [128, 128]),
                      sb(f"zos{si}", [128, 128], BF16), sb(f"sg{si}", [128, 128]), sb(f"sqm{si}", [128, 128], BF16),
                      sb(f"yms{si}", [128, 128], BF16)))
    sst = sb("sst", [64, 2, 129])
    psall = [psb(f"ps{i}", [128, 512]) for i in range(8)]
    acc0, acc1 = psall[0], psall[1]
    pS3 = psall[2:5]
    ps = [acc0, acc1] + psall[2:5]
    mviews = [(psall[5], psall[6], psall[7]), (acc0, acc1, psall[2])]
    mgate = psall[3]
    cc_sems = []

    ld_q = "sp"; st_q = "pool"

    def dma(q, out, in_, reads, writes, sbuf):
        P.add(q, lambda e, o=out, i=in_: e.dma_start(out=o, in_=i), reads, writes, dma=sbuf)

    def load(sbuf, out, in_, dram=()):
        dma(ld_q, out, in_, list(dram), [sbuf], sbuf.sg)

    def store(sbuf, out, in_, dram=()):
        dma(st_q, out, in_, [sbuf], list(dram), sbuf.sg)

    def share(bufs):
        for b_ in bufs[1:]:
            b_.sg = bufs[0]

    share([bmod, g1, g2, gq, gkv, gh, sk, gb, gf, selw, scT, cm, cmf])
    share([gt, mqs, mks, mkts, mvts])
    share([hk2, hv2, kcs, vcs, sst])
    share([Cst[0][0], Cst[0][1], Cst[0][2], Cst[0][3]])
    share([cosS, sinS])

    def mm(out, lhsT, rhs, start, stop, reads, writes):
        P.add("pe", lambda e: e.matmul(out, lhsT=lhsT, rhs=rhs, start=start, stop=stop), reads, writes)

    def act(out, in_, func, reads, writes, bias=0.0, scale=1.0):
        P.add("act", lambda e: e.activation(out=out, in_=in_, func=func, bias=bias, scale=scale), reads, writes)

    def tt(eng, out, in0, in1, op, reads, writes):
        P.add(eng, lambda e: e.tensor_tensor(out=out, in0=in0, in1=in1, op=op), reads, writes)

    def ts(eng, out, in0, s1, op0, reads, writes, s2=None, op1=None):
        if op1 is None:
            P.add(eng, lambda e: e.tensor_scalar(out=out, in0=in0, scalar1=s1, scalar2=None, op0=op0), reads, writes)
        else:
            P.add(eng, lambda e: e.tensor_scalar(out=out, in0=in0, scalar1=s1, scalar2=s2, op0=op0, op1=op1), reads, writes)

    def stt(eng, out, in0, scalar, in1, op0, op1, reads, writes):
        P.add(eng, lambda e: e.scalar_tensor_tensor(out=out, in0=in0, scalar=scalar, in1=in1, op0=op0, op1=op1), reads, writes)

    def cp(eng, out, in_, reads, writes):
        if eng == "act":
            P.add("act", lambda e: e.copy(out=out, in_=in_), reads, writes)
        else:
            P.add(eng, lambda e: e.tensor_copy(out=out, in_=in_), reads, writes)

    def recip(out, in_, reads, writes):
        P.add("dve", lambda e: e.reciprocal(out=out, in_=in_), reads, writes)

    def transpose(out, in_, reads, writes):
        P.add("pe", lambda e: e.transpose(out, in_, cm[:, 6, :]), list(reads) + [cm], writes)

    rr = {"ps": 0, "wt": 0, "tmp": 0, "stg": 0, "zb": 0, "pT": 0, "wm": 0, "kk": 0}

    def nxt(key, lst):
        rr[key] = (rr[key] + 1) % len(lst)
        return lst[rr[key]]

    EPS_AP = lambda n=128, p0=0: cst[p0:p0 + n, 0:1]
    ONE_AP = lambda n=128, p0=0: cst[p0:p0 + n, 1:2]

    def rstd_from(psum, n, scale, out_buf, np_=128):
        act(out_buf[0:np_, 0:n], psum[0:np_, 0:n], AF.Ln, [psum, cst], [out_buf], bias=EPS_AP(np_), scale=scale)
        act(out_buf[0:np_, 0:n], out_buf[0:np_, 0:n], AF.Exp, [out_buf], [out_buf], scale=-0.5)

    P.add("pool", lambda e: e.memset(cst[:, 0:1], EPS), [], [cst])
    P.add("pool", lambda e: e.memset(cst[:, 1:2], 1.0), [], [cst])
    P.add("pool", lambda e: e.memset(vsw[:], 1.0), [], [vsw])
    P.add("pool", lambda e: e.memset(vcs[:], 1.0), [], [vcs])
    P.add("pool", lambda e: e.memset(mvts[:], 1.0), [], [mvts])
    P.add("pool", lambda e: e.memset(hvs[:], 1.0), [], [hvs])
    d_cm = DB("cmaskb"); castsem = Buf(None, "castsem")
    dma("pool", cmask_b, cmask_in, [], [d_cm], castsem)
    load(cm, cm[:], cmask_b, [d_cm])
    load(cmf, cmf[:, 0:2, :], cmask_in[:, 4:6, :])
    load(cmf, cmf[:, 2:3, :], cmask_in[:, 7:8, :])
    load(selw, selw[:], selw_in)
    load(scT, scT[:], cT_in)
    load(gf, gf[:], gf_in)
    def cast_thunks(l):
        cs_m = Buf(None, f"castm{l}"); cs_i = Buf(None, f"casti{l}"); cs_ = Buf(None, f"cast{l}")
        th = []

        def C(out, in_, dep, sem):
            th.append(lambda: dma("pool", out, in_, [], [dep], sem))
        for i in range(8):
            C(w_modb[l, i * 12:(i + 1) * 12], wmod_in[l, i * 12:(i + 1) * 12], d_wm[l], cs_m)
        for i in range(7):
            C(w_ina[l, 2 * i:2 * i + 2], wina_in[l, 2 * i:2 * i + 2], d_win[l], cs_i)
        C(w_inb[l], winb_in[l], d_win[l], cs_i)
        C(w_uq[l], wuq_in[l], d_win[l], cs_i)
        C(w_ukv[l], wukv_in[l], d_win[l], cs_i)
        for i in range(4):
            C(w_outa[l, i * 4:(i + 1) * 4], wouta_in[l, i * 4:(i + 1) * 4], d_w[l], cs_)
            C(w_outb[l, i * 4:(i + 1) * 4], woutb_in[l, i * 4:(i + 1) * 4], d_w[l], cs_)
        for i in range(8):
            C(w_ff1[l, i * 4:(i + 1) * 4], wff1_in[l, i * 4:(i + 1) * 4], d_w[l], cs_)
        for i in range(8):
            C(w_ff2[l, i * 8:(i + 1) * 8], wff2_in[l, i * 8:(i + 1) * 8], d_w[l], cs_)
        return th

    for t_ in cast_thunks(0):
        t_()
    xcp = Buf(None, "xcp")
    for i in range(9):
        c0 = i * TT; n = TT if i < 8 else CTX
        dma("sp", xT[:, c0:c0 + n], xT_in[:, c0:c0 + n], [], [d_x[i]], xcp)
    act(tmp[0][:, 0:32], scT[:].rearrange("p a b -> p (a b)"), AF.Exp, [scT], [tmp[0]], scale=-1.0)
    ts("dve", tmp[0][:, 0:32], tmp[0][:, 0:32], 1.0, ALU.add, [tmp[0]], [tmp[0]])
    recip(tmp[0][:, 0:32], tmp[0][:, 0:32], [tmp[0]], [tmp[0]])
    tt("dve", scT[:].rearrange("p a b -> p (a b)"), scT[:].rearrange("p a b -> p (a b)"), tmp[0][:, 0:32], ALU.mult, [scT, tmp[0]], [scT])

    cp("dve", scb[:], scT[:], [scT], [scb])
    def run(g):
        for _ in g:
            pass

    def chain(gs):
        for g in gs:
            yield from g

    def interleave(main, side, km, ks):
        done = False
        while not done:
            for _ in range(km):
                try:
                    next(main)
                except StopIteration:
                    done = True
                    break
            if done:
                break
            for _ in range(ks):
                try:
                    next(side)
                except StopIteration:
                    side = iter(())
                    break
        for _ in side:
            pass

    onesb = cm[:, 7, :]

    for l in range(L):
        need_ctx = not (last and l == L - 1)
        load(bmod, bmod[:], bmod_in[l]); load(g1, g1[:], g1_in[l]); load(g2, g2[:], g2_in[l])
        load(gq, gq[:], gq_in[l]); load(gkv, gkv[:], gkv_in[l]); load(gh, gh[:], gh_in[l])
        load(sk, sk[:], sink_in[l].partition_broadcast(128)); load(gb, gb[:], gbias_in[l].partition_broadcast(128))
        load(wuq_s, wuq_s[:].rearrange("p a b -> p (a b)"), w_uq[l], [d_win[l]])
        load(wukv_s, wukv_s[:].rearrange("p a b -> p (a b)"), w_ukv[l], [d_win[l]])
        act(sk[:], sk[:], AF.Exp, [sk], [sk])
        pm = acc0
        for j in range(96):
            w = nxt("wm", wm)
            load(w, w[:].rearrange("p a b -> p (a b)"), w_modb[l, j], [d_wm[l]])
            for kc in range(KC):
                mm(pm[:, 2 * j:2 * j + 2], w[:, kc, :], scb[:, kc, :], kc == 0, kc == KC - 1, [w, scb], [pm])
        tt("dve", modT[:], pm[:, 0:192].rearrange("p (a b) -> p a b", b=2),
           bmod[:].unsqueeze(2).to_broadcast([128, 96, 2]), ALU.add, [pm, bmod], [modT])
        for (a, g, off) in ((a1, g1, 16), (a2, g2, 64)):
            ts("dve", a[:], modT[:, off:off + 16, :], 1.0, ALU.add, [modT], [a])
            tt("dve", a[:], a[:], g[:].unsqueeze(2).to_broadcast([128, 16, 2]), ALU.mult, [a, g], [a])
        SH1, GT1, SH2, GT2 = 0, 32, 48, 80

        def norm_mod(n, r, a, shoff):
            for c in range(KC):
                act(big[:, c, 0:n], xt[:, c, 0:n], AF.Square, [xt], [big])
            p = nxt("ps", ps)
            for c in range(KC):
                mm(p[:, 0:n], onesb, big[:, c, 0:n], c == 0, c == KC - 1, [cm, big], [p])
            rstd_from(p, n, 1.0 / D, rs)
            for c in range(KC):
                t_ = nxt("tmp", tmp)
                tt("dve", t_[:, 0:n], xt[:, c, 0:n], rs[:, 0:n], ALU.mult, [xt, rs], [t_])
                act(hT[:, c, 0:n], t_[:, 0:n], AF.Identity, [t_, a, modT], [hT],
                    bias=modT[:, shoff + c, r:r + 1], scale=a[:, c, r:r + 1])

        def phaseA(ti):
            isctx = ti == 8
            c0 = ti * TT; n = CTX if isctx else TT; r = 1 if isctx else 0
            nsub = n // 128
            load(xt, xt[:, :, 0:n], xT[:, c0:c0 + n].rearrange("(c p) n -> p c n", p=128), [d_x[ti]])
            if not isctx:
                load(cosS, cosS[:], cos_in[:, c0:c0 + n]); load(sinS, sinS[:], sin_in[:, c0:c0 + n])
            norm_mod(n, r, a1, SH1)

            def rope(src_ps, np_, dst):
                z = nxt("zb", zb)
                cp("act", z[0:np_, 0:n], src_ps[0:np_, 0:n], [src_ps], [z])
                if isctx:
                    cp("dve", dst, z[0:np_, 0:n], [z], [dstbuf[0]])
                    return
                pr = nxt("ps", ps)
                mm(pr[0:np_, 0:n], cm[0:np_, 0, 0:np_], z[0:np_, 0:n], True, True, [cm, z], [pr])
                t1 = nxt("tmp", tmp)
                tt("dve", t1[0:np_, 0:n], z[0:np_, 0:n], cosS[0:np_, 0:n], ALU.mult, [z, cosS], [t1])
                t2 = nxt("tmp", tmp)
                tt("dve", t2[0:np_, 0:n], pr[0:np_, 0:n], sinS[0:np_, 0:n], ALU.mult, [pr, sinS], [t2])
                tt("dve", dst, t1[0:np_, 0:n], t2[0:np_, 0:n], ALU.add, [t1, t2], [dstbuf[0]])

            dstbuf = [None]

            def latent_norm(nch, gvec, src_chunks_done):
                p = nxt("ps", ps)
                for c in range(nch):
                    act(big[:, c, 0:n], zf[:, c, 0:n], AF.Square, [zf], [big])
                for c in range(nch):
                    mm(p[:, 0:n], onesb, big[:, c, 0:n], c == 0, c == nch - 1, [cm, big], [p])
                rstd_from(p, n, 1.0 / (nch * 128), rs)
                for c in range(nch):
                    stt("dve", cz[:, c, 0:n], zf[:, c, 0:n], gvec[:, c:c + 1], rs[:, 0:n], ALU.mult, ALU.mult,
                        [zf, gvec, rs], [cz])

            for blk in range(15):
                w = nxt("wt", wt)
                if blk < 14:
                    load(w, w[:], w_ina[l, blk], [d_win[l]]); wv = w[:].rearrange("p (a b) -> p a b", b=256)
                else:
                    load(w, w[:, 0:KC * 80], w_inb[l], [d_win[l]]); wv = w[:, 0:KC * 80].rearrange("p (a b) -> p a b", b=80)
                if blk in (10, 11):
                    s = nxt("stg", stg)
                    for j in range(nsub):
                        p = nxt("ps", ps)
                        for kc in range(KC):
                            mm(p[:, 0:256], hT[:, kc, j * 128:(j + 1) * 128], wv[:, kc, :], kc == 0, kc == KC - 1, [hT, w], [p])
                        cp("act" if j % 2 else "dve", s[:, j, 0:256], p[:, 0:256], [p], [s])
                    store(s, mvt[c0:c0 + n, (blk - 10) * 256:(blk - 9) * 256].rearrange("(j p) f -> p j f", p=128), s[:, 0:nsub, 0:256], [d_mt[ti]])
                    yield
                    continue
                if blk == 14:
                    p = nxt("ps", ps)
                    for kc in range(KC):
                        mm(p[0:64, 0:n], wv[:, kc, 0:64], hT[:, kc, 0:n], kc == 0, kc == KC - 1, [w, hT], [p])
                    s = nxt("stg", stg); dstbuf[0] = s
                    rope(p, 64, s[0:64, 0, 0:n])
                    if isctx:
                        store(s, kc_r[:, :], s[0:64, 0, 0:n], [d_kc])
                    else:
                        store(s, GR[0:512, :].rearrange("(f t) n -> f t n", t=8)[:, ti, :], s[0:64, 0, 0:n], [d_gr])
                    for j in range(nsub):
                        p = nxt("ps", ps)
                        for kc in range(KC):
                            mm(p[:, 0:16], hT[:, kc, j * 128:(j + 1) * 128], wv[:, kc, 64:80], kc == 0, kc == KC - 1, [hT, w], [p])
                        tt("dve", gstg[:, j, :], p[:, 0:16], gb[:], ALU.add, [p, gb], [gstg])
                    store(gstg, gts[c0:c0 + n, :].rearrange("(j p) f -> p j f", p=128), gstg[:, 0:nsub, :], [d_mt[ti]])
                    yield
                    continue
                if blk in (7, 9):
                    s = nxt("stg", stg)
                    c_lo, c_n = (128, 128) if blk == 7 else (0, 256)
                    for j in range(nsub):
                        p = nxt("ps", ps)
                        for kc in range(KC):
                            mm(p[:, 0:c_n], hT[:, kc, j * 128:(j + 1) * 128], wv[:, kc, c_lo:c_lo + c_n], kc == 0, kc == KC - 1, [hT, w], [p])
                        cp("act" if j % 2 else "dve", s[:, j, 0:c_n], p[:, 0:c_n], [p], [s])
                    if blk == 7:
                        store(s, swav[c0:c0 + n, :].rearrange("(j p) f -> p j f", p=128), s[:, 0:nsub, 0:128], [d_sw[ti]])
                        if ti == 7:
                            store(s, GR[544:576, :].rearrange("r (q j) -> (r q) j", q=4), s[:, 3, 0:128], [d_gr])
                    else:
                        store(s, mkt[c0:c0 + n, :].rearrange("(j p) f -> p j f", p=128), s[:, 0:nsub, 0:256], [d_mt[ti]])
                    yield
                for oc2 in range(2):
                    oc = blk * 2 + oc2
                    if oc == 15:
                        continue
                    p = nxt("ps", ps)
                    for kc in range(KC):
                        mm(p[:, 0:n], wv[:, kc, oc2 * 128:(oc2 + 1) * 128], hT[:, kc, 0:n], kc == 0, kc == KC - 1, [w, hT], [p])
                    if oc < 4:
                        cp("act", zf[:, oc, 0:n], p[:, 0:n], [p], [zf])
                        if oc == 3:
                            latent_norm(4, gq, None)
                            s = nxt("stg", stg); s2 = nxt("stg", stg)
                            for qc in range(6):
                                p2 = nxt("ps", ps)
                                for kc in range(4):
                                    mm(p2[:, 0:n], wuq_s[:, kc, qc * 128:(qc + 1) * 128], cz[:, kc, 0:n], kc == 0, kc == 3, [wuq_s, cz], [p2])
                                if qc < 4:
                                    cp("act", s[:, qc, 0:n], p2[:, 0:n], [p2], [s])
                                else:
                                    dstbuf[0] = s2
                                    rope(p2, 128, s2[:, qc - 4, 0:n])
                            store(s, qmla[0:512, c0:c0 + n].rearrange("(c p) n -> p c n", p=128), s[:, 0:4, 0:n], [d_q[ti]])
                            store(s2, qmla[512:768, c0:c0 + n].rearrange("(c p) n -> p c n", p=128), s2[:, 0:2, 0:n], [d_q[ti]])
                    elif oc < 6:
                        cp("act", zf[:, oc - 4, 0:n], p[:, 0:n], [p], [zf])
                        if oc == 5:
                            latent_norm(2, gkv, None)
                            s = nxt("stg", stg)
                            for h in range(4):
                                p2 = nxt("ps", ps)
                                for kc in range(2):
                                    mm(p2[:, 0:n], wukv_s[:, kc, h * 128:(h + 1) * 128], cz[:, kc, 0:n], kc == 0, kc == 1, [wukv_s, cz], [p2])
                                cp("act", s[:, h, 0:n], p2[:, 0:n], [p2], [s])
                            if isctx:
                                store(s, kc_n[:, :].rearrange("(c p) n -> p c n", p=128), s[:, 0:4, 0:n], [d_kc])
                            else:
                                for i_ in range(2):
                                    store(s, GK[i_][:, :].rearrange("(c p t) n -> p c t n", p=128, t=8)[:, :, ti, :], s[:, 2 * i_:2 * i_ + 2, 0:n], [d_gk[i_]])
                            s = nxt("stg", stg)
                            for j in range(nsub):
                                p2 = nxt("ps", ps)
                                for kc in range(2):
                                    mm(p2[:, :], cz[:, kc, j * 128:(j + 1) * 128], wukv_s[:, kc, 512:1024], kc == 0, kc == 1, [cz, wukv_s], [p2])
                                cp("dve", s[:, j, :], p2[:, :], [p2], [s])
                            if isctx:
                                store(s, vc_m[:, :].rearrange("(j p) f -> p j f", p=128), s[:, 0:nsub, :], [d_kc])
                            else:
                                store(s, GV[c0 // 2048][c0 % 2048:c0 % 2048 + n, :].rearrange("(j p) f -> p j f", p=128), s[:, 0:nsub, :], [d_gv[c0 // 2048]])
                    elif oc < 15:
                        s = nxt("stg", stg); dstbuf[0] = s
                        rope(p, 128, s[:, 0, 0:n])
                        if oc < 14:
                            store(s, swaq[(oc - 6) * 128:(oc - 5) * 128, c0:c0 + n], s[:, 0, 0:n], [d_sw[ti]])
                        else:
                            store(s, swak[:, c0:c0 + n], s[:, 0, 0:n], [d_sw[ti]])
                            if ti == 7:
                                store(s, GR[512:544, :].rearrange("r (q j) -> (r q) j", q=4), s[:, 0, 384:512], [d_gr])
                    elif 18 <= oc < 20:
                        z = nxt("zb", zb)
                        cp("act", z[:, 0:n], p[:, 0:n], [p], [z])
                        store(z, mk[(oc - 18) * 128:(oc - 17) * 128, c0:c0 + n], z[:, 0:n], [d_mt[ti]])
                    elif oc == 15:
                        pass
                    elif oc < 18:
                        z = nxt("zb", zb)
                        cp("act", z[:, 0:n], p[:, 0:n], [p], [z])
                        store(z, mq[(oc - 16) * 128:(oc - 15) * 128, c0:c0 + n], z[:, 0:n], [d_mt[ti]])
                    else:
                        z = nxt("zb", zb)
                        cp("dve", z[:, 0:n], p[:, 0:n], [p], [z])
                        store(z, zo[(oc - 24) * 128:(oc - 23) * 128, c0:c0 + n], z[:, 0:n], [d_mt[ti]])
                    yield

        def mlstm_tile(tok0, direction, first_dir, init_from=None):
            d = direction
            ti = tok0 // TT
            load(gt, gt[:], gts[tok0:tok0 + 128, :], [d_mt[ti]])
            load(mqs, mqs[:], mq[:, tok0:tok0 + 128].rearrange("(h d) n -> d h n", d=64), [d_mt[ti]])
            load(mks, mks[:], mk[:, tok0:tok0 + 128].rearrange("(h d) n -> d h n", d=64), [d_mt[ti]])
            load(mkts, mkts[:], mkt[tok0:tok0 + 128, :], [d_mt[ti]])
            load(mvts, mvts[:, :, 0:128], mvt[tok0:tok0 + 128, :].rearrange("p (h f) -> p h f", f=128), [d_mt[ti]])
            fo = d * 8
            act(lf[:, 0:4], gt[:, fo + 4:fo + 8], AF.Exp, [gt], [lf], scale=-1.0)
            act(lf[:, 0:4], lf[:, 0:4], AF.Ln, [lf, cst], [lf], bias=ONE_AP())
            ts("dve", lf[:, 0:4], lf[:, 0:4], -1.0, ALU.mult, [lf], [lf])
            cp("dve", lfr[:, 0:4, :], lf[:, 0:4].unsqueeze(2).to_broadcast([128, 4, 64]), [lf], [lfr])
            Tm = cmf[:, d, :]
            p = mgate
            mm(p[:, 0:4], Tm, lf[:, 0:4], True, True, [cmf, lf], [p])
            cp("dve", bb[:, 0:4], p[:, 0:4], [p], [bb])
            mm(p[:, 8:12], cmf[:, 0, :], lf[:, 0:4], True, False, [cmf, lf], [p])
            mm(p[:, 8:12], cmf[:, 1, :], lf[:, 0:4], False, True, [cmf, lf], [p])
            tt("dve", gg[:, 0:4], p[:, 8:12], lf[:, 0:4], ALU.subtract, [p, lf], [gg])
            tt("dve", bk[:, 0:4], gt[:, fo:fo + 4], bb[:, 0:4], ALU.subtract, [gt, bb], [bk])
            tt("dve", kst[:, 0:4], bk[:, 0:4], gg[:, 0:4], ALU.add, [bk, gg], [kst])
            act(bk[:, 0:4], bk[:, 0:4], AF.Exp, [bk], [bk])
            act(kst[:, 0:4], kst[:, 0:4], AF.Exp, [kst], [kst])
            order = (0, 1) if d == 0 else (1, 0)

            def head_stages(h, si):
                C = Cst[d][h]
                B_ = msets[si]
                abc, eg, qp, ptm, kpp, Cbf, nrep, dm, hdir, hfs, zos, sg, sqm, yms = B_
                pN, pD, ptmp = mviews[si]
                ecol = (63, 127) if d == 0 else (0, 64)
                st = []

                def s1():
                    mm(ptmp[0:64, 0:128], lfr[:, h, :], Tm, True, True, [lfr, cmf], [ptmp])
                st.append(s1)

                def s2():
                    act(abc[:, :], ptmp[0:64, 0:128], AF.Exp, [ptmp], [abc])
                    for c in range(2):
                        cp("act", eg[:, c:c + 1], abc[:, ecol[c]:ecol[c] + 1], [abc], [eg])
                    stt("dve", qp[:, :], mqs[:, h, :], 0.125, abc[:, :], ALU.mult, ALU.mult, [mqs, abc], [qp])
                st.append(s2)

                def s3():
                    mm(ptmp[:, 0:128], mks[:, h, :], qp[:, :], True, True, [mks, qp], [ptmp])
                st.append(s3)

                def s4():
                    stt("dve", ptm[:, :], ptmp[:, 0:128], bk[:, h:h + 1], cm[:, 4 + d, :], ALU.mult, ALU.mult, [ptmp, bk, cm], [ptm])
                    ts("pool", kpp[:, :], mkts[:, h * 64:(h + 1) * 64], kst[:, h:h + 1], ALU.mult, [mkts, kst], [kpp])
                st.append(s4)

                def s5():
                    mm(pN[:, 0:128], mvts[:, h, 0:128], ptm[:, :], True, False, [mvts, ptm], [pN])
                    mm(pD[:, 0:128], onesb, ptm[:, :], True, False, [cm, ptm], [pD])
                st.append(s5)
                for ci, c in enumerate(order):
                    cs = slice(c * 64, (c + 1) * 64)
                    lastc = ci == 1

                    def s6a(c=c):
                        cp("act", Cbf[:, :], C[:, 0:128], [C], [Cbf])
                        cp("act", nrep[:, :], C[:, 128:129].to_broadcast([64, 128]), [C], [nrep])
                    st.append(s6a)

                    def s6b(c=c, cs=cs, lastc=lastc):
                        mm(pN[:, cs], Cbf[:, :], qp[:, cs], False, lastc, [Cbf, qp], [pN])
                        mm(pD[:, cs], nrep[:, :], qp[:, cs], False, lastc, [nrep, qp], [pD])
                        mm(ptmp[0:64, 0:129], kpp[cs, :], mvts[cs, h, :], True, True, [kpp, mvts], [ptmp])
                    st.append(s6b)

                    def s6c(c=c):
                        stt("dve", C[:, :], C[:, :], eg[:, c:c + 1], ptmp[0:64, 0:129], ALU.mult, ALU.add, [C, eg, ptmp], [C])
                    st.append(s6c)

                def s7():
                    act(dm[:, :], pD[:, 0:128], AF.Abs, [pD], [dm])
                    ts("dve", dm[:, :], dm[:, :], 1.0, ALU.max, [dm], [dm])
                    recip(dm[:, :], dm[:, :], [dm], [dm])
                    tt("dve", hdir[:, :], pN[:, 0:128], dm[:, :], ALU.mult, [pN, dm], [hdir])
                    if first_dir:
                        store(hdir, hf[h * 128:(h + 1) * 128, tok0:tok0 + 128], hdir[:, :], [d_hf])
                    else:
                        load(hfs, hfs[:, :], hf[h * 128:(h + 1) * 128, tok0:tok0 + 128], [d_hf])
                        load(zos, zos[:, :], zo[h * 128:(h + 1) * 128, tok0:tok0 + 128], [d_mt[ti]])
                        tt("dve", hfs[:, :], hfs[:, :], hdir[:, :], ALU.add, [hfs, hdir], [hfs])
                        act(sqm[:, :], hfs[:, :], AF.Square, [hfs], [sqm])
                st.append(s7)
                if not first_dir:
                    def s8():
                        mm(ptmp[:, 0:128], onesb, sqm[:, :], True, True, [cm, sqm], [ptmp])
                    st.append(s8)

                    def s9():
                        act(sg[:, :], ptmp[:, 0:128], AF.Ln, [ptmp, cst], [sg], bias=EPS_AP(), scale=1.0 / 128)
                        act(sg[:, :], sg[:, :], AF.Exp, [sg], [sg], scale=-0.5)
                        stt("dve", hfs[:, :], hfs[:, :], gh[:, h:h + 1], sg[:, :], ALU.mult, ALU.mult, [hfs, gh, sg], [hfs])
                        act(sg[:, :], zos[:, :], AF.Exp, [zos], [sg], scale=-1.0)
                        ts("dve", sg[:, :], sg[:, :], 1.0, ALU.add, [sg], [sg])
                        recip(sg[:, :], sg[:, :], [sg], [sg])
                        tt("dve", yms[:, :], hfs[:, :], sg[:, :], ALU.mult, [hfs, sg], [yms])
                        store(yms, ymT[h * 128:(h + 1) * 128, tok0:tok0 + 128], yms[:, :], [d_ymt[ti]])
                    st.append(s9)
                return st

            yield
            for hp in range(2):
                sa = head_stages(2 * hp, 0); sb_ = head_stages(2 * hp + 1, 1)
                for k in range(len(sa)):
                    sa[k](); sb_[k]()
                    yield

        def zero_states(d):
            for h in range(4):
                P.add("pool", lambda e, t=Cst[d][h]: e.memset(t[:], 0.0), [], [Cst[d][h]])

        zero_states(0); zero_states(1)
        run(phaseA(8))
        ctx_side = [mlstm_tile(T + tl * 128, 0, True) for tl in range(2)]
        if need_ctx:
            ctx_side += [mlstm_tile(T + tl * 128, 1, False) for tl in (1, 0)]
        for ti in range(8):
            run(phaseA(ti))
        run(chain(ctx_side))
        for tl in range(32):
            run(mlstm_tile(tl * 128, 0, True))
        for h in range(4):
            store(Cst[0][h], s_in[h * 64:(h + 1) * 64, :], Cst[0][h][:, :], [d_sin])

        groups = [[0, 1], [2, 3], [4, 5], [6, 7]]
        def allgather(src, dst, dsrc, ddst):
            P.add("pool", lambda e: e.collective_compute("AllGather", ALU.bypass, replica_groups=groups,
                                                         ins=[src.opt()], outs=[dst.opt()]),
                  [dsrc], [ddst], dma=CCSem(None, "cc"))
        allgather(s_in, s_out, d_sin, d_sout)
        allgather(GR, GRo, d_gr, d_gro)
        for i_ in range(2):
            allgather(GK[i_], GKo[i_], d_gk[i_], d_gko[i_])
            allgather(GV[i_], GVo[i_], d_gv[i_], d_gvo[i_])
        for h in range(4):
            C = Cst[1][h]
            load(sst, sst[:], s_out.rearrange("(r h d) f -> h d r f", r=2, h=4)[h], [d_sout])
            ts("dve", C[:, :], sst[:, 0, :], selw[0:64, 0:1], ALU.mult, [sst, selw], [C])
            stt("dve", C[:, :], sst[:, 1, :], selw[0:64, 1:2], C[:, :], ALU.mult, ALU.add, [sst, selw, C], [C])
        for k_ in range(2):
            load(hk2, hk2[:, k_, :, :], bass_ap_halo_k(GRo, k_), [d_gro])
        load(hv2, hv2[:], bass_ap_halo_v(GRo), [d_gro])
        for g in range(2):
            ts("dve", hks[:, g, :], hk2[:, 0, g, :], selw[0:64, 0:1], ALU.mult, [hk2, selw], [hks])
            stt("dve", hks[:, g, :], hk2[:, 1, g, :], selw[0:64, 1:2], hks[:, g, :], ALU.mult, ALU.add, [hk2, selw, hks], [hks])
            ts("dve", hvs[:, g, 0:64], hv2[:, 0, g * 64:(g + 1) * 64], selw[:, 0:1], ALU.mult, [hv2, selw], [hvs])
            stt("dve", hvs[:, g, 0:64], hv2[:, 1, g * 64:(g + 1) * 64], selw[:, 1:2], hvs[:, g, 0:64], ALU.mult, ALU.add, [hv2, selw, hvs], [hvs])
        load(kcs, kcs[:], swak[:, T:NT].rearrange("(g d) n -> d g n", d=64), [d_sw[8]])
        for c_ in range(2):
            load(vcs, vcs[:, c_, :, 0:64], swav[T + c_ * 128:T + (c_ + 1) * 128, :].rearrange("p (g f) -> p g f", f=64), [d_sw[8]])

        def mla(ti):
            isctx = ti == 8
            c0 = ti * TT; n = CTX if isctx else TT
            load(qn, qn[:, :, 0:n], qmla[0:512, c0:c0 + n].rearrange("(h p) n -> p h n", p=128), [d_q[ti]])
            load(qr, qr[:, :, 0:n], qmla[512:768, c0:c0 + n].rearrange("(h p) n -> p h n", p=64), [d_q[ti]])
            groups = []
            for h in range(4):
                groups.append((h, ("c", 0)))
                if not isctx:
                    groups += [(h, ("g", r_, q_)) for r_ in range(2) for q_ in range(4)]
            items = []
            for gi, (h, gsp) in enumerate(groups):
                for c in range(2 if gsp[0] == "c" else 8):
                    items.append((gi, c, h))
            nper = 2 + (0 if isctx else 64)
            loaded = set()

            def ensure(gi):
                if gi >= len(groups) or gi in loaded:
                    return
                loaded.add(gi)
                h, gsp = groups[gi]; sl = gi % 3
                K, R, V = kk[sl], kr[sl], vv[sl]
                if gsp[0] == "c":
                    load(K, K[:, 0:CTX], kc_n[h * 128:(h + 1) * 128, :], [d_kc])
                    load(R, R[:, 0:CTX], kc_r[:, :], [d_kc])
                    load(V, V[:, 0:2, :], vc_m[:, h * 128:(h + 1) * 128].rearrange("(c p) f -> p c f", p=128), [d_kc])
                else:
                    r_ = gsp[1]; q_ = gsp[2]
                    t0 = q_ * 2
                    load(K, K[:, :].rearrange("p (t n) -> p t n", n=512), GKo[h // 2][r_ * 2048:(r_ + 1) * 2048, :].rearrange("(f t) n -> f t n", t=8)[(h % 2) * 128:(h % 2 + 1) * 128, t0:t0 + 2, :], [d_gko[h // 2]])
                    load(R, R[:, :].rearrange("p (t n) -> p t n", n=512), GRo[r_ * 576:r_ * 576 + 512, :].rearrange("(f t) n -> f t n", t=8)[:, t0:t0 + 2, :], [d_gro])
                    vb = r_ * 2048 + (q_ % 2) * 1024
                    load(V, V[:, :, :], GVo[q_ // 2][vb:vb + 1024, h * 128:(h + 1) * 128].rearrange("(c p) f -> p c f", p=128), [d_gvo[q_ // 2]])

            def emit_S(i):
                gi, c, h = items[i]
                ensure(gi); ensure(gi + 1)
                K, R = kk[gi % 3], kr[gi % 3]
                pS = pS3[i % 3]; pt = pT[i % 3]
                mm(pS[:, 0:n], K[:, c * 128:(c + 1) * 128], qn[:, h, 0:n], True, False, [K, qn], [pS])
                mm(pS[:, 0:n], R[:, c * 128:(c + 1) * 128], qr[:, h, 0:n], False, True, [R, qr], [pS])
                act(pt[:, 0:n], pS[:, 0:n], AF.Exp, [pS], [pt], scale=MLA_SCALE)

            def emit_PV(i):
                gi, c, h = items[i]
                V = vv[gi % 3]; pt = pT[i % 3]
                idx = i - h * nper
                pO, pDn = acc0, acc1
                mm(pO[:, 0:n], V[:, c, :], pt[:, 0:n], idx == 0, idx == nper - 1, [V, pt], [pO])
                mm(pDn[:, 0:n], onesb, pt[:, 0:n], idx == 0, idx == nper - 1, [cm, pt], [pDn])
                if idx == nper - 1:
                    recip(rec[:, 0:n], pDn[:, 0:n], [pDn], [rec])
                    tt("dve", yT[:, h, 0:n], pO[:, 0:n], rec[:, 0:n], ALU.mult, [pO, rec], [yT])

            LA = 2
            for i in range(min(LA, len(items))):
                emit_S(i)
            for i in range(len(items)):
                if i + LA < len(items):
                    emit_S(i + LA)
                emit_PV(i)
                yield

        def swa(ti):
            isctx = ti == 8
            c0 = ti * TT; n = CTX if isctx else TT
            load(qsw, qsw[0:64, :, 0:n], swaq[:, c0:c0 + n].rearrange("(h d) n -> d h n", d=64), [d_sw[ti]])
            if not isctx:
                lo = max(c0 - 128, 0); hi = min(c0 + TT + 128, T)
                o = lo - (c0 - 128)
                deps = [d_sw[i] for i in range(max(ti - 1, 0), min(ti + 2, 8))]
                load(ksw, ksw[:, :, o:o + hi - lo], swak[:, lo:hi].rearrange("(g d) n -> d g n", d=64), deps)
                for c_ in range((hi - lo) // 128):
                    load(vsw, vsw[:, o // 128 + c_, :, 0:64],
                         swav[lo + c_ * 128:lo + (c_ + 1) * 128, :].rearrange("p (g f) -> p g f", f=64), deps)
            units = [(j, g, hh) for j in range(n // 128) for g in range(2) for hh in range(2)]
            items = []
            for u, (j, g, hh) in enumerate(units):
                jb = ti * 4 + j
                chunks = [("c", 0, None), ("c", 1, None)]
                if not isctx:
                    if jb > 0:
                        chunks.append(("l", j, 1))
                    chunks.append(("l", j + 1, None))
                    if jb < 31:
                        chunks.append(("l", j + 2, 2))
                    else:
                        chunks.append(("h", 0, 3))
                for ci, ch in enumerate(chunks):
                    items.append((u, ci, ch, len(chunks)))
            dsws = (tmp[0], tmp[1]); accs = (acc0, acc1); sctr = [0]

            def operands(u, ch):
                j, g, hh = units[u]
                kind, idx, msk = ch
                if kind == "c":
                    return kcs[:, g, idx * 128:(idx + 1) * 128], vcs[:, idx, g, :], kcs, vcs
                if kind == "l":
                    return ksw[:, g, idx * 128:(idx + 1) * 128], vsw[:, idx, g, :], ksw, vsw
                return hks[:, g, :], hvs[:, g, :], hks, hvs

            def emit_S(i):
                u, ci, ch, nchk = items[i]
                j, g, hh = units[u]; h0 = g * 8 + hh * 4
                kl, vl, kb, vb = operands(u, ch)
                rhs = qsw[0:64, h0:h0 + 4, j * 128:(j + 1) * 128]
                pS = pS3[sctr[0] % 3]; sctr[0] += 1; pt = pT[i % 3]
                mm(pS[:, :].rearrange("p (a b) -> p a b", b=128), kl, rhs, True, True, [kb, qsw], [pS])
                act(pt[:, :], pS[:, :], AF.Exp, [pS], [pt], scale=0.125)
                if ch[2] is not None:
                    tt("dve", pt[:, :].rearrange("p (a b) -> p a b", b=128), pt[:, :].rearrange("p (a b) -> p a b", b=128),
                       cm[:, ch[2], :].unsqueeze(1).to_broadcast([128, 4, 128]), ALU.mult, [pt, cm], [pt])

            def tailA(u):
                j, g, hh = units[u]; h0 = g * 8 + hh * 4
                pO = accs[u % 2]; dsw_ = dsws[u % 2]
                for q in range(4):
                    ts("dve", dsw_[64:65, q * 128:(q + 1) * 128], pO[64:65, q * 128:(q + 1) * 128],
                       sk[64:65, h0 + q:h0 + q + 1], ALU.add, [pO, sk], [dsw_])
                recip(dsw_[64:65, :], dsw_[64:65, :], [dsw_], [dsw_])

            def tailB(u):
                j, g, hh = units[u]; h0 = g * 8 + hh * 4
                pO = accs[u % 2]; dsw_ = dsws[u % 2]; pB = pS3[sctr[0] % 3]; sctr[0] += 1
                mm(pB[0:64, :], cmf[64:65, 2, 0:64], dsw_[64:65, :], True, True, [cmf, dsw_], [pB])
                cp("act", rbs[0:64, 0, :], pB[0:64, :], [pB], [rbs])
                tt("dve", ysw[0:64, h0:h0 + 4, j * 128:(j + 1) * 128], pO[0:64, :].rearrange("p (a b) -> p a b", b=128),
                   rbs[0:64, 0, :].rearrange("p (a b) -> p a b", b=128), ALU.mult, [pO, rbs], [ysw])

            def emit_PV(i):
                u, ci, ch, nchk = items[i]
                kl, vl, kb, vb = operands(u, ch)
                pO = accs[u % 2]; pt = pT[i % 3]
                mm(pO[0:65, :], vl, pt[:, :], ci == 0, ci == nchk - 1, [vb, pt], [pO])
                if ci == nchk - 1:
                    tailA(u)
                if ci == 0 and u > 0:
                    tailB(u - 1)

            LA = 2
            for i in range(min(LA, len(items))):
                emit_S(i)
            for i in range(len(items)):
                if i + LA < len(items):
                    emit_S(i + LA)
                emit_PV(i)
                yield
            tailB(len(units) - 1)

        def phaseC(ti):
            isctx = ti == 8
            c0 = ti * TT; n = CTX if isctx else TT; r = 1 if isctx else 0
            yield from mla(ti)
            yield from swa(ti)
            load(yT, yT[:, 4:8, 0:n], ymT[:, c0:c0 + n].rearrange("(c p) n -> p c n", p=128), [d_ymt[ti]])
            load(xt, xt[:, :, 0:n], xT[:, c0:c0 + n].rearrange("(c p) n -> p c n", p=128), [d_x[ti]])
            for oc in range(16):
                w = nxt("wt", wt)
                load(w, w[:, 0:1024], w_outa[l, oc], [d_w[l]])
                load(w, w[0:64, 1024:1024 + 2048], w_outb[l, oc], [d_w[l]])
                wa = w[:, 0:1024].rearrange("p (a b) -> p a b", b=128)
                wb = w[0:64, 1024:3072].rearrange("p (a b) -> p a b", b=128)
                p = nxt("ps", ps)
                for kc in range(8):
                    mm(p[:, 0:n], wa[:, kc, :], yT[:, kc, 0:n], kc == 0, False, [w, yT], [p])
                for hd in range(16):
                    mm(p[:, 0:n], wb[:, hd, :], ysw[0:64, hd, 0:n], False, hd == 15, [w, ysw], [p])
                stt("dve", xt[:, oc, 0:n], p[:, 0:n], modT[:, GT1 + oc, r:r + 1], xt[:, oc, 0:n], ALU.mult, ALU.add,
                    [p, modT, xt], [xt])
                yield
            norm_mod(n, r, a2, SH2)
            for half in range(4):
                for blk in range(8):
                    w = nxt("wt", wt)
                    load(w, w[:], w_ff1[l, half * 8 + blk], [d_w[l]])
                    wv = w[:].rearrange("p (a b) -> p a b", b=256)
                    for o4 in range(2):
                        hc = blk * 2 + o4
                        p = nxt("ps", ps)
                        for kc in range(KC):
                            mm(p[:, 0:n], wv[:, kc, o4 * 128:(o4 + 1) * 128], hT[:, kc, 0:n], kc == 0, kc == KC - 1, [w, hT], [p])
                        t_ = nxt("tmp", tmp)
                        ts("dve", t_[:, 0:n], p[:, 0:n], 0.0, ALU.max, [p], [t_])
                        act(big[:, hc, 0:n], t_[:, 0:n], AF.Square, [t_], [big])
                        yield
                for oc in range(16):
                    w = nxt("wt", wt)
                    load(w, w[:, 0:2048], w_ff2[l, half * 16 + oc], [d_w[l]])
                    wv = w[:, 0:2048].rearrange("p (a b) -> p a b", b=128)
                    p = nxt("ps", ps)
                    for kc in range(16):
                        mm(p[:, 0:n], wv[:, kc, :], big[:, kc, 0:n], kc == 0, kc == 15, [w, big], [p])
                    stt("dve", xt[:, oc, 0:n], p[:, 0:n], modT[:, GT2 + oc, r:r + 1], xt[:, oc, 0:n], ALU.mult, ALU.add,
                        [p, modT, xt], [xt])
                    yield
            if last and l == L - 1:
                for c in range(KC):
                    act(big[:, c, 0:n], xt[:, c, 0:n], AF.Square, [xt], [big])
                p = nxt("ps", ps)
                for c in range(KC):
                    mm(p[:, 0:n], onesb, big[:, c, 0:n], c == 0, c == KC - 1, [cm, big], [p])
                rstd_from(p, n, 1.0 / D, rs)
                for c in range(KC):
                    stt("dve", xt[:, c, 0:n], xt[:, c, 0:n], gf[:, c:c + 1], rs[:, 0:n], ALU.mult, ALU.mult, [xt, gf, rs], [xt])
                store(xt, out_ap[:, c0:c0 + n].rearrange("(c p) n -> p c n", p=128), xt[:, :, 0:n], [d_out])
            else:
                store(xt, xT[:, c0:c0 + n].rearrange("(c p) n -> p c n", p=128), xt[:, :, 0:n], [d_x[ti]])
                if l == L - 1:
                    store(xt, out_ap[:, c0:c0 + n].rearrange("(c p) n -> p c n", p=128), xt[:, :, 0:n], [d_out])

        def bwd_side(t):
            return chain([mlstm_tile((t * 4 + q_) * 128, 1, False) for q_ in (3, 2, 1, 0)])

        nxt_casts = cast_thunks(l + 1) if l + 1 < L else []
        per = (len(nxt_casts) + 6) // 7
        for t in range(7, -1, -1):
            run(bwd_side(t))
        if need_ctx:
            run(phaseC(8))
        for ti in range(7, -1, -1):
            run(phaseC(ti))
            for t_ in nxt_casts[(7 - ti) * per:(8 - ti) * per]:
                t_()

    P.add("pool", lambda e: e.engine_nop() if hasattr(e, "engine_nop") else e.memset(cst[:, 7:8], 0.0), [d_out], [cst])
    P.emit(stack)
    stack.close()
    return nc


def bass_ap_halo_k(g_out, k_):
    return g_out[k_ * 576 + 512:k_ * 576 + 544, :].rearrange("r (q j) -> (r q) j", q=4).rearrange("(g d) j -> d g j", d=64)


def bass_ap_halo_v(g_out):
    return g_out.rearrange("(k x) n -> k x n", k=2)[:, 544:576, :].rearrange("k r (q j) -> (r q) k j", q=4)


def _const_masks():
    m = np.zeros((128, 8, 128), np.float32)
    i = np.arange(128)[:, None]; t = np.arange(128)[None, :]
    pm = np.arange(128)
    src = np.where((pm % 32) < 16, pm + 16, pm - 16)
    m[src, 0, pm] = 1.0
    m[:, 1, :] = (i >= t); m[:, 2, :] = (i <= t); m[:, 3, :] = (i + t >= 127)
    same = (i // 64) == (t // 64)
    m[:, 4, :] = same & (i <= t); m[:, 5, :] = same & (i >= t)
    m[:, 6, :] = (i == t); m[:, 7, :] = 1.0
    return m


def _rope_tables(pos):
    half = 32
    inv = (np.float32(10000.0) ** (-np.arange(0, half, 2, dtype=np.float32) / np.float32(half))).astype(np.float32)
    row = (pos // 64).astype(np.float32); col = (pos % 64).astype(np.float32)
    ar = row[:, None] * inv; ac = col[:, None] * inv
    ang = np.concatenate([ar, ar, ac, ac], axis=-1)
    cos = np.cos(ang).astype(np.float32).T; sin = np.sin(ang).astype(np.float32).T
    sgn = np.where((np.arange(64) % 32) < 16, -1.0, 1.0).astype(np.float32)[:, None]
    sin = sin * sgn
    return np.ascontiguousarray(np.concatenate([cos, cos], 0)), np.ascontiguousarray(np.concatenate([sin, sin], 0))


def _fm(v, k):
    return np.ascontiguousarray(v.reshape(k, 128).T)


def _prep_gate_parts(inp, odd):
    L = DEPTH
    gate = np.arange(3136, 3152)
    if odd:
        gate = np.concatenate([gate[8:16], gate[0:8]])
    cols = np.concatenate([np.arange(3584 - 16 + 16 + 0, 3584 - 16 + 16 + 0)[:0], np.arange(768, 832), gate])
    wb = inp["w_in"][:, :, cols].reshape(L, KC, 128, 80)
    out = {"win_b": np.ascontiguousarray(wb.transpose(0, 2, 1, 3), dtype=np.float32).reshape(L, 128, KC * 80)}
    gbv = inp["mlstm_gate_bias"].reshape(L, 16)
    if odd:
        gbv = np.concatenate([gbv[:, 8:16], gbv[:, 0:8]], axis=1)
    out["gbias"] = np.ascontiguousarray(gbv, dtype=np.float32)
    return out


def _prep_weights(inp, odd):
    L = DEPTH
    A = lambda a: np.ascontiguousarray(a, dtype=np.float32)
    w = {}
    wm = inp["w_mod"].reshape(L, KC, 128, 96, 128)
    w["wmod"] = A(wm.transpose(0, 3, 2, 1, 4)).reshape(L, 96, 128, KC * 128)
    w["bmodT"] = A(inp["b_mod"].reshape(L, 96, 128).transpose(0, 2, 1))
    w["g1T"] = A(inp["g_norm1"].reshape(L, KC, 128).transpose(0, 2, 1))
    w["g2T"] = A(inp["g_norm2"].reshape(L, KC, 128).transpose(0, 2, 1))
    gate = np.arange(3136, 3152)
    if odd:
        gate = np.concatenate([gate[8:16], gate[0:8]])
    cols = np.concatenate([np.arange(0, 768), np.arange(832, 3136), np.arange(3152, 3664), np.arange(768, 832), gate])
    win = inp["w_in"][:, :, cols]
    wa = win[:, :, :3584].reshape(L, KC, 128, 14, 256)
    w["win_a"] = A(wa.transpose(0, 3, 2, 1, 4)).reshape(L, 14, 128, KC * 256)
    wb = win[:, :, 3584:].reshape(L, KC, 128, 80)
    w["win_b"] = A(wb.transpose(0, 2, 1, 3)).reshape(L, 128, KC * 80)
    w["gqT"] = A(inp["mla_g_q"].reshape(L, 4, 128).transpose(0, 2, 1))
    w["gkvT"] = A(inp["mla_g_kv"].reshape(L, 2, 128).transpose(0, 2, 1))
    qc = np.concatenate([np.concatenate([h * 192 + np.arange(128) for h in range(4)]),
                         np.concatenate([h * 192 + 128 + np.arange(64) for h in range(4)])])
    wuq = inp["mla_w_uq"][:, :, qc].reshape(L, 4, 128, 768)
    w["wuq"] = A(wuq.transpose(0, 2, 1, 3)).reshape(L, 128, 4 * 768)
    kvc = np.concatenate([np.concatenate([h * 256 + np.arange(128) for h in range(4)]),
                          np.concatenate([h * 256 + 128 + np.arange(128) for h in range(4)])])
    wukv = inp["mla_w_ukv"][:, :, kvc].reshape(L, 2, 128, 1024)
    w["wukv"] = A(wukv.transpose(0, 2, 1, 3)).reshape(L, 128, 2 * 1024)
    w["sink"] = A(inp["swa_sink"])
    gbv = inp["mlstm_gate_bias"].reshape(L, 16)
    if odd:
        gbv = np.concatenate([gbv[:, 8:16], gbv[:, 0:8]], axis=1)
    w["gbias"] = A(gbv)
    w["ghT"] = A(inp["mlstm_g_h"].transpose(0, 2, 1))
    wo = inp["w_out"]
    rows_a = np.concatenate([np.arange(0, 512), np.arange(1536, 2048)])
    woa = wo[:, rows_a, :].reshape(L, 8, 128, 16, 128)
    w["wout_a"] = A(woa.transpose(0, 3, 2, 1, 4)).reshape(L, 16, 128, 1024)
    wob = wo[:, 512:1536, :].reshape(L, 16, 64, 16, 128)
    w["wout_b"] = A(wob.transpose(0, 3, 2, 1, 4)).reshape(L, 16, 64, 2048)
    f1 = inp["w_ff1"].reshape(L, KC, 128, 32, 256)
    w["wff1"] = A(f1.transpose(0, 3, 2, 1, 4)).reshape(L, 32, 128, KC * 256)
    f2 = inp["w_ff2"].reshape(L, 4, 16, 128, 16, 128)
    w["wff2"] = A(f2.transpose(0, 1, 4, 3, 2, 5)).reshape(L, 64, 128, 16 * 128)
    w["gfT"] = _fm(np.asarray(inp["g_final"], np.float32), KC)
    w["cmask"] = _const_masks()
    return w


_PER_LAYER = ("wmod", "bmodT", "g1T", "g2T", "win_a", "win_b", "gqT", "gkvT", "wuq", "wukv", "sink", "gbias",
              "ghT", "wout_a", "wout_b", "wff1", "wff2")
_NC_CACHE = {}


def _get_nc(layers, last):
    key = (layers, last)
    if key not in _NC_CACHE:
        _NC_CACHE[key] = build_program(layers, True, last)
    return _NC_CACHE[key]


FUSED = True


def kernel(**inp):
    inp = {k: np.asarray(v) for k, v in inp.items()}
    x = inp["x"]; ctx = inp["ctx"]; c = inp["c"]; c_ctx = inp["c_ctx"]
    wts = [_prep_weights(inp, 0), None]
    wts[1] = dict(wts[0])
    wts[1].update(_prep_gate_parts(inp, 1))
    per_core = []
    for core in range(8):
        b = core // 2; odd = core % 2
        if odd:
            xs = x[b, T:SEQ][::-1]; cs = ctx[b][::-1]; pos = (SEQ - 1 - np.arange(T))
        else:
            xs = x[b, 0:T]; cs = ctx[b]; pos = np.arange(T)
        xT = np.ascontiguousarray(np.concatenate([xs, cs], 0).T.astype(np.float32))
        cT = np.stack([_fm(c[b].astype(np.float32), KC), _fm(c_ctx.astype(np.float32), KC)], axis=-1)
        cos2, sin2 = _rope_tables(pos)
        selw = np.zeros((128, 2), np.float32); selw[:, 1 - odd] = 1.0
        per_core.append({"xT": xT, "cT": np.ascontiguousarray(cT), "cos2": cos2, "sin2": sin2, "selw": selw})

    def maps(lsl, xTs):
        out = []
        for core in range(8):
            w = wts[core % 2]
            m = dict(per_core[core])
            if xTs is not None:
                m["xT"] = xTs[core]
            for k in _PER_LAYER:
                m[k] = w[k][lsl]
            m["gfT"] = w["gfT"]; m["cmask"] = w["cmask"]
            out.append(m)
        return out

    if FUSED:
        nc = _get_nc(DEPTH, True)
        res = run_bass_kernel_spmd(nc, maps(slice(0, DEPTH), None), core_ids=list(range(8)))
        outs = [np.asarray(r["outT"]) for r in res.results]
    else:
        xTs = None
        for l in range(DEPTH):
            lastl = l == DEPTH - 1
            nc = _get_nc(1, lastl)
            res = run_bass_kernel_spmd(nc, maps(slice(l, l + 1), xTs), core_ids=list(range(8)))
            if lastl:
                outs = [np.asarray(r["outT"]) for r in res.results]
            else:
                xTs = [np.asarray(r["xT_out"]) for r in res.results]
    out = np.empty((BATCH, SEQ, D), np.float32)
    for core in range(8):
        b = core // 2
        o = outs[core].T
        if core % 2:
            out[b, T:SEQ] = o[::-1]
        else:
            out[b, 0:T] = o
    return out
```
